# Optimizing a Trainium2 kernel written in Bass

```python
import jax, jax.numpy as jnp
from jax import lax
import numpy as np

D_MODEL = 1024
BATCH = 8
SEQ = 2048
DEPTH = 4

N_A = DEPTH // 2
N_B = DEPTH - N_A
HEAD_DIM = 64
N_MIX_HEADS = 12
N_KV_GROUPS = 2
HEADS_PER_GROUP = N_MIX_HEADS // N_KV_GROUPS
N_MEM_HEADS = 4
N_MEM = 256
L_CMP = 32
CMP_STRIDE = 16
CMP_HIDDEN = 256
L_SEL = 64
TOP_N = 16
WINDOW = 512
Q_BLOCK = 128
D_FF = 2816
CONV_WIDTH = 3
ROPE_THETA = 10000.0
EPS = 1e-6
NEG = -1e30
FORCE_BONUS = 1e4

MIX_W = N_MIX_HEADS * HEAD_DIM
MEM_W = N_MEM_HEADS * HEAD_DIM
BRANCH_KV_W = 6 * N_KV_GROUPS * HEAD_DIM
GATE_W = 3 * N_MIX_HEADS
A_IN = MIX_W + BRANCH_KV_W + GATE_W + MEM_W
B_IN = MIX_W + MEM_W
SHARED_W = 2 * MIX_W + N_MIX_HEADS

kernel_name = "yoco_nsa_fox_hybrid"


def rms_norm(x, g):
    xf = x.astype(jnp.float32)
    y = xf * lax.rsqrt(jnp.mean(xf * xf, axis=-1, keepdims=True) + EPS)
    return (y * g.astype(jnp.float32)).astype(x.dtype)


def rope(x, pos):
    half = HEAD_DIM // 2
    inv = ROPE_THETA ** (-jnp.arange(half, dtype=jnp.float32) / half)
    ang = pos.astype(jnp.float32)[:, None] * inv[None, :]
    shape = (1, pos.shape[0]) + (1,) * (x.ndim - 3) + (half,)
    cos = jnp.cos(ang).reshape(shape)
    sin = jnp.sin(ang).reshape(shape)
    xf = x.astype(jnp.float32)
    x1, x2 = xf[..., :half], xf[..., half:]
    return jnp.concatenate([x1 * cos - x2 * sin, x2 * cos + x1 * sin], axis=-1).astype(x.dtype)


def masked_softmax(s, mask):
    s = jnp.where(mask, s, NEG)
    s = s - jnp.max(s, axis=-1, keepdims=True)
    e = jnp.exp(s) * mask
    return e / jnp.maximum(jnp.sum(e, axis=-1, keepdims=True), 1e-30)


def compress_blocks(u, pos_emb, w1, b1, w2, b2):
    B, S, G, dh = u.shape
    n_cmp = (S - L_CMP) // CMP_STRIDE + 1
    idx = jnp.arange(n_cmp)[:, None] * CMP_STRIDE + jnp.arange(L_CMP)[None, :]
    blocks = u[:, idx] + pos_emb[:, None, :]
    flat = blocks.transpose(0, 1, 3, 2, 4).reshape(B, n_cmp, G, L_CMP * dh)
    return jax.nn.gelu(flat @ w1 + b1) @ w2 + b2


def nsa_attention(q, kc, vc, k_slc, v_slc, k_win, v_win, gates):
    B, S, G, HG, dh = q.shape
    n_cmp = kc.shape[1]
    n_sel = S // L_SEL
    n_top = min(TOP_N, n_sel)
    n_qb = S // Q_BLOCK
    scale = HEAD_DIM ** -0.5
    cmp_start = jnp.arange(n_cmp) * CMP_STRIDE
    cmp_end = cmp_start + L_CMP - 1
    sel_start = jnp.arange(n_sel) * L_SEL
    overlap = ((cmp_start[:, None] < sel_start[None, :] + L_SEL)
               & (cmp_start[:, None] + L_CMP > sel_start[None, :])).astype(jnp.float32)
    kb = k_slc.reshape(B, n_sel, L_SEL, G, dh).transpose(0, 3, 1, 2, 4)
    vb = v_slc.reshape(B, n_sel, L_SEL, G, dh).transpose(0, 3, 1, 2, 4)
    kw_pad = jnp.pad(k_win, ((0, 0), (WINDOW, 0), (0, 0), (0, 0)))
    vw_pad = jnp.pad(v_win, ((0, 0), (WINDOW, 0), (0, 0), (0, 0)))
    bi = jnp.arange(B)[:, None, None, None]
    gi = jnp.arange(G)[None, :, None, None]
    j_sel = jnp.arange(n_sel)

    def block(args):
        c, qc, gc = args
        t = c * Q_BLOCK + jnp.arange(Q_BLOCK)
        s = jnp.einsum('btghd,bngd->bghtn', qc, kc).astype(jnp.float32) * scale
        p_cmp = masked_softmax(s, cmp_end[None, :] <= t[:, None])
        o_cmp = jnp.einsum('bghtn,bngd->btghd', p_cmp.astype(vc.dtype), vc)
        imp = jnp.einsum('bghtn,nj->bgtj', p_cmp, overlap)
        blk_t = t // L_SEL
        forced = (j_sel[None, :] == 0) | (j_sel[None, :] == blk_t[:, None]) | (j_sel[None, :] == blk_t[:, None] - 1)
        valid = j_sel[None, :] <= blk_t[:, None]
        score = jnp.where(valid, imp + FORCE_BONUS * forced.astype(jnp.float32), NEG)
        _, idx = lax.top_k(score, n_top)
        kg = kb[bi, gi, idx]
        vg = vb[bi, gi, idx]
        key_pos = idx[..., None] * L_SEL + jnp.arange(L_SEL)
        m_sel = (key_pos <= t[None, None, :, None, None]).reshape(B, G, 1, Q_BLOCK, n_top * L_SEL)
        s = jnp.einsum('btghd,bgtnld->bghtnl', qc, kg).astype(jnp.float32) * scale
        p = masked_softmax(s.reshape(B, G, HG, Q_BLOCK, n_top * L_SEL), m_sel)
        o_slc = jnp.einsum('bghtk,bgtkd->btghd', p.astype(vg.dtype),
                           vg.reshape(B, G, Q_BLOCK, n_top * L_SEL, dh))
        kw = lax.dynamic_slice_in_dim(kw_pad, c * Q_BLOCK, WINDOW + Q_BLOCK, axis=1)
        vw = lax.dynamic_slice_in_dim(vw_pad, c * Q_BLOCK, WINDOW + Q_BLOCK, axis=1)
        s_pos = c * Q_BLOCK - WINDOW + jnp.arange(WINDOW + Q_BLOCK)
        m_win = (s_pos[None, :] >= 0) & (s_pos[None, :] <= t[:, None]) & (t[:, None] - s_pos[None, :] < WINDOW)
        s = jnp.einsum('btghd,bsgd->bghts', qc, kw).astype(jnp.float32) * scale
        p = masked_softmax(s, m_win)
        o_win = jnp.einsum('bghts,bsgd->btghd', p.astype(vw.dtype), vw)
        g = gc.reshape(B, Q_BLOCK, G, HG, 3)
        return g[..., 0:1] * o_cmp + g[..., 1:2] * o_slc + g[..., 2:3] * o_win

    q_chunks = jnp.moveaxis(q.reshape(B, n_qb, Q_BLOCK, G, HG, dh), 1, 0)
    g_chunks = jnp.moveaxis(gates.reshape(B, n_qb, Q_BLOCK, N_MIX_HEADS, 3), 1, 0)
    out = lax.map(block, (jnp.arange(n_qb), q_chunks, g_chunks))
    return jnp.moveaxis(out, 0, 1).reshape(B, S, MIX_W)


def nsa_mixer(h, w_in, gate_b, cmp_pos, cmp_w1, cmp_b1, cmp_w2, cmp_b2, pos):
    B, S, _ = h.shape
    proj = h @ w_in
    q = proj[..., :MIX_W].reshape(B, S, N_MIX_HEADS, HEAD_DIM)
    kv = proj[..., MIX_W:MIX_W + BRANCH_KV_W].reshape(B, S, 6, N_KV_GROUPS, HEAD_DIM)
    gate_logit = proj[..., MIX_W + BRANCH_KV_W:MIX_W + BRANCH_KV_W + GATE_W]
    q_mem = proj[..., MIX_W + BRANCH_KV_W + GATE_W:].reshape(B, S, N_MEM_HEADS, HEAD_DIM)
    q = rope(q, pos).reshape(B, S, N_KV_GROUPS, HEADS_PER_GROUP, HEAD_DIM)
    k_cmp, v_cmp, k_slc, v_slc, k_win, v_win = (kv[:, :, i] for i in range(6))
    n_cmp = (S - L_CMP) // CMP_STRIDE + 1
    cmp_end = jnp.arange(n_cmp) * CMP_STRIDE + L_CMP - 1
    kc = rope(compress_blocks(k_cmp, cmp_pos[0], cmp_w1[0], cmp_b1[0], cmp_w2[0], cmp_b2[0]), cmp_end)
    vc = compress_blocks(v_cmp, cmp_pos[1], cmp_w1[1], cmp_b1[1], cmp_w2[1], cmp_b2[1])
    gates = jax.nn.sigmoid(gate_logit + gate_b).reshape(B, S, N_MIX_HEADS, 3)
    o = nsa_attention(q, kc, vc, rope(k_slc, pos), v_slc, rope(k_win, pos), v_win, gates)
    return o, q_mem


def forgetting_attention(q, k, v, dcum):
    B, S, H, dh = q.shape
    scale = HEAD_DIM ** -0.5
    outs = []
    for c in range(S // Q_BLOCK):
        lo, hi = c * Q_BLOCK, (c + 1) * Q_BLOCK
        t = lo + jnp.arange(Q_BLOCK)
        s = jnp.einsum('bthd,bshd->bhts', q[:, lo:hi], k[:, :hi]).astype(jnp.float32) * scale
        s = s + dcum[:, :, lo:hi, None] - dcum[:, :, None, :hi]
        p = masked_softmax(s, jnp.arange(hi)[None, :] <= t[:, None])
        outs.append(jnp.einsum('bhts,bshd->bthd', p.astype(v.dtype), v[:, :hi]))
    return jnp.concatenate(outs, axis=1).reshape(B, S, MIX_W)


def memory_attention(q_mem, mem, g_mem, w_mem_kv):
    B, M, _ = mem.shape
    kv = (rms_norm(mem, g_mem) @ w_mem_kv).reshape(B, M, 2, N_MEM_HEADS, HEAD_DIM)
    s = jnp.einsum('bthd,bmhd->bhtm', q_mem, kv[:, :, 0]).astype(jnp.float32) * HEAD_DIM ** -0.5
    p = jax.nn.softmax(s, axis=-1).astype(kv.dtype)
    o = jnp.einsum('bhtm,bmhd->bthd', p, kv[:, :, 1])
    return o.reshape(q_mem.shape[0], q_mem.shape[1], MEM_W)


def conv_ffn(h, w_up, conv_w, conv_b, w_down):
    u = h @ w_up
    C = u.shape[-1]
    u = lax.conv_general_dilated(u, conv_w[:, None, :], window_strides=(1,),
                                 padding=[(CONV_WIDTH - 1, 0)],
                                 dimension_numbers=('NWC', 'WIO', 'NWC'),
                                 feature_group_count=C) + conv_b
    a, b = u[..., :D_FF], u[..., D_FF:]
    return (jax.nn.silu(a) * b) @ w_down


def setup_inputs(seed: int = 0) -> dict:
    key = jax.random.key(seed)
    ks = jax.random.split(key, 24)
    f32 = jnp.float32
    nrm = lambda k, shape, scale: jax.random.normal(k, shape, f32) * scale
    return {
        "x": nrm(ks[0], (BATCH, SEQ, D_MODEL), 1.0),
        "mem": nrm(ks[1], (BATCH, N_MEM, D_MODEL), 1.0),
        "attn_norm": 1.0 + nrm(ks[2], (DEPTH, D_MODEL), 0.02),
        "ffn_norm": 1.0 + nrm(ks[3], (DEPTH, D_MODEL), 0.02),
        "mem_norm": 1.0 + nrm(ks[4], (DEPTH, D_MODEL), 0.02),
        "w_mem_kv": nrm(ks[5], (DEPTH, D_MODEL, 2 * MEM_W), D_MODEL ** -0.5),
        "w_o": nrm(ks[6], (DEPTH, MIX_W + MEM_W, D_MODEL), (MIX_W + MEM_W) ** -0.5),
        "w_up": nrm(ks[7], (DEPTH, D_MODEL, 2 * D_FF), D_MODEL ** -0.5),
        "conv_w": nrm(ks[8], (DEPTH, CONV_WIDTH, 2 * D_FF), CONV_WIDTH ** -0.5),
        "conv_b": nrm(ks[9], (DEPTH, 2 * D_FF), 0.01),
        "w_down": nrm(ks[10], (DEPTH, D_FF, D_MODEL), D_FF ** -0.5),
        "a_w_in": nrm(ks[11], (N_A, D_MODEL, A_IN), D_MODEL ** -0.5),
        "a_gate_b": nrm(ks[12], (N_A, GATE_W), 0.01),
        "a_cmp_pos": nrm(ks[13], (N_A, 2, L_CMP, HEAD_DIM), 0.1),
        "a_cmp_w1": nrm(ks[14], (N_A, 2, L_CMP * HEAD_DIM, CMP_HIDDEN), (L_CMP * HEAD_DIM) ** -0.5),
        "a_cmp_b1": nrm(ks[15], (N_A, 2, CMP_HIDDEN), 0.01),
        "a_cmp_w2": nrm(ks[16], (N_A, 2, CMP_HIDDEN, HEAD_DIM), CMP_HIDDEN ** -0.5),
        "a_cmp_b2": nrm(ks[17], (N_A, 2, HEAD_DIM), 0.01),
        "b_w_in": nrm(ks[18], (N_B, D_MODEL, B_IN), D_MODEL ** -0.5),
        "kv_norm": 1.0 + nrm(ks[19], (D_MODEL,), 0.02),
        "w_kv_shared": nrm(ks[20], (D_MODEL, SHARED_W), D_MODEL ** -0.5),
        "b_fgate": 3.0 + nrm(ks[21], (N_MIX_HEADS,), 0.1),
        "final_norm": 1.0 + nrm(ks[22], (D_MODEL,), 0.02),
    }


def reference(x, mem, attn_norm, ffn_norm, mem_norm, w_mem_kv, w_o, w_up, conv_w, conv_b, w_down,
              a_w_in, a_gate_b, a_cmp_pos, a_cmp_w1, a_cmp_b1, a_cmp_w2, a_cmp_b2,
              b_w_in, kv_norm, w_kv_shared, b_fgate, final_norm):
    B, S, _ = x.shape
    pos = jnp.arange(S)
    k_sh = v_sh = dcum = None
    for l in range(DEPTH):
        h = rms_norm(x, attn_norm[l])
        if l < N_A:
            o_mix, q_mem = nsa_mixer(h, a_w_in[l], a_gate_b[l], a_cmp_pos[l], a_cmp_w1[l], a_cmp_b1[l],
                                     a_cmp_w2[l], a_cmp_b2[l], pos)
        else:
            if l == N_A:
                hs = rms_norm(x, kv_norm) @ w_kv_shared
                k_sh = hs[..., :MIX_W].reshape(B, S, N_MIX_HEADS, HEAD_DIM)
                v_sh = hs[..., MIX_W:2 * MIX_W].reshape(B, S, N_MIX_HEADS, HEAD_DIM)
                log_f = jax.nn.log_sigmoid(hs[..., 2 * MIX_W:].astype(jnp.float32) + b_fgate.astype(jnp.float32))
                dcum = jnp.cumsum(log_f, axis=1).transpose(0, 2, 1)
            proj = h @ b_w_in[l - N_A]
            q = proj[..., :MIX_W].reshape(B, S, N_MIX_HEADS, HEAD_DIM)
            q_mem = proj[..., MIX_W:].reshape(B, S, N_MEM_HEADS, HEAD_DIM)
            o_mix = forgetting_attention(q, k_sh, v_sh, dcum)
        o_mem = memory_attention(q_mem, mem, mem_norm[l], w_mem_kv[l])
        x = x + jnp.concatenate([o_mix, o_mem], axis=-1) @ w_o[l]
        x = x + conv_ffn(rms_norm(x, ffn_norm[l]), w_up[l], conv_w[l], conv_b[l], w_down[l])
    return rms_norm(x, final_norm)
```

```python
import numpy as np
from contextlib import ExitStack
import concourse.bass as bass
import concourse.mybir as mybir
from concourse.bass_utils import run_bass_kernel_spmd

F32 = mybir.dt.float32
BF16 = mybir.dt.bfloat16
AF = mybir.ActivationFunctionType
ALU = mybir.AluOpType

D = 1024; S = 2048; DEPTH = 4; NA = 2
DH = 64; NH = 12; NMH = 4; NMEM = 256
DFF = 2816; NCH = 44
TQ = 512; NQB = 4
NCMP = 127
EPS = 1e-6
SCALE = 0.125

def _swap64(cols):
    cols = np.asarray(cols).reshape(-1, 64)
    return np.concatenate([cols[:, 32:], cols[:, :32]], axis=1).reshape(-1)

def nsa_cols():
    q = np.arange(768)
    kv0 = 768
    def kvc(i, g):
        return kv0 + (i * 2 + g) * 64 + np.arange(64)
    fm = []
    qs = _swap64(q)
    for i in range(6):
        fm += [q[128 * i:128 * i + 128], qs[128 * i:128 * i + 128]]
    for i in (2, 4):
        for g in range(2):
            fm += [np.concatenate([kvc(i, g), kvc(i, g)])]
            fm += [np.concatenate([_swap64(kvc(i, g)), _swap64(kvc(i, g))])]
    fm += [np.concatenate([kvc(0, 0), kvc(0, 1)])]
    fm += [np.concatenate([kvc(1, 0), kvc(1, 1)])]
    qm = 768 + 768 + 36 + np.arange(256)
    fm += [qm]
    fmc = np.concatenate(fm)
    gates = 768 + 768 + np.arange(36)
    tm = np.concatenate([kvc(3, 0), kvc(3, 0), kvc(3, 1), kvc(3, 1),
                         kvc(5, 0), kvc(5, 0), kvc(5, 1), kvc(5, 1)])
    return fmc, gates, tm

NSA_FM, NSA_GATE, NSA_TM = nsa_cols()
NSA_NCOL = len(NSA_FM) + 128 + len(NSA_TM)


class Buf:
    __slots__ = ("name", "w", "readers", "excl")
    def __init__(self, name, excl=False):
        self.name = name; self.w = None; self.readers = {}; self.excl = excl


class Ker:
    def __init__(self, nc, stack):
        self.nc = nc
        self.eng = {"pe": nc.tensor, "act": nc.scalar, "dve": nc.vector, "pool": nc.gpsimd, "sp": nc.sync}
        self.sem = {e: stack.enter_context(nc.semaphore("s_" + e)) for e in self.eng}
        self.cnt = {e: 0 for e in self.eng}
        self.seen = {e: {} for e in self.eng}
        self.nds = 8
        self.dsem = {q: [stack.enter_context(nc.semaphore(f"d_{q}{i}")) for i in range(self.nds)]
                     for q in ("sp", "pool")}
        self.dval = {q: [0] * self.nds for q in ("sp", "pool")}
        self.dnext = {"sp": 0, "pool": 0}
        self.nwait = 0
        self.nops = {e: 0 for e in self.eng}
        self.phases = []

    def _need(self, e, tok):
        kind, a, v = tok
        key = (kind, a)
        if self.seen[e].get(key, 0) >= v:
            return
        if kind == "e":
            assert v <= self.cnt[a], f"wait on pending (non-incrementing) instruction of {a}"
            self.eng[e].wait_ge(self.sem[a], v)
        else:
            self.eng[e].wait_ge(self.dsem[a[0]][a[1]], v)
        self.nwait += 1
        self.seen[e][key] = v

    def _deps(self, e, reads, writes):
        for b in reads:
            if b.w is not None:
                if not (b.w[0] == "e" and b.w[1] == e and e == "pe"):
                    self._need(e, b.w)
            if b.excl:
                for (k, a), v in b.readers.items():
                    if k == "e" and a == e:
                        continue
                    self._need(e, (k, a, v))
        for b in writes:
            if b.w is not None and not (b.w[0] == "e" and b.w[1] == e):
                self._need(e, b.w)
            for (k, a), v in b.readers.items():
                if k == "e" and a == e:
                    continue
                self._need(e, (k, a, v))

    def op(self, e, fn, reads=(), writes=(), inc=True):
        self._deps(e, reads, writes)
        ins = fn(self.eng[e])
        self.nops[e] += 1
        if inc:
            ins.then_inc(self.sem[e], 1)
            self.cnt[e] += 1
            c = self.cnt[e]
        else:
            c = self.cnt[e] + 1
        for b in reads:
            k = ("e", e)
            if b.readers.get(k, 0) < c:
                b.readers[k] = c
        for b in writes:
            b.w = ("e", e, c); b.readers = {}
        return ins

    def dma(self, q, out_ap, in_ap, reads=(), writes=()):
        i = self.dnext[q]; self.dnext[q] = (i + 1) % self.nds
        if self.dval[q][i] > 0:
            self._need(q, ("d", (q, i), self.dval[q][i]))
        self._deps(q, reads, writes)
        self.dval[q][i] += 16
        v = self.dval[q][i]
        self.eng[q].dma_start(out=out_ap, in_=in_ap).then_inc(self.dsem[q][i], 16)
        for b in reads:
            b.readers[("d", (q, i))] = v
        for b in writes:
            b.w = ("d", (q, i), v); b.readers = {}

    def barrier(self):
        for e in ("pe", "act", "dve", "pool", "sp"):
            for f in ("pe", "act", "dve", "pool"):
                if f != e and self.cnt[f] > 0:
                    self._need(e, ("e", f, self.cnt[f]))
            for q in ("sp", "pool"):
                for i in range(self.nds):
                    if self.dval[q][i] > 0:
                        self._need(e, ("d", (q, i), self.dval[q][i]))

    def finish(self):
        for q in ("sp", "pool"):
            for i in range(self.nds):
                if self.dval[q][i] > 0:
                    self._need("sp", ("d", (q, i), self.dval[q][i]))


class Rot:
    def __init__(self, items):
        self.items = items; self.i = 0
    def get(self):
        it = self.items[self.i]; self.i = (self.i + 1) % len(self.items)
        return it


def build(cfg):
    layers = cfg.get("layers", [0, 1, 2, 3])
    do_final = cfg.get("final", True)
    mode = cfg.get("mode", "full")
    nc = bass.Bass("TRN2", target_bir_lowering=False)
    st = ExitStack()

    def din(name, shape, dt=F32):
        return nc.dram_tensor(name, list(shape), dt, kind="ExternalInput").ap()

    xT_d = din("xT", [D, S])
    memT_d = din("memT", [D, NMEM])
    gains_d = din("gains", [128, 14, 8])
    wmem_d = din("w_mem_kv", [DEPTH, D, 512])
    wo_d = din("w_o", [DEPTH, D, D])
    wup_d = din("w_up", [DEPTH, D, 2 * DFF])
    wdn_d = din("w_down", [DEPTH, DFF, D])
    cw_d = din("convw", [128, DEPTH, 3, NCH])
    cb_d = din("convb", [128, DEPTH, NCH])
    awin_d = din("a_w_aug", [NA, D, NSA_NCOL])
    agb_d = din("a_gate_b", [36, NA])
    cw1_d = din("cmp_w1", [NA, 2, 128, 32, 256])
    cpos_d = din("cmp_pos", [128, NA, 2, 32])
    cb1_d = din("cmp_b1", [128, NA, 2, 2])
    cw2k_d = din("cmp_w2k", [NA, 256, 256])
    cb2k_d = din("cmp_b2k", [128, NA, 2])
    cw2v_d = din("cmp_w2v", [NA, 256, 128])
    cb2v_d = din("cmp_b2v", [NA, 128])
    bwin_d = din("b_w_in", [NA, D, D])
    wkv_d = din("w_kv_aug", [D, 768 + 768 + 128])
    bfg_d = din("b_fgate_bc", [128, 12])
    cos_d = din("ropecos", [128, S]); sin_d = din("ropesin", [128, S])
    cosc_d = din("ropecosc", [128, 128]); sinc_d = din("ropesinc", [128, 128])
    ident_d = din("ident", [128, 128])
    tri_d = din("tri", [128, 128]); tric_d = din("tric", [128, 128])
    cmpmask_d = din("cmpmask", [128, S])
    overlap_d = din("overlap", [128, 32])
    expand_d = din("expand", [32, 16, 128])
    bonus_d = din("bonus", [128, 16, 32])
    gsel_d = din("rowidx", [36, 128])
    sel127_d = din("sel127", [128, 128])
    out_d = nc.dram_tensor("outT", [D, S], F32, kind="ExternalOutput").ap()
    dumps = {}

    K = Ker(nc, st)

    def sb(name, shape, dt):
        return st.enter_context(nc.sbuf_tensor("sb_" + name, list(shape), dt))

    def ps(name, shape=(128, 512), dt=F32):
        return st.enter_context(nc.psum_tensor("ps_" + name, list(shape), dt))

    xT = sb("xT", [128, 8, S], F32)
    xB = [[Buf(f"x{k}_{c}") for c in range(NQB)] for k in range(8)]
    hT = sb("hT", [128, 8, TQ], BF16); hB = [Buf(f"hT{k}") for k in range(8)]
    rstd = sb("rstd", [128, TQ], F32); rstdB = Buf("rstd")
    gains = sb("gains", [128, 14, 8], F32); gB = Buf("gains")
    NW = 4
    wt = [sb(f"wt{i}", [128, 2048], BF16) for i in range(NW)]
    wrot = Rot([(wt[i], Buf(f"wt{i}")) for i in range(NW)])
    c_ones = sb("c_ones", [128, 128], BF16)
    c_onesm = sb("c_onesm", [128, 128], BF16)
    c_ones32 = sb("c_ones32", [128, 128], F32)
    c_eps = sb("c_eps", [128, 1], F32)
    c_tri = sb("c_tri", [128, 128], BF16); c_tric = sb("c_tric", [128, 128], BF16)
    c_tri32 = sb("c_tri32", [128, 128], F32)
    c_ident = sb("c_ident", [128, 128], F32)
    c_sel127 = sb("c_sel127", [128, 128], F32)
    cB = Buf("consts")
    convw = sb("convw", [128, DEPTH, 3, NCH], F32); convb = sb("convb", [128, DEPTH, NCH], F32)
    qT = sb("qT", [128, 6, TQ], BF16); qB = Buf("qT")
    qmT = sb("qmT", [128, 2, TQ], BF16); qmB = Buf("qmT")
    oT = sb("oT", [128, 8, TQ], BF16); oB = Buf("oT")
    gated = sb("gated", [128, 11, TQ], BF16); gatedB = Buf("gated")
    sq = gated; sqB = gatedB
    ubuf = [sb(f"ubuf{i}", [128, TQ + 2], F32) for i in range(4)]
    ubB = [Buf(f"ubuf{i}") for i in range(4)]
    halo = sb("halo", [128, NCH, 2], F32); haloB = [Buf(f"halo{i}") for i in range(NCH)]
    sil = rstd; silB = rstdB
    E_t = [sb(f"E{i}", [128, TQ], BF16) for i in range(4)]
    Erot = Rot([(E_t[i], Buf(f"E{i}")) for i in range(4)])
    den_sb = [sb(f"den{i}", [128, TQ], F32) for i in range(2)]
    denrot = Rot([(den_sb[i], Buf(f"den{i}")) for i in range(2)])
    den_sb_items = denrot.items
    oacc1 = sb("oacc1", [128, TQ], F32); oaccB = Buf("oacc")
    tmpf = [sb(f"tmpf{i}", [128, TQ], F32) for i in range(2)]
    tmprot = Rot([(tmpf[i], Buf(f"tmpf{i}")) for i in range(2)])
    cacc = tmpf; caccB = [tmprot.items[i][1] for i in range(2)]
    mhT = oT; mhB = oB
    kmT = sb("kmT", [128, 2, NMEM], BF16); kmB = Buf("kmT")
    vm = sb("vm", [128, 2, 256], BF16); vmB = Buf("vm")
    KVBYTES = 48 * 1024
    kvraw = sb("kvraw", [128, KVBYTES // 2], BF16)
    fkT = kvraw[:, 0:6 * S].rearrange("p (k t) -> p k t", k=6); fkB = [Buf(f"fk{c}") for c in range(NQB)]
    fV = kvraw[:, 6 * S:12 * S].rearrange("p (j n) -> p j n", j=16); fVB = [Buf(f"fv{c}") for c in range(NQB)]
    dcum = sb("dcum", [128, 16, 12], F32); dcumB = [Buf(f"dcum{j}") for j in range(16)]
    logf = sb("logf", [128, 16, 12], F32); logfB = [Buf(f"logf{j}") for j in range(16)]
    fbias = logf
    dref = sb("dref", [128, 12], F32); drefB = Buf("dref")
    bfg = sb("bfg", [128, 12], F32)
    o = 0
    def carve(n):
        nonlocal o
        v = kvraw[:, o:o + n]; o += n
        return v
    kslcT = carve(2 * S).rearrange("p (g t) -> p g t", g=2); kwinT = carve(2 * S).rearrange("p (g t) -> p g t", g=2)
    vslc = carve(16 * 256).rearrange("p (j n) -> p j n", j=16); vwin = carve(16 * 256).rearrange("p (j n) -> p j n", j=16)
    ucmp = carve(2 * S).rearrange("p (k t) -> p k t", k=2)
    cmpmask = carve(TQ)
    kcT = carve(2 * 128).rearrange("p (g n) -> p g n", g=2)
    vc = carve(2 * 128).rearrange("p (g n) -> p g n", g=2)
    selT = carve(2 * TQ).rearrange("p (g t) -> p g t", g=2)
    expand = carve(16 * 128).rearrange("p (j s) -> p j s", j=16)
    assert o * 2 <= KVBYTES
    nsaKB = [Buf(f"nsak{c}") for c in range(NQB)]
    cmpB = Buf("cmp"); selB = Buf("selT"); hidB = Buf("hid"); cmB = Buf("cmpmask")
    overlap = sb("overlap", [128, 32], F32)
    bonus = sb("bonus", [128, 4, 32], F32); bonusB = Buf("bonus")
    e36 = sb("e36", [36, TQ], F32); e36B = Buf("e36")
    agb = sb("agb", [36, NA], F32)
    rowidx = sb("rowidx", [36, 128], F32)
    selh = [sb(f"selh{i}", [36, 128], F32) for i in range(2)]
    selhrot = Rot([(selh[i], Buf(f"selh{i}")) for i in range(2)])
    ropeB = Buf("rope")
    cosc = sb("cosc", [128, 128], F32); sinc = sb("sinc", [128, 128], F32)
    cpos = sb("cpos", [128, NA, 2, 32], BF16); cb1 = sb("cb1", [128, NA, 2, 2], F32)
    cb2k = sb("cb2k", [128, NA, 2], F32); cb2v = sb("cb2v", [1, NA, 128], BF16)
    hb = sb("hb", [128, 8], F32); hbB = Buf("hb")
    g_x = [sb(f"g_x{i}", [128, 128], F32) for i in range(4)]; g_xB = [Buf(f"g_x{i}") for i in range(4)]
    imp_s4 = sb("imp_s4", [128, 4, 32], F32); impB = Buf("imp_s")
    sc2 = sb("sc2", [128, 32], F32); sc2B = Buf("sc2")
    mx8 = sb("mx8", [128, 16], F32); mx8B = Buf("mx8")
    sel_s = sb("sel_s", [128, 32], F32); selsB = Buf("sel_s")

    pA = Rot([(ps(f"pA{i}"), Buf(f"pA{i}", True)) for i in range(2)])
    pA2 = Rot([pA.items[0]])
    pS = Rot([(ps(f"pS{i}"), Buf(f"pS{i}", True)) for i in range(2)] + [pA.items[1]])
    pN = Rot([(ps(f"pN{i}"), Buf(f"pN{i}", True)) for i in range(2)])
    pD = Rot([(ps(f"pD{i}"), Buf(f"pD{i}", True)) for i in range(2)])

    def mm(out, lhsT, rhs, start, stop, reads, writes, inc=True):
        return K.op("pe", lambda e: e.matmul(out, lhsT, rhs, start=start, stop=stop), reads, writes, inc)

    def act(out, in_, func, reads, writes, bias=0.0, scale=1.0):
        return K.op("act", lambda e: e.activation(out, in_, func, bias=bias, scale=scale), reads, writes)

    def tt(eng, out, in0, in1, op, reads, writes):
        return K.op(eng, lambda e: e.tensor_tensor(out, in0, in1, op), reads, writes)

    def ts(eng, out, in0, s1, s2, op0, op1, reads, writes):
        if op1 is None:
            return K.op(eng, lambda e: e.tensor_scalar(out, in0, s1, None, op0), reads, writes)
        return K.op(eng, lambda e: e.tensor_scalar(out, in0, s1, s2, op0, op1), reads, writes)

    def stt(eng, out, in0, scalar, in1, op0, op1, reads, writes):
        return K.op(eng, lambda e: e.scalar_tensor_tensor(out, in0, scalar, in1, op0, op1), reads, writes)

    def cp(eng, out, in_, reads, writes):
        return K.op(eng, lambda e: e.tensor_copy(out, in_), reads, writes)

    def _issue256(w_ap, col0, ncols):
        t, tb = wrot.get()
        wv = t[:, 0:8 * 256].rearrange("p (k n) -> p k n", k=8)
        K.dma("pool", wv[:, :, :ncols], w_ap.rearrange("(k p) n -> p k n", p=128)[:, :, col0:col0 + ncols],
              writes=(tb,))
        return wv, tb

    plan_q = []; issued_q = []
    AHEAD = 3

    def plan(specs):
        assert not plan_q and not issued_q
        plan_q.extend(specs)
        while plan_q and len(issued_q) < AHEAD:
            sp = plan_q.pop(0); issued_q.append((sp, _issue256(*sp)))

    def load256(w_ap, col0, ncols=256):
        if not issued_q and not plan_q:
            return _issue256(w_ap, col0, ncols)
        while plan_q and len(issued_q) < 1 + AHEAD:
            sp = plan_q.pop(0); issued_q.append((sp, _issue256(*sp)))
        sp, tile = issued_q.pop(0)
        assert sp[1] == col0 and sp[2] == ncols, (sp[1:], col0, ncols)
        return tile

    def wstream(loads, ahead):
        issued = []
        def get(i):
            while len(issued) < min(len(loads), i + 1 + ahead):
                issued.append(loads[len(issued)]())
            return issued[i]
        return get

    def load_w(dram_ap, view):
        t, b = wrot.get()
        K.dma("pool", view(t), dram_ap, reads=(), writes=(b,))
        return t, b

    K.dma("sp", gains[:], gains_d, writes=(gB,))
    K.dma("sp", convw[:], cw_d, writes=(cB,)); K.dma("sp", convb[:], cb_d, writes=(cB,))
    K.dma("sp", c_ident[:], ident_d, writes=(cB,))
    K.dma("sp", c_tri32[:], tri_d, writes=(cB,))
    K.dma("sp", c_sel127[:], sel127_d, writes=(cB,))
    K.dma("pool", c_tri[:], tri_d, writes=(cB,)); K.dma("pool", c_tric[:], tric_d, writes=(cB,))
    K.dma("sp", bfg[:], bfg_d, writes=(cB,))
    K.dma("sp", overlap[:], overlap_d, writes=(cB,)); pass
    K.dma("sp", rowidx[:], gsel_d, writes=(cB,)); K.dma("sp", agb[:], agb_d, writes=(cB,))
    K.op("dve", lambda e: e.tensor_scalar(agb[:], agb[:], -1.0, None, ALU.mult), reads=(cB,), writes=(cB,))
    K.dma("sp", cosc[:], cosc_d, writes=(cB,)); K.dma("sp", sinc[:], sinc_d, writes=(cB,))
    K.dma("pool", cpos[:], cpos_d, writes=(cB,)); K.dma("sp", cb1[:], cb1_d, writes=(cB,))
    K.dma("sp", cb2k[:], cb2k_d, writes=(cB,))
    K.dma("pool", cb2v[:], cb2v_d.rearrange("(o l) n -> o l n", o=1), writes=(cB,))
    K.op("dve", lambda e: e.memset(c_ones[:], 1.0), writes=(cB,))
    K.op("dve", lambda e: e.memset(c_onesm[:], 1.0 / 1024.0), writes=(cB,))
    K.op("dve", lambda e: e.memset(c_ones32[:], 1.0), writes=(cB,))
    K.op("dve", lambda e: e.memset(c_eps[:], EPS), writes=(cB,))
    K.op("dve", lambda e: e.memset(halo[:], 0.0), writes=tuple(haloB))
    for k in range(8):
        K.dma("sp", xT[:, k, :], xT_d[k * 128:(k + 1) * 128, :], writes=tuple(xB[k]))
    K.barrier()

    def rmsnorm_block(src, srcB, gidx, ncols, dst, dstB, col0=0, src_list=None):
        sl = (lambda k: src_list[k]) if src_list is not None else (lambda k: src[:, k, col0:col0 + ncols])
        for k in range(8):
            K.op("act", lambda e, k=k: e.activation(sq[:, k, :ncols], sl(k), AF.Square),
                 reads=(srcB[k],), writes=(sqB,))
        pt, pb = pA.get()
        for k in range(8):
            mm(pt[:, :ncols], c_onesm[:], sq[:, k, :ncols], k == 0, k == 7, (sqB, cB), (pb,), inc=(k == 7))
        act(rstd[:, :ncols], pt[:, :ncols], AF.Ln, (pb, cB), (rstdB,), bias=c_eps[:, 0:1])
        act(rstd[:, :ncols], rstd[:, :ncols], AF.Exp, (rstdB,), (rstdB,), scale=-0.5)
        for k in range(8):
            stt("dve", dst[:, k, :ncols], sl(k),
                gains[:, gidx, k:k + 1], rstd[:, :ncols], ALU.mult, ALU.mult, (srcB[k], rstdB, gB),
                (dstB[k] if isinstance(dstB, list) else dstB,))

    def proj_fm(wtile, wb, wcol0, ncol, rhsT, rhsB, ncols_tok):
        pt, pb = pA.get()
        for k in range(8):
            mm(pt[:ncol, :ncols_tok], wtile[:, k, wcol0:wcol0 + ncol], rhsT[:, k, :ncols_tok],
               k == 0, k == 7, (wb, rhsB[k] if isinstance(rhsB, list) else rhsB), (pb,), inc=(k == 7))
        return pt, pb

    def ffn_block(l, c):
        xcB = [xB[k][c] for k in range(8)]
        rmsnorm_block(xT, xcB, 4 + l, TQ, hT, hB, col0=c * TQ)
        srcu = wup_d[l].rearrange("(k p) f -> p k f", p=128)
        srcd = wdn_d[l].rearrange("(i p) n -> p i n", p=128)
        loads = []
        def mk_up(c0, npair):
            def f():
                t, tb = wrot.get()
                wv = t[:, 0:8 * 256].rearrange("p (k n) -> p k n", k=8)
                K.dma("pool", wv[:, :, 0:128 * npair], srcu[:, :, c0:c0 + 128 * npair], writes=(tb,))
                return wv, tb
            return f
        def mk_dn(half, f0, nf, nn):
            def f():
                t, tb = wrot.get()
                wv = t[:, 0:nf * 256].rearrange("p (i n) -> p i n", i=nf)
                K.dma("pool", wv, srcd[:, half * 11 + f0:half * 11 + f0 + nf, nn * 256:(nn + 1) * 256], writes=(tb,))
                return wv, tb
            return f
        for half in range(2):
            for pi in range(0, 11, 2):
                npair = min(2, 11 - pi)
                for ab in range(2):
                    loads.append(mk_up(ab * DFF + (half * 11 + pi) * 128, npair))
            for nn in range(4):
                for (f0, nf) in ((0, 6), (6, 5)):
                    loads.append(mk_dn(half, f0, nf, nn))
        wget = wstream(loads, 2)
        li = 0
        for half in range(2):
            for pi in range(0, 11, 2):
                npair = min(2, 11 - pi)
                i0 = half * 11 + pi
                tiles = [wget(li), wget(li + 1)]; li += 2
                for j in range(npair):
                    i = i0 + j
                    accs = []
                    par = (pi + j) % 2
                    for ab in range(2):
                        ch = i + 22 * ab
                        wv, tb = tiles[ab]
                        pt, pb = proj_fm(wv, tb, j * 128, 128, hT, hB, TQ)
                        ui = ab + 2 * par
                        ub, ubb = ubuf[ui], ubB[ui]
                        if c > 0:
                            cp("pool", ub[:, 0:2], halo[:, ch, :], (haloB[ch],), (ubb,))
                        else:
                            K.op("pool", lambda e, ub=ub: e.memset(ub[:, 0:2], 0.0), writes=(ubb,))
                        act(ub[:, 2:TQ + 2], pt[:, :], AF.Copy, (pb,), (ubb,))
                        cp("pool", halo[:, ch, :], ub[:, TQ:TQ + 2], (ubb,), (haloB[ch],))
                        ca, cab = (cacc[ab], caccB[ab]) if par == 0 else den_sb_items[ab]
                        eng = "dve"
                        K.op("act", lambda e, ca=ca, pt=pt, ch=ch: e.activation(
                            ca[:], pt[:, :], AF.Identity, bias=convb[:, l, ch:ch + 1], scale=convw[:, l, 2, ch:ch + 1]),
                            reads=(pb, cB), writes=(cab,))
                        stt(eng, ca[:], ub[:, 1:TQ + 1], convw[:, l, 1, ch:ch + 1], ca[:], ALU.mult, ALU.add,
                            (ubb, cB, cab), (cab,))
                        stt(eng, ca[:], ub[:, 0:TQ], convw[:, l, 0, ch:ch + 1], ca[:], ALU.mult, ALU.add,
                            (ubb, cB, cab), (cab,))
                        accs.append((ca, cab))
                    sl_t, sl_b = (sil, silB) if par == 0 else (oacc1, oaccB)
                    act(sl_t[:], accs[0][0][:], AF.Silu, (accs[0][1],), (sl_b,))
                    tt("pool", gated[:, pi + j, :], sl_t[:], accs[1][0][:], ALU.mult, (sl_b, accs[1][1]), (gatedB,))
            for nn in range(4):
                tiles = [wget(li), wget(li + 1)]; li += 2
                for n2 in range(2):
                    n = nn * 2 + n2
                    pt, pb = pA.get()
                    for i in range(11):
                        wv, tb = tiles[0] if i < 6 else tiles[1]
                        ii = i if i < 6 else i - 6
                        mm(pt[:, :], wv[:, ii, n2 * 128:(n2 + 1) * 128], gated[:, i, :], i == 0, i == 10,
                           (tb, gatedB), (pb,), inc=(i == 10))
                    tt("dve", xT[:, n, c * TQ:(c + 1) * TQ], xT[:, n, c * TQ:(c + 1) * TQ], pt[:, :], ALU.add,
                       (pb, xB[n][c]), (xB[n][c],))

    def mem_kv(l):
        hold = [(tmpf[0], tmprot.items[0][1]), (tmpf[1], tmprot.items[1][1]), den_sb_items[0], den_sb_items[1]]
        srcs = []; srcBs = []
        for k in range(8):
            t_, b_ = hold[k // 2]
            ap_ = t_[:, (k % 2) * 256:(k % 2) * 256 + 256]
            K.dma("sp", ap_, memT_d[k * 128:(k + 1) * 128, :], writes=(b_,))
            srcs.append(ap_); srcBs.append(b_)
        plan([(wmem_d[l], 0, 256), (wmem_d[l], 256, 256)])
        rmsnorm_block(None, srcBs, 8 + l, NMEM, mhT, mhB, src_list=srcs)
        wv, tb = load256(wmem_d[l], 0)
        for ch in range(2):
            pt, pb = proj_fm(wv, tb, ch * 128, 128, mhT, mhB, NMEM)
            act(kmT[:, ch, :], pt[:, :NMEM], AF.Copy, (pb,), (kmB,))
        wv, tb = load256(wmem_d[l], 256)
        for mt in range(2):
            pt, pb = pA.get()
            for k in range(8):
                mm(pt[:, :256], mhT[:, k, mt * 128:(mt + 1) * 128], wv[:, k, 0:256], k == 0, k == 7,
                   (tb, mhB), (pb,), inc=(k == 7))
            act(vm[:, mt, :], pt[:, :256], AF.Copy, (pb,), (vmB,))

    class Pipe:
        def __init__(self, depth=1):
            self.depth = depth; self.pending = []
        def push(self, A, B):
            r = A()
            self.pending.append((B, r))
            while len(self.pending) > self.depth:
                b, rr = self.pending.pop(0); b(rr)
        def flush(self):
            while self.pending:
                b, rr = self.pending.pop(0); b(rr)

    pipe = Pipe(2)

    def attn_tile(kT_ap, kreads, q_ap, qreads, ncol, bias, post, V_ap, vreads, pn_t, pn_b, pd_t, pd_b,
                  first, last, col0, after=None, krows=128, extra=None, preB=None, preB_late=None):
        def A():
            st_t, st_b = pS.get()
            mm(st_t[:krows, :ncol], kT_ap, q_ap, True, extra is None, kreads + qreads, (st_b,))
            if extra is not None:
                mm(st_t[:krows, :ncol], extra[0], extra[1], False, True, extra[2], (st_b,))
            e_t, e_b = Erot.get()
            K.op("act", lambda e: e.activation(e_t[:krows, :ncol], st_t[:krows, :ncol], AF.Exp, bias=bias[0],
                                               scale=SCALE),
                 reads=(st_b,) + bias[1], writes=(e_b,))
            if post is not None:
                post(e_t, e_b)
            if preB is not None:
                preB()
            return e_t, e_b
        def B(r):
            e_t, e_b = r
            if preB_late is not None:
                preB_late()
            mm(pn_t[:, col0:col0 + ncol], V_ap, e_t[:krows, :ncol], first, last, vreads + (e_b,), (pn_b,))
            mm(pd_t[:, col0:col0 + ncol], c_ones[:krows, :], e_t[:krows, :ncol], first, last, (e_b, cB), (pd_b,))
            if after is not None:
                after()
        pipe.push(A, B)

    def mask_sub(mask_ap):
        def post(e_t, e_b):
            tt("pool", e_t[:, 0:128], e_t[:, 0:128], mask_ap, ALU.mult, (e_b, cB), (e_b,))
        return post

    def causal_attention(kT_fn, V_fn, q_ap_fn, c, bias_fn, pr, njt=None, sel_fn=None, after_fn=None, preB=None):
        pn_t, pn_b = pN.get(); pd_t, pd_b = pD.get()
        tiles = list(range(4 * c + 4))
        nt = len(tiles)
        order = [4 * c] + list(range(4 * c)) + [4 * c + 1, 4 * c + 2, 4 * c + 3]
        for idx, j in enumerate(order):
            i = j - 4 * c
            if i < 0:
                col0, ncol, post = 0, TQ, None
            else:
                col0, ncol = 128 * i, TQ - 128 * i
                post = mask_sub(c_tri[:, :])
            extra = None
            if sel_fn is not None:
                post, extra = sel_fn(j, col0, ncol, post)
            kap, kr = kT_fn(j); vap, vr = V_fn(j)
            qap, qr = q_ap_fn(col0, ncol)
            aft = None
            if idx == nt - 1 and after_fn is not None:
                aft = (lambda: after_fn(pn_t, pn_b, pd_t, pd_b))
            attn_tile(kap, kr, qap, qr, ncol, bias_fn(j), post, vap, vr, pn_t, pn_b, pd_t, pd_b,
                      idx == 0, idx == nt - 1, col0, after=aft, extra=extra,
                      preB=(preB if idx == nt - 1 else None))
        return pn_t, pn_b, pd_t, pd_b

    def finish_head_plain(pn_t, pn_b, pd_t, pd_b, pr, dst_ap):
        d_t, d_b = denrot.get()
        act(d_t[pr, :], pd_t[pr, :], AF.Ln, (pd_b,), (d_b,))
        act(d_t[pr, :], d_t[pr, :], AF.Exp, (d_b,), (d_b,), scale=-1.0)
        tt("dve", dst_ap, pn_t[pr, :], d_t[pr, :], ALU.mult, (pn_b, d_b), (oB,))

    def mem_attention(c):
        for hm in range(4):
            ch, off = hm // 2, 64 * (hm % 2)
            pr = slice(off, off + 64)
            pn_t, pn_b = pN.get(); pd_t, pd_b = pD.get()
            for mt in range(2):
                aft = None
                if mt == 1:
                    aft = (lambda pn_t=pn_t, pn_b=pn_b, pd_t=pd_t, pd_b=pd_b, pr=pr, ch=ch:
                           finish_head_plain(pn_t, pn_b, pd_t, pd_b, pr, oT[pr, 6 + ch, :]))
                attn_tile(kmT[pr, ch, mt * 128:(mt + 1) * 128], (kmB,), qmT[pr, ch, :], (qmB,), TQ, (0.0, ()),
                          None, vm[:, mt, ch * 128:(ch + 1) * 128], (vmB,), pn_t, pn_b, pd_t, pd_b,
                          mt == 0, mt == 1, 0, after=aft)
        pipe.flush()

    def wo_block(l, c):
        plan([(wo_d[l], nn * 256, 256) for nn in range(4)])
        for nn in range(4):
            wv, tb = load256(wo_d[l], nn * 256)
            for n2 in range(2):
                n = nn * 2 + n2
                pt, pb = proj_fm(wv, tb, n2 * 128, 128, oT, oB, TQ)
                tt("dve", xT[:, n, c * TQ:(c + 1) * TQ], xT[:, n, c * TQ:(c + 1) * TQ], pt[:, :], ALU.add,
                   (pb, xB[n][c]), (xB[n][c],))

    def fox_shared_kv(c):
        plan([(wkv_d, cc * 256, 256) for cc in range(3)]
             + [(wkv_d, 768 + cc * 256, 256 if cc < 3 else 128) for cc in range(4)])
        xcB = [xB[k][c] for k in range(8)]
        rmsnorm_block(xT, xcB, 12, TQ, hT, hB, col0=c * TQ)
        for cc in range(3):
            wv, tb = load256(wkv_d, cc * 256)
            for j in range(2):
                ch = cc * 2 + j
                pt, pb = proj_fm(wv, tb, j * 128, 128, hT, hB, TQ)
                act(fkT[:, ch, c * TQ:(c + 1) * TQ], pt[:, :], AF.Copy, (pb,), (fkB[c],))
        for cc in range(4):
            ncols = 256 if cc < 3 else 128
            wv, tb = load256(wkv_d, 768 + cc * 256, ncols)
            for jt in range(4):
                j = 4 * c + jt
                pt, pb = pA.get()
                for k in range(8):
                    mm(pt[:, :ncols], hT[:, k, jt * 128:(jt + 1) * 128], wv[:, k, :ncols], k == 0, k == 7,
                       (tb, hB[k]), (pb,), inc=(k == 7))
                if cc < 3:
                    act(fV[:, j, cc * 256:(cc + 1) * 256], pt[:, :256], AF.Copy, (pb,), (fVB[c],))
                else:
                    tt("dve", logf[:, j, :], pt[:, 0:12], bfg[:], ALU.add, (pb, cB), (logfB[j],))
                    if cfg.get("dbg", 0) == 3:
                        continue
                    act(logf[:, j, :], logf[:, j, :], AF.Exp, (logfB[j],), (logfB[j],), scale=-1.0)
                    if cfg.get("dbg", 0) == 4:
                        continue
                    act(logf[:, j, :], logf[:, j, :], AF.Ln, (logfB[j], cB), (logfB[j],), bias=c_ones32[:, 0:1])
                    if cfg.get("dbg", 0) == 5:
                        continue
                    ts("dve", logf[:, j, :], logf[:, j, :], -1.0, None, ALU.mult, None, (logfB[j],), (logfB[j],))
        for jt in range(4):
            if cfg.get("dbg", 0) == 1:
                break
            j = 4 * c + jt
            pt, pb = pA.get()
            for jj in range(j + 1):
                lhs = c_tri32[:] if jj == j else c_ones32[:]
                mm(pt[:, :12], lhs, logf[:, jj, :], jj == 0, jj == j, (cB, logfB[jj]), (pb,), inc=(jj == j))
            act(dcum[:, j, :], pt[:, :12], AF.Copy, (pb,), (dcumB[j],))

    def fox_attention(l, c):
        pt, pb = pA2.get()
        mm(pt[:, :12], c_sel127[:], dcum[:, 4 * c + 1, :], True, True, (cB, dcumB[4 * c + 1]), (pb,))
        act(dref[:], pt[:, :12], AF.Copy, (pb,), (drefB,))
        for j in range(4 * c + 4):
            tt("pool", fbias[:, j, :], dref[:], dcum[:, j, :], ALU.subtract, (drefB, dcumB[j]), (logfB[j],))
        for h in range(NH):
            ch, off = h // 2, 64 * (h % 2)
            pr = slice(off, off + 64)
            causal_attention(
                lambda j, pr=pr, ch=ch: (fkT[pr, ch, j * 128:(j + 1) * 128], (fkB[j // 4],)),
                lambda j, ch=ch: (fV[:, j, ch * 128:(ch + 1) * 128], (fVB[j // 4],)),
                lambda col0, ncol, pr=pr, ch=ch: (qT[pr, ch, col0:col0 + ncol], (qB,)),
                c, lambda j, h=h: (fbias[:, j, h:h + 1], (logfB[j],)), pr,
                after_fn=(lambda a, b, c_, d, pr=pr, ch=ch: finish_head_plain(a, b, c_, d, pr, oT[pr, ch, :])))
        pipe.flush()

    def fox_q_proj(l, c):
        plan([(bwin_d[l - NA], cc * 256, 256) for cc in range(4)])
        xcB = [xB[k][c] for k in range(8)]
        rmsnorm_block(xT, xcB, l, TQ, hT, hB, col0=c * TQ)
        for cc in range(4):
            wv, tb = load256(bwin_d[l - NA], cc * 256)
            for j in range(2):
                ch = cc * 2 + j
                pt, pb = proj_fm(wv, tb, j * 128, 128, hT, hB, TQ)
                if ch < 6:
                    act(qT[:, ch, :], pt[:, :], AF.Copy, (pb,), (qB,))
                else:
                    act(qmT[:, ch - 6, :], pt[:, :], AF.Copy, (pb,), (qmB,))

    def rope_evac(pt, pb, pts, pbs, dst_ap, dstB, cs, sn, rB, ncols):
        t1, t1b = tmprot.get()
        tt("dve", t1[:, :ncols], pt[:, :ncols], cs, ALU.mult, (pb,) + rB, (t1b,))
        t2, t2b = tmprot.get()
        tt("dve", t2[:, :ncols], pts[:, :ncols], sn, ALU.mult, (pbs,) + rB, (t2b,))
        tt("pool", dst_ap, t1[:, :ncols], t2[:, :ncols], ALU.add, (t1b, t2b), (dstB,))

    def nsa_load_w(l, col0, ncols):
        return load256(awin_d[l], col0, ncols)

    def nsa_rope_tables(c):
        rc, rcb = denrot.items[0]; rs, rsb = denrot.items[1]
        K.dma("sp", rc[:], cos_d[:, c * TQ:(c + 1) * TQ], writes=(rcb,))
        K.dma("sp", rs[:], sin_d[:, c * TQ:(c + 1) * TQ], writes=(rsb,))
        return rc, rs, (rcb, rsb)

    def nsa_kv_proj(l, c):
        plan([(awin_d[l], (6 + 2 * bi + g) * 256, 256) for bi in range(2) for g in range(2)]
             + [(awin_d[l], 20 * 128, 256)] + [(awin_d[l], 3072 + 128 + vi * 256, 256) for vi in range(2)])
        xcB = [xB[k][c] for k in range(8)]
        rmsnorm_block(xT, xcB, l, TQ, hT, hB, col0=c * TQ)
        rc, rs, rB = nsa_rope_tables(c)
        tsl = slice(c * TQ, (c + 1) * TQ)
        for bi, dstT in ((0, kslcT), (1, kwinT)):
            for g in range(2):
                wv, tb = nsa_load_w(l, (6 + 2 * bi + g) * 256, 256)
                pt, pb = proj_fm(wv, tb, 0, 128, hT, hB, TQ)
                pts, pbs = proj_fm(wv, tb, 128, 128, hT, hB, TQ)
                rope_evac(pt, pb, pts, pbs, dstT[:, g, tsl], nsaKB[c], rc[:], rs[:], rB, TQ)
        wv, tb = nsa_load_w(l, 20 * 128, 256)
        for kv in range(2):
            pt, pb = proj_fm(wv, tb, kv * 128, 128, hT, hB, TQ)
            act(ucmp[:, kv, tsl], pt[:, :], AF.Copy, (pb,), (nsaKB[c],))
        for vi, vdst in ((0, vslc), (1, vwin)):
            wv, tb = nsa_load_w(l, 3072 + 128 + vi * 256, 256)
            for jt in range(4):
                j = 4 * c + jt
                pt, pb = pA.get()
                for k in range(8):
                    mm(pt[:, :256], hT[:, k, jt * 128:(jt + 1) * 128], wv[:, k, :], k == 0, k == 7, (tb, hB[k]), (pb,),
                       inc=(k == 7))
                act(vdst[:, j, :], pt[:, 0:256], AF.Copy, (pb,), (nsaKB[c],))

    def gelu_tanh(dst_ap, dstB, pt, pb, bias_ap, npart, ncols):
        x, xb = g_x[0], g_xB[0]; x2, x2b = g_x[1], g_xB[1]; th, thb = g_x[2], g_xB[2]
        act(x[:npart, :ncols], pt[:npart, :ncols], AF.Identity, (pb, cB), (xb,), bias=bias_ap)
        tt("dve", x2[:npart, :ncols], x[:npart, :ncols], x[:npart, :ncols], ALU.mult, (xb,), (x2b,))
        ts("dve", x2[:npart, :ncols], x2[:npart, :ncols], 0.044715, 1.0, ALU.mult, ALU.add, (x2b,), (x2b,))
        tt("dve", x2[:npart, :ncols], x2[:npart, :ncols], x[:npart, :ncols], ALU.mult, (x2b, xb), (x2b,))
        act(th[:npart, :ncols], x2[:npart, :ncols], AF.Tanh, (x2b,), (thb,), scale=0.7978845608028654)
        ts("dve", th[:npart, :ncols], th[:npart, :ncols], 1.0, 0.5, ALU.add, ALU.mult, (thb,), (thb,))
        tt("dve", dst_ap, th[:npart, :ncols], x[:npart, :ncols], ALU.mult, (thb, xb), (dstB,))

    def nsa_compress(l):
        allk = tuple(nsaKB)
        K.op("dve", lambda e: e.memset(expand, 0.0), writes=(cmpB,))
        K.op("dve", lambda e: e.memset(selT, 0.0), writes=(selB,))
        K.dma("pool", expand[0:32], expand_d, writes=(cmpB,))
        for kv in range(2):
            halves = []
            for hf in range(4):
                t, tb = wrot.get()
                wv = t[:, 0:8 * 256].rearrange("p (l n) -> p l n", l=8)
                K.dma("pool", wv, cw1_d[l, kv][:, hf * 8:(hf + 1) * 8, :], writes=(tb,))
                halves.append((wv, tb))
            for hc in range(2):
                pt, pb = pA.get()
                for li in range(32):
                    wv, tb = halves[li // 8]
                    mm(pt[:, 0:1], wv[0:64, li % 8, hc * 128:(hc + 1) * 128], cpos[0:64, l, kv, li:li + 1],
                       li == 0, li == 31, (tb, cB), (pb,), inc=(li == 31))
                tt("dve", hb[:, hc:hc + 1], pt[:, 0:1], cb1[:, l, kv, hc:hc + 1], ALU.add, (pb, cB), (hbB,))
                for g in range(2):
                    pr = slice(64 * g, 64 * g + 64)
                    pt2, pb2 = pA.get()
                    for li in range(32):
                        wv, tb = halves[li // 8]
                        rhs = ucmp[pr, kv, li:li + 16 * (NCMP - 1) + 1:16]
                        mm(pt2[:, :NCMP], wv[pr, li % 8, hc * 128:(hc + 1) * 128], rhs, li == 0, li == 31,
                           (tb,) + allk, (pb2,), inc=(li == 31))
                    gelu_tanh(hid_g[g][:, hc, :NCMP], hidB, pt2, pb2, hb[:, hc:hc + 1], 128, NCMP)
            if kv == 0:
                t, tb = wrot.get()
                wv = t[:, 0:2 * 256].rearrange("p (k n) -> p k n", k=2)
                K.dma("pool", wv, cw2k_d[l].rearrange("(k p) n -> p k n", p=128), writes=(tb,))
                for g in range(2):
                    pt, pb = pA.get(); pts, pbs = pA.get()
                    for hc in range(2):
                        mm(pt[:, :NCMP], wv[:, hc, 0:128], hid_g[g][:, hc, :NCMP], hc == 0, hc == 1, (tb, hidB),
                           (pb,))
                    for hc in range(2):
                        mm(pts[:, :NCMP], wv[:, hc, 128:256], hid_g[g][:, hc, :NCMP], hc == 0, hc == 1, (tb, hidB),
                           (pbs,))
                    a, ab_ = g_x[0], g_xB[0]; b, bb_ = g_x[1], g_xB[1]
                    act(a[:, :NCMP], pt[:, :NCMP], AF.Identity, (pb, cB), (ab_,), bias=cb2k[:, l, 0:1])
                    act(b[:, :NCMP], pts[:, :NCMP], AF.Identity, (pbs, cB), (bb_,), bias=cb2k[:, l, 1:2])
                    tt("dve", a[:, :NCMP], a[:, :NCMP], cosc[:, :NCMP], ALU.mult, (ab_, cB), (ab_,))
                    tt("dve", b[:, :NCMP], b[:, :NCMP], sinc[:, :NCMP], ALU.mult, (bb_, cB), (bb_,))
                    tt("dve", kcT[:, g, :NCMP], a[:, :NCMP], b[:, :NCMP], ALU.add, (ab_, bb_), (cmpB,))
            else:
                t, tb = wrot.get()
                wv = t[:, 0:2 * 128].rearrange("p (k n) -> p k n", k=2)
                K.dma("pool", wv, cw2v_d[l].rearrange("(k p) n -> p k n", p=128), writes=(tb,))
                for g in range(2):
                    pt, pb = pA.get()
                    for hc in range(2):
                        mm(pt[:NCMP, :128], hid_g[g][:, hc, :NCMP], wv[:, hc, :], hc == 0, False, (tb, hidB), (pb,),
                           inc=False)
                    mm(pt[:NCMP, :128], c_ones[0:1, :NCMP], cb2v[0:1, l, :], False, True, (cB,), (pb,))
                    act(vc[:NCMP, g, :], pt[:NCMP, :128], AF.Copy, (pb,), (cmpB,))

    hid_g = [sb(f"hid_g{g}", [128, 2, 128], BF16) for g in range(2)]

    def nsa_q_proj(l, c):
        plan([(awin_d[l], ch * 256, 256) for ch in range(6)] + [(awin_d[l], 22 * 128, 256), (awin_d[l], 3072, 128)])
        xcB = [xB[k][c] for k in range(8)]
        rmsnorm_block(xT, xcB, l, TQ, hT, hB, col0=c * TQ)
        rc, rs, rB = nsa_rope_tables(c)
        for ch in range(6):
            wv, tb = nsa_load_w(l, ch * 256, 256)
            pt, pb = proj_fm(wv, tb, 0, 128, hT, hB, TQ)
            pts, pbs = proj_fm(wv, tb, 128, 128, hT, hB, TQ)
            rope_evac(pt, pb, pts, pbs, qT[:, ch, :], qB, rc[:], rs[:], rB, TQ)
        wv, tb = nsa_load_w(l, 22 * 128, 256)
        for j in range(2):
            pt, pb = proj_fm(wv, tb, j * 128, 128, hT, hB, TQ)
            act(qmT[:, j, :], pt[:, :], AF.Copy, (pb,), (qmB,))
        wv, tb = nsa_load_w(l, 3072, 128)
        pt, pb = proj_fm(wv, tb, 0, 36, hT, hB, TQ)
        act(e36[:, :], pt[:36, :], AF.Exp, (pb, cB), (e36B,), bias=agb[:, l:l + 1], scale=-1.0)

    def nsa_attention(l, c):
        K.dma("pool", cmpmask, cmpmask_d[:, c * TQ:(c + 1) * TQ], writes=(cmB,))
        K.dma("sp", bonus[:], bonus_d[:, 4 * c:4 * c + 4, :], writes=(bonusB,))
        use_sel = c >= 2

        def cmp_scores(h, g, pr, ch):
            st_t, st_b = pS.get()
            mm(st_t[:NCMP, :], kcT[pr, g, :NCMP], qT[pr, ch, :], True, True, (cmpB, qB), (st_b,))
            e_t, e_b = Erot.get()
            act(e_t[:NCMP, :], st_t[:NCMP, :], AF.Exp, (st_b,), (e_b,), scale=SCALE)
            tt("dve", e_t[:NCMP, :], e_t[:NCMP, :], cmpmask[:NCMP, :], ALU.mult, (e_b, cmB), (e_b,))
            pd_t, pd_b = pD.get()
            mm(pd_t[:, :], c_ones[:NCMP, :], e_t[:NCMP, :], True, True, (cB, e_b), (pd_b,))
            d_t, d_b = denrot.get()
            ts("dve", d_t[:, :], pd_t[:, :], 1e-30, None, ALU.max, None, (pd_b,), (d_b,))
            act(d_t[:, :], d_t[:, :], AF.Ln, (d_b,), (d_b,))
            act(d_t[:, :], d_t[:, :], AF.Exp, (d_b,), (d_b,), scale=-1.0)
            return e_t, e_b, d_t, d_b

        for g in range(2):
            heads = list(range(6 * g, 6 * g + 6))
            if use_sel:
                pipe.flush()
                ip_t, ip_b = pA2.get()
                for h in heads:
                    ch, off = h // 2, 64 * (h % 2)
                    pr = slice(off, off + 64)
                    e_t, e_b, d_t, d_b = cmp_scores(h, g, pr, ch)
                    pn, pnB = tmprot.get()
                    tt("dve", pn[:NCMP, :], e_t[:NCMP, :], d_t[:NCMP, :], ALU.mult, (e_b, d_b), (pnB,))
                    for tt_i in range(4):
                        mm(ip_t[:, tt_i * 32:(tt_i + 1) * 32], pn[:NCMP, tt_i * 128:(tt_i + 1) * 128],
                           overlap[:NCMP, :], h == heads[0] and tt_i == 0, h == heads[-1] and tt_i == 3,
                           (pnB, cB), (ip_b,))
                for tt_i in range(4):
                    tt("dve", imp_s4[:, tt_i, :], ip_t[:, tt_i * 32:(tt_i + 1) * 32], bonus[:, tt_i, :], ALU.add,
                       (ip_b, bonusB), (impB,))
                for tt_i in range(4):
                    imp_s = imp_s4[:, tt_i, :]
                    K.op("dve", lambda e: e.max(out=mx8[:, 0:8], in_=imp_s), reads=(impB,), writes=(mx8B,))
                    K.op("dve", lambda e: e.match_replace(out=sc2[:, :], in_to_replace=mx8[:, 0:8],
                                                          in_values=imp_s, imm_value=-3.0e38),
                         reads=(impB, mx8B), writes=(sc2B,))
                    K.op("dve", lambda e: e.max(out=mx8[:, 8:16], in_=sc2[:, :]), reads=(sc2B,), writes=(mx8B,))
                    ts("dve", sel_s[:, :], imp_s, mx8[:, 15:16], None, ALU.is_ge, None, (impB, mx8B),
                       (selsB,))
                    ts("dve", sel_s[:, :], sel_s[:, :], -1.0, 30000.0, ALU.add, ALU.mult, (selsB,), (selsB,))
                    tp_t, tp_b = pA2.get()
                    K.op("pe", lambda e, tp_t=tp_t: e.transpose(tp_t[:32, :128], sel_s[:, :], c_ident[:, :]),
                         reads=(selsB, cB), writes=(tp_b,))
                    act(selT[:32, g, tt_i * 128:(tt_i + 1) * 128], tp_t[:32, :128], AF.Copy, (tp_b,), (selB,))
            for h in heads:
                ch, off = h // 2, 64 * (h % 2)
                pr = slice(off, off + 64)
                qfn = lambda col0, ncol, pr=pr, ch=ch: (qT[pr, ch, col0:col0 + ncol], (qB,))
                head_gates(l, h)

                def fin(br, last, h=h, pr=pr, ch=ch):
                    def f(pn_t, pn_b, pd_t, pd_b):
                        d_t, d_b = denrot.get()
                        ts("dve", d_t[pr, :], pd_t[pr, :], 1e-30, None, ALU.max, None, (pd_b,), (d_b,))
                        finish_gated(h, br, pn_t, pn_b, d_t, d_b, pr, first=(br == 0), last=last,
                                     dst=oT[pr, ch, :])
                    return f

                pn_t, pn_b = pN.get(); pd_t, pd_b = pD.get()
                f0 = fin(0, False)
                attn_tile(kcT[pr, g, :NCMP], (cmpB,), qT[pr, ch, :], (qB,), TQ, (0.0, ()),
                          (lambda e_t, e_b: tt("dve", e_t[:NCMP, :], e_t[:NCMP, :], cmpmask[:NCMP, :], ALU.mult,
                                               (e_b, cmB), (e_b,))),
                          vc[:NCMP, g, :], (cmpB,), pn_t, pn_b, pd_t, pd_b, True, True, 0,
                          after=(lambda f0=f0, a=pn_t, b=pn_b, c_=pd_t, d=pd_b: f0(a, b, c_, d)), krows=NCMP,
                          preB_late=(None if cfg.get("br_only") is not None else (lambda h=h: prep_gate(h, 0))))

                def sel_fn(j, col0, ncol, post0, g=g):
                    if not use_sel:
                        return post0, None
                    return post0, (expand[:, j, :], selT[:, g, col0:col0 + ncol], (cmpB, selB))
                causal_attention(
                    lambda j, pr=pr, g=g: (kslcT[pr, g, j * 128:(j + 1) * 128], (nsaKB[j // 4],)),
                    lambda j, g=g: (vslc[:, j, g * 128:(g + 1) * 128], (nsaKB[j // 4],)),
                    qfn, c, lambda j: (0.0, ()), pr, sel_fn=sel_fn, after_fn=fin(1, False),
                    preB=(None if cfg.get("br_only") is not None else (lambda h=h: prep_gate(h, 1))))
                pn_t, pn_b = pN.get(); pd_t, pd_b = pD.get()
                order = [4 * c] + [j for j in range(4 * c - 4, 4 * c) if j >= 0] + [4 * c + 1, 4 * c + 2, 4 * c + 3]
                f2 = fin(2, True)
                for idx, j in enumerate(order):
                    i = j - 4 * c
                    if i >= 0:
                        col0, ncol = 128 * i, TQ - 128 * i
                        post = mask_sub(c_tri[:, :])
                    else:
                        ii = i + 4
                        col0, ncol = 0, 128 * (ii + 1)
                        def post(e_t, e_b, ii=ii):
                            tt("pool", e_t[:, 128 * ii:128 * ii + 128], e_t[:, 128 * ii:128 * ii + 128],
                               c_tric[:, :], ALU.mult, (e_b, cB), (e_b,))
                    aft = None
                    if idx == len(order) - 1:
                        aft = (lambda f2=f2, a=pn_t, b=pn_b, c_=pd_t, d=pd_b: f2(a, b, c_, d))
                    attn_tile(kwinT[pr, g, j * 128:(j + 1) * 128], (nsaKB[j // 4],),
                              qT[pr, ch, col0:col0 + ncol], (qB,), ncol, (0.0, ()), post,
                              vwin[:, j, g * 128:(g + 1) * 128], (nsaKB[j // 4],), pn_t, pn_b, pd_t, pd_b,
                              idx == 0, idx == len(order) - 1, col0, after=aft,
                              preB=((lambda h=h: prep_gate(h, 2))
                                    if (idx == len(order) - 1 and cfg.get("br_only") is None) else None))
        pipe.flush()

    def head_gates(l, h):
        return

    gate_bc = {}

    def prep_gate(h, br):
        s_t, s_b = selhrot.get()
        ts("dve", s_t[:, :], rowidx[:, :], float(3 * h + br), None, ALU.is_equal, None, (cB,), (s_b,))
        gp_t, gp_b = pA2.get()
        mm(gp_t[:, :], s_t[:36, :], e36[:36, :], True, True, (s_b, e36B), (gp_b,))
        gate_bc[(h, br)] = (gp_t, gp_b)

    def finish_gated(h, br, pn_t, pn_b, d_t, d_b, pr, first, last=False, dst=None):
        if cfg.get("br_only") is not None:
            if br == cfg["br_only"]:
                ch_ = h // 2
                K.op("dve", lambda e: e.reciprocal(d_t[pr, :], d_t[pr, :]), reads=(d_b,), writes=(d_b,))
                tt("dve", oT[pr, ch_, :], pn_t[pr, :], d_t[pr, :], ALU.mult, (pn_b, d_b), (oB,))
            return
        gp_t, gp_b = gate_bc.pop((h, br))
        oacc = oacc1
        w_t, w_b = tmprot.get()
        stt("dve", w_t[pr, :], gp_t[pr, :], 1.0, d_t[pr, :], ALU.add, ALU.mult, (gp_b, d_b), (w_b,))
        act(w_t[pr, :], w_t[pr, :], AF.Ln, (w_b,), (w_b,))
        act(w_t[pr, :], w_t[pr, :], AF.Exp, (w_b,), (w_b,), scale=-1.0)
        if first:
            tt("dve", oacc[pr, :], pn_t[pr, :], w_t[pr, :], ALU.mult, (pn_b, w_b), (oaccB,))
            return
        tt("dve", w_t[pr, :], pn_t[pr, :], w_t[pr, :], ALU.mult, (pn_b, w_b), (w_b,))
        if last:
            tt("dve", dst, oacc[pr, :], w_t[pr, :], ALU.add, (oaccB, w_b), (oB,))
        else:
            tt("dve", oacc[pr, :], oacc[pr, :], w_t[pr, :], ALU.add, (oaccB, w_b), (oaccB,))

    skip = set(cfg.get("skip", ()))
    def maybe(fn):
        def w(*a):
            if fn.__name__ in skip:
                return
            K.phases.append((fn.__name__, a, dict(K.nops)))
            return fn(*a)
        return w
    mem_kv = maybe(mem_kv); nsa_kv_proj = maybe(nsa_kv_proj); nsa_compress = maybe(nsa_compress)
    nsa_q_proj = maybe(nsa_q_proj); nsa_attention = maybe(nsa_attention); mem_attention = maybe(mem_attention)
    wo_block = maybe(wo_block); ffn_block = maybe(ffn_block); fox_shared_kv = maybe(fox_shared_kv)
    fox_q_proj = maybe(fox_q_proj); fox_attention = maybe(fox_attention)
    for l in layers:
        if mode == "ffn":
            for c in range(NQB):
                ffn_block(l, c)
            continue
        mem_kv(l)
        if l < NA:
            for c in range(NQB):
                nsa_kv_proj(l, c)
            nsa_compress(l)
            for c in range(NQB):
                nsa_q_proj(l, c)
                nsa_attention(l, c)
                mem_attention(c)
                wo_block(l, c)
                if mode != "attn":
                    ffn_block(l, c)
            K.barrier()
        else:
            if l == NA:
                K.barrier()
                for c in range(NQB):
                    fox_shared_kv(c)
            for c in range(NQB):
                fox_q_proj(l, c)
                fox_attention(l, c)
                mem_attention(c)
                wo_block(l, c)
                if mode != "attn":
                    ffn_block(l, c)
    if do_final:
        for c in range(NQB):
            xcB = [xB[k][c] for k in range(8)]
            for k in range(8):
                K.op("act", lambda e, k=k: e.activation(sq[:, k, :], xT[:, k, c * TQ:(c + 1) * TQ], AF.Square),
                     reads=(xcB[k],), writes=(sqB,))
            pt, pb = pA.get()
            for k in range(8):
                mm(pt[:, :], c_onesm[:], sq[:, k, :], k == 0, k == 7, (sqB, cB), (pb,), inc=(k == 7))
            act(rstd[:, :], pt[:, :], AF.Ln, (pb, cB), (rstdB,), bias=c_eps[:, 0:1])
            act(rstd[:, :], rstd[:, :], AF.Exp, (rstdB,), (rstdB,), scale=-0.5)
            for k in range(8):
                stt("dve", xT[:, k, c * TQ:(c + 1) * TQ], xT[:, k, c * TQ:(c + 1) * TQ], gains[:, 13, k:k + 1],
                    rstd[:, :], ALU.mult, ALU.mult, (xcB[k], rstdB, gB), (xcB[k],))
    for k in range(8):
        K.dma("sp", out_d[k * 128:(k + 1) * 128, :], xT[:, k, :], reads=tuple(xB[k]))
    if cfg.get("dump"):
        K.barrier()
        dl = {"oT": (oT, [128, 8, TQ], BF16), "qT": (qT, [128, 6, TQ], BF16), "qmT": (qmT, [128, 2, TQ], BF16),
              "kslcT": (kslcT, [128, 2, S], BF16), "kwinT": (kwinT, [128, 2, S], BF16),
              "vslc": (vslc, [128, 16, 256], BF16), "vwin": (vwin, [128, 16, 256], BF16),
              "ucmp": (ucmp, [128, 2, S], BF16), "kcT": (kcT, [128, 2, 128], BF16), "vc": (vc, [128, 2, 128], BF16),
              "selT": (selT, [128, 2, TQ], BF16), "kmT": (kmT, [128, 2, NMEM], BF16), "vm": (vm, [128, 2, 256], BF16),
              "hT": (hT, [128, 8, TQ], BF16), "hid0": (hid_g[0], [128, 2, 128], BF16)}
        for nm in cfg["dump"]:
            t_, shp, dt_ = dl[nm]
            dd = nc.dram_tensor("dump_" + nm, shp, dt_, kind="ExternalOutput").ap()
            K.dma("sp", dd, t_ if not hasattr(t_, "ap") else t_[:], reads=())
    K.finish()
    st.close()
    return nc, K


def _consts():
    c = {}
    c["ident"] = np.eye(128, dtype=np.float32)
    s = np.arange(128)[:, None]; t = np.arange(128)[None, :]
    c["tri"] = (s <= t).astype(np.float32)
    c["tric"] = (t < s).astype(np.float32)
    n = np.arange(128)[:, None]; tt = np.arange(S)[None, :]
    c["cmpmask"] = ((16 * n + 31 <= tt) & (n < NCMP)).astype(np.float32)
    cs = np.arange(128) * 16
    ss = np.arange(32) * 64
    ov = ((cs[:, None] < ss[None, :] + 64) & (cs[:, None] + 32 > ss[None, :])).astype(np.float32)
    ov[NCMP:] = 0
    c["overlap"] = ov
    ex = np.zeros((32, 16, 128), np.float32)
    for jt in range(16):
        for s_ in range(128):
            ex[2 * jt + s_ // 64, jt, s_] = 1.0
    c["expand"] = ex
    bn = np.zeros((128, 16, 32), np.float32)
    for tix in range(16):
        tpos = tix * 128 + np.arange(128)
        blk = tpos // 64
        j = np.arange(32)[None, :]
        forced = (j == 0) | (j == blk[:, None]) | (j == blk[:, None] - 1)
        valid = j <= blk[:, None]
        bn[:, tix, :] = np.where(valid, 1e4 * forced, -1e30)
    c["bonus"] = bn
    c["rowidx"] = np.broadcast_to(np.arange(36, dtype=np.float32)[:, None], (36, 128)).copy()
    s127 = np.zeros((128, 128), np.float32); s127[127, :] = 1.0
    c["sel127"] = s127
    half = 32
    inv = (10000.0 ** (-np.arange(half, dtype=np.float32) / half)).astype(np.float32)
    def tables(pos):
        ang = pos.astype(np.float32)[None, :] * inv[:, None]
        co = np.cos(ang).astype(np.float32); si = np.sin(ang).astype(np.float32)
        cos64 = np.concatenate([co, co], 0); sin64 = np.concatenate([-si, si], 0)
        return np.concatenate([cos64, cos64], 0), np.concatenate([sin64, sin64], 0)
    c["ropecos"], c["ropesin"] = tables(np.arange(S))
    pc = np.arange(128) * 16 + 31
    c["ropecosc"], c["ropesinc"] = tables(pc)
    return {k: np.ascontiguousarray(v, dtype=np.float32) for k, v in c.items()}


def _fm(vec_list):
    a = np.stack(vec_list, 0).reshape(len(vec_list), 8, 128)
    return np.ascontiguousarray(a.transpose(2, 0, 1))


def prep_shared(inp):
    f = lambda a: np.ascontiguousarray(np.asarray(a, dtype=np.float32))
    sh = dict(_consts())
    gl = [inp["attn_norm"][i] for i in range(4)] + [inp["ffn_norm"][i] for i in range(4)] + \
         [inp["mem_norm"][i] for i in range(4)] + [inp["kv_norm"], inp["final_norm"]]
    sh["gains"] = _fm([np.asarray(g) for g in gl])
    sh["w_mem_kv"] = f(inp["w_mem_kv"]); sh["w_o"] = f(inp["w_o"]); sh["w_up"] = f(inp["w_up"])
    sh["w_down"] = f(inp["w_down"])
    cw = np.asarray(inp["conv_w"]).reshape(DEPTH, 3, NCH, 128)
    sh["convw"] = f(cw.transpose(3, 0, 1, 2))
    sh["convb"] = f(np.asarray(inp["conv_b"]).reshape(DEPTH, NCH, 128).transpose(2, 0, 1))
    aw = np.asarray(inp["a_w_in"])
    aug = np.zeros((NA, D, NSA_NCOL), np.float32)
    aug[:, :, :3072] = aw[:, :, NSA_FM]
    aug[:, :, 3072:3072 + 36] = aw[:, :, NSA_GATE]
    aug[:, :, 3200:] = aw[:, :, NSA_TM]
    sh["a_w_aug"] = aug
    sh["a_gate_b"] = f(np.asarray(inp["a_gate_b"]).T)
    w1 = np.asarray(inp["a_cmp_w1"]).reshape(NA, 2, 32, 64, 256).transpose(0, 1, 3, 2, 4)
    sh["cmp_w1"] = f(np.concatenate([w1, w1], axis=2))
    pos = np.asarray(inp["a_cmp_pos"]).transpose(3, 0, 1, 2)
    sh["cmp_pos"] = f(np.concatenate([pos, pos], 0))
    sh["cmp_b1"] = f(np.asarray(inp["a_cmp_b1"]).reshape(NA, 2, 2, 128).transpose(3, 0, 1, 2))
    w2 = np.asarray(inp["a_cmp_w2"]); b2 = np.asarray(inp["a_cmp_b2"])
    sw = _swap64(np.arange(64))
    w2k = w2[:, 0]
    sh["cmp_w2k"] = f(np.concatenate([w2k, w2k, w2k[:, :, sw], w2k[:, :, sw]], axis=2))
    b2k = b2[:, 0]
    b2kp = np.concatenate([b2k, b2k], 1); b2ks = np.concatenate([b2k[:, sw], b2k[:, sw]], 1)
    sh["cmp_b2k"] = f(np.stack([b2kp, b2ks], -1).transpose(1, 0, 2))
    w2v = w2[:, 1]
    sh["cmp_w2v"] = f(np.concatenate([w2v, w2v], axis=2))
    sh["cmp_b2v"] = f(np.concatenate([b2[:, 1], b2[:, 1]], 1))
    sh["b_w_in"] = f(inp["b_w_in"])
    wkv = np.asarray(inp["w_kv_shared"])
    wa = np.zeros((D, 768 + 768 + 128), np.float32)
    wa[:, :1536] = wkv[:, :1536]; wa[:, 1536:1548] = wkv[:, 1536:1548]
    sh["w_kv_aug"] = wa
    sh["b_fgate_bc"] = f(np.broadcast_to(np.asarray(inp["b_fgate"])[None, :], (128, 12)))
    return sh


_CACHE = {}

def run(inputs, cfg, n_cores=8, x_override=None):
    key = repr(sorted(cfg.items()))
    if key not in _CACHE:
        _CACHE[key] = build(cfg)
    nc, K = _CACHE[key]
    sh = prep_shared(inputs)
    x = np.asarray(inputs["x"], dtype=np.float32) if x_override is None else x_override
    mem = np.asarray(inputs["mem"], dtype=np.float32)
    in_maps = []
    for b in range(n_cores):
        m = dict(sh)
        m["xT"] = np.ascontiguousarray(x[b].T)
        m["memT"] = np.ascontiguousarray(mem[b].T)
        in_maps.append(m)
    res = run_bass_kernel_spmd(nc, in_maps, core_ids=list(range(n_cores)))
    global LAST_RES
    LAST_RES = res.results
    return np.stack([np.ascontiguousarray(r["outT"].T) for r in res.results], 0)


def kernel(**inputs):
    return run(inputs, {"layers": (0, 1, 2, 3), "final": True}).astype(np.float32)
```

```python
import numpy as np
from contextlib import ExitStack
import concourse.bass as bass
import concourse.mybir as mybir
from concourse.bass_utils import run_bass_kernel_spmd

F32 = mybir.dt.float32
BF16 = mybir.dt.bfloat16
AF = mybir.ActivationFunctionType
ALU = mybir.AluOpType

D = 1024; S = 2048; DEPTH = 4; NA = 2
DH = 64; NH = 12; NMH = 4; NMEM = 256
DFF = 2816; NCH = 44
TQ = 512; NQB = 4
NCMP = 127
EPS = 1e-6
SCALE = 0.125

def _swap64(cols):
    cols = np.asarray(cols).reshape(-1, 64)
    return np.concatenate([cols[:, 32:], cols[:, :32]], axis=1).reshape(-1)

def nsa_cols():
    q = np.arange(768)
    kv0 = 768
    def kvc(i, g):
        return kv0 + (i * 2 + g) * 64 + np.arange(64)
    fm = []
    qs = _swap64(q)
    for i in range(6):
        fm += [q[128 * i:128 * i + 128], qs[128 * i:128 * i + 128]]
    for i in (2, 4):
        for g in range(2):
            fm += [np.concatenate([kvc(i, g), kvc(i, g)])]
            fm += [np.concatenate([_swap64(kvc(i, g)), _swap64(kvc(i, g))])]
    fm += [np.concatenate([kvc(0, 0), kvc(0, 1)])]
    fm += [np.concatenate([kvc(1, 0), kvc(1, 1)])]
    qm = 768 + 768 + 36 + np.arange(256)
    fm += [qm]
    fmc = np.concatenate(fm)
    gates = 768 + 768 + np.arange(36)
    tm = np.concatenate([kvc(3, 0), kvc(3, 0), kvc(3, 1), kvc(3, 1),
                         kvc(5, 0), kvc(5, 0), kvc(5, 1), kvc(5, 1)])
    return fmc, gates, tm

NSA_FM, NSA_GATE, NSA_TM = nsa_cols()
NSA_NCOL = len(NSA_FM) + 128 + len(NSA_TM)


class Buf:
    __slots__ = ("name", "w", "readers", "excl")
    def __init__(self, name, excl=False):
        self.name = name; self.w = None; self.readers = {}; self.excl = excl


class Ker:
    def __init__(self, nc, stack):
        self.nc = nc
        self.eng = {"pe": nc.tensor, "act": nc.scalar, "dve": nc.vector, "pool": nc.gpsimd, "sp": nc.sync}
        self.sem = {e: stack.enter_context(nc.semaphore("s_" + e)) for e in self.eng}
        self.cnt = {e: 0 for e in self.eng}
        self.seen = {e: {} for e in self.eng}
        self.nds = 8
        self.dsem = {q: [stack.enter_context(nc.semaphore(f"d_{q}{i}")) for i in range(self.nds)]
                     for q in ("sp", "pool")}
        self.dval = {q: [0] * self.nds for q in ("sp", "pool")}
        self.dnext = {"sp": 0, "pool": 0}
        self.nwait = 0
        self.nops = {e: 0 for e in self.eng}
        self.phases = []

    def _need(self, e, tok):
        kind, a, v = tok
        key = (kind, a)
        if self.seen[e].get(key, 0) >= v:
            return
        if kind == "e":
            assert v <= self.cnt[a], f"wait on pending (non-incrementing) instruction of {a}"
            self.eng[e].wait_ge(self.sem[a], v)
        else:
            self.eng[e].wait_ge(self.dsem[a[0]][a[1]], v)
        self.nwait += 1
        self.seen[e][key] = v

    def _deps(self, e, reads, writes):
        for b in reads:
            if b.w is not None:
                if not (b.w[0] == "e" and b.w[1] == e and e == "pe"):
                    self._need(e, b.w)
            if b.excl:
                for (k, a), v in b.readers.items():
                    if k == "e" and a == e:
                        continue
                    self._need(e, (k, a, v))
        for b in writes:
            if b.w is not None and not (b.w[0] == "e" and b.w[1] == e):
                self._need(e, b.w)
            for (k, a), v in b.readers.items():
                if k == "e" and a == e:
                    continue
                self._need(e, (k, a, v))

    def op(self, e, fn, reads=(), writes=(), inc=True):
        self._deps(e, reads, writes)
        ins = fn(self.eng[e])
        self.nops[e] += 1
        if inc:
            ins.then_inc(self.sem[e], 1)
            self.cnt[e] += 1
            c = self.cnt[e]
        else:
            c = self.cnt[e] + 1
        for b in reads:
            k = ("e", e)
            if b.readers.get(k, 0) < c:
                b.readers[k] = c
        for b in writes:
            b.w = ("e", e, c); b.readers = {}
        return ins

    def dma(self, q, out_ap, in_ap, reads=(), writes=()):
        i = self.dnext[q]; self.dnext[q] = (i + 1) % self.nds
        if self.dval[q][i] > 0:
            self._need(q, ("d", (q, i), self.dval[q][i]))
        self._deps(q, reads, writes)
        self.dval[q][i] += 16
        v = self.dval[q][i]
        self.eng[q].dma_start(out=out_ap, in_=in_ap).then_inc(self.dsem[q][i], 16)
        for b in reads:
            b.readers[("d", (q, i))] = v
        for b in writes:
            b.w = ("d", (q, i), v); b.readers = {}

    def barrier(self):
        for e in ("pe", "act", "dve", "pool", "sp"):
            for f in ("pe", "act", "dve", "pool"):
                if f != e and self.cnt[f] > 0:
                    self._need(e, ("e", f, self.cnt[f]))
            for q in ("sp", "pool"):
                for i in range(self.nds):
                    if self.dval[q][i] > 0:
                        self._need(e, ("d", (q, i), self.dval[q][i]))

    def finish(self):
        for q in ("sp", "pool"):
            for i in range(self.nds):
                if self.dval[q][i] > 0:
                    self._need("sp", ("d", (q, i), self.dval[q][i]))


class Rot:
    def __init__(self, items):
        self.items = items; self.i = 0
    def get(self):
        it = self.items[self.i]; self.i = (self.i + 1) % len(self.items)
        return it


def build(cfg):
    layers = cfg.get("layers", [0, 1, 2, 3])
    do_final = cfg.get("final", True)
    mode = cfg.get("mode", "full")
    nc = bass.Bass("TRN2", target_bir_lowering=False)
    st = ExitStack()

    def din(name, shape, dt=F32):
        return nc.dram_tensor(name, list(shape), dt, kind="ExternalInput").ap()

    xT_d = din("xT", [D, S])
    memT_d = din("memT", [D, NMEM])
    gains_d = din("gains", [128, 14, 8])
    wmem_d = din("w_mem_kv", [DEPTH, D, 512])
    wo_d = din("w_o", [DEPTH, D, D])
    wup_d = din("w_up", [DEPTH, D, 2 * DFF])
    wdn_d = din("w_down", [DEPTH, DFF, D])
    cw_d = din("convw", [128, DEPTH, 3, NCH])
    cb_d = din("convb", [128, DEPTH, NCH])
    awin_d = din("a_w_aug", [NA, D, NSA_NCOL])
    agb_d = din("a_gate_b", [36, NA])
    cw1_d = din("cmp_w1", [NA, 2, 128, 32, 256])
    cpos_d = din("cmp_pos", [128, NA, 2, 32])
    cb1_d = din("cmp_b1", [128, NA, 2, 2])
    cw2k_d = din("cmp_w2k", [NA, 256, 256])
    cb2k_d = din("cmp_b2k", [128, NA, 2])
    cw2v_d = din("cmp_w2v", [NA, 256, 128])
    cb2v_d = din("cmp_b2v", [NA, 128])
    bwin_d = din("b_w_in", [NA, D, D])
    wkv_d = din("w_kv_aug", [D, 768 + 768 + 128])
    bfg_d = din("b_fgate_bc", [128, 12])
    cos_d = din("ropecos", [128, S]); sin_d = din("ropesin", [128, S])
    cosc_d = din("ropecosc", [128, 128]); sinc_d = din("ropesinc", [128, 128])
    ident_d = din("ident", [128, 128])
    tri_d = din("tri", [128, 128]); tric_d = din("tric", [128, 128])
    cmpmask_d = din("cmpmask", [128, S])
    overlap_d = din("overlap", [128, 32])
    expand_d = din("expand", [32, 16, 128])
    bonus_d = din("bonus", [128, 16, 32])
    gsel_d = din("rowidx", [36, 128])
    sel127_d = din("sel127", [128, 128])
    out_d = nc.dram_tensor("outT", [D, S], F32, kind="ExternalOutput").ap()
    dumps = {}

    K = Ker(nc, st)

    def sb(name, shape, dt):
        return st.enter_context(nc.sbuf_tensor("sb_" + name, list(shape), dt))

    def ps(name, shape=(128, 512), dt=F32):
        return st.enter_context(nc.psum_tensor("ps_" + name, list(shape), dt))

    xT = sb("xT", [128, 8, S], F32)
    xB = [[Buf(f"x{k}_{c}") for c in range(NQB)] for k in range(8)]
    hT = sb("hT", [128, 8, TQ], BF16); hB = [Buf(f"hT{k}") for k in range(8)]
    rstd = sb("rstd", [128, TQ], F32); rstdB = Buf("rstd")
    gains = sb("gains", [128, 14, 8], F32); gB = Buf("gains")
    NW = 4
    wt = [sb(f"wt{i}", [128, 2048], BF16) for i in range(NW)]
    wrot = Rot([(wt[i], Buf(f"wt{i}")) for i in range(NW)])
    c_ones = sb("c_ones", [128, 128], BF16)
    c_onesm = sb("c_onesm", [128, 128], BF16)
    c_ones32 = sb("c_ones32", [128, 128], F32)
    c_eps = sb("c_eps", [128, 1], F32)
    c_tri = sb("c_tri", [128, 128], BF16); c_tric = sb("c_tric", [128, 128], BF16)
    c_tri32 = sb("c_tri32", [128, 128], F32)
    c_ident = sb("c_ident", [128, 128], F32)
    c_sel127 = sb("c_sel127", [128, 128], F32)
    cB = Buf("consts")
    convw = sb("convw", [128, DEPTH, 3, NCH], F32); convb = sb("convb", [128, DEPTH, NCH], F32)
    qT = sb("qT", [128, 6, TQ], BF16); qB = Buf("qT")
    qmT = sb("qmT", [128, 2, TQ], BF16); qmB = Buf("qmT")
    oT = sb("oT", [128, 8, TQ], BF16); oB = Buf("oT")
    gated = sb("gated", [128, 11, TQ], BF16); gatedB = Buf("gated")
    sq = gated; sqB = gatedB
    ubuf = [sb(f"ubuf{i}", [128, TQ + 2], F32) for i in range(4)]
    ubB = [Buf(f"ubuf{i}") for i in range(4)]
    halo = sb("halo", [128, NCH, 2], F32); haloB = [Buf(f"halo{i}") for i in range(NCH)]
    sil = rstd; silB = rstdB
    E_t = [sb(f"E{i}", [128, TQ], BF16) for i in range(4)]
    Erot = Rot([(E_t[i], Buf(f"E{i}")) for i in range(4)])
    den_sb = [sb(f"den{i}", [128, TQ], F32) for i in range(2)]
    denrot = Rot([(den_sb[i], Buf(f"den{i}")) for i in range(2)])
    den_sb_items = denrot.items
    oacc1 = sb("oacc1", [128, TQ], F32); oaccB = Buf("oacc")
    tmpf = [sb(f"tmpf{i}", [128, TQ], F32) for i in range(2)]
    tmprot = Rot([(tmpf[i], Buf(f"tmpf{i}")) for i in range(2)])
    cacc = tmpf; caccB = [tmprot.items[i][1] for i in range(2)]
    mhT = oT; mhB = oB
    kmT = sb("kmT", [128, 2, NMEM], BF16); kmB = Buf("kmT")
    vm = sb("vm", [128, 2, 256], BF16); vmB = Buf("vm")
    KVBYTES = 48 * 1024
    kvraw = sb("kvraw", [128, KVBYTES // 2], BF16)
    fkT = kvraw[:, 0:6 * S].rearrange("p (k t) -> p k t", k=6); fkB = [Buf(f"fk{c}") for c in range(NQB)]
    fV = kvraw[:, 6 * S:12 * S].rearrange("p (j n) -> p j n", j=16); fVB = [Buf(f"fv{c}") for c in range(NQB)]
    dcum = sb("dcum", [128, 16, 12], F32); dcumB = [Buf(f"dcum{j}") for j in range(16)]
    logf = sb("logf", [128, 16, 12], F32); logfB = [Buf(f"logf{j}") for j in range(16)]
    fbias = logf
    dref = sb("dref", [128, 12], F32); drefB = Buf("dref")
    bfg = sb("bfg", [128, 12], F32)
    o = 0
    def carve(n):
        nonlocal o
        v = kvraw[:, o:o + n]; o += n
        return v
    kslcT = carve(2 * S).rearrange("p (g t) -> p g t", g=2); kwinT = carve(2 * S).rearrange("p (g t) -> p g t", g=2)
    vslc = carve(16 * 256).rearrange("p (j n) -> p j n", j=16); vwin = carve(16 * 256).rearrange("p (j n) -> p j n", j=16)
    ucmp = carve(2 * S).rearrange("p (k t) -> p k t", k=2)
    cmpmask = carve(TQ)
    kcT = carve(2 * 128).rearrange("p (g n) -> p g n", g=2)
    vc = carve(2 * 128).rearrange("p (g n) -> p g n", g=2)
    selT = carve(2 * TQ).rearrange("p (g t) -> p g t", g=2)
    expand = carve(16 * 128).rearrange("p (j s) -> p j s", j=16)
    assert o * 2 <= KVBYTES
    nsaKB = [Buf(f"nsak{c}") for c in range(NQB)]
    cmpB = Buf("cmp"); selB = Buf("selT"); hidB = Buf("hid"); cmB = Buf("cmpmask")
    overlap = sb("overlap", [128, 32], F32)
    bonus = sb("bonus", [128, 4, 32], F32); bonusB = Buf("bonus")
    e36 = sb("e36", [36, TQ], F32); e36B = Buf("e36")
    agb = sb("agb", [36, NA], F32)
    rowidx = sb("rowidx", [36, 128], F32)
    selh = [sb(f"selh{i}", [36, 128], F32) for i in range(2)]
    selhrot = Rot([(selh[i], Buf(f"selh{i}")) for i in range(2)])
    ropeB = Buf("rope")
    cosc = sb("cosc", [128, 128], F32); sinc = sb("sinc", [128, 128], F32)
    cpos = sb("cpos", [128, NA, 2, 32], BF16); cb1 = sb("cb1", [128, NA, 2, 2], F32)
    cb2k = sb("cb2k", [128, NA, 2], F32); cb2v = sb("cb2v", [1, NA, 128], BF16)
    hb = sb("hb", [128, 8], F32); hbB = Buf("hb")
    g_x = [sb(f"g_x{i}", [128, 128], F32) for i in range(4)]; g_xB = [Buf(f"g_x{i}") for i in range(4)]
    imp_s4 = sb("imp_s4", [128, 4, 32], F32); impB = Buf("imp_s")
    sc2 = sb("sc2", [128, 32], F32); sc2B = Buf("sc2")
    mx8 = sb("mx8", [128, 16], F32); mx8B = Buf("mx8")
    sel_s = sb("sel_s", [128, 32], F32); selsB = Buf("sel_s")

    pA = Rot([(ps(f"pA{i}"), Buf(f"pA{i}", True)) for i in range(2)])
    pA2 = Rot([pA.items[0]])
    pS = Rot([(ps(f"pS{i}"), Buf(f"pS{i}", True)) for i in range(2)] + [pA.items[1]])
    pN = Rot([(ps(f"pN{i}"), Buf(f"pN{i}", True)) for i in range(2)])
    pD = Rot([(ps(f"pD{i}"), Buf(f"pD{i}", True)) for i in range(2)])

    def mm(out, lhsT, rhs, start, stop, reads, writes, inc=True):
        return K.op("pe", lambda e: e.matmul(out, lhsT, rhs, start=start, stop=stop), reads, writes, inc)

    def act(out, in_, func, reads, writes, bias=0.0, scale=1.0):
        return K.op("act", lambda e: e.activation(out, in_, func, bias=bias, scale=scale), reads, writes)

    def tt(eng, out, in0, in1, op, reads, writes):
        return K.op(eng, lambda e: e.tensor_tensor(out, in0, in1, op), reads, writes)

    def ts(eng, out, in0, s1, s2, op0, op1, reads, writes):
        if op1 is None:
            return K.op(eng, lambda e: e.tensor_scalar(out, in0, s1, None, op0), reads, writes)
        return K.op(eng, lambda e: e.tensor_scalar(out, in0, s1, s2, op0, op1), reads, writes)

    def stt(eng, out, in0, scalar, in1, op0, op1, reads, writes):
        return K.op(eng, lambda e: e.scalar_tensor_tensor(out, in0, scalar, in1, op0, op1), reads, writes)

    def cp(eng, out, in_, reads, writes):
        return K.op(eng, lambda e: e.tensor_copy(out, in_), reads, writes)

    def _issue256(w_ap, col0, ncols):
        t, tb = wrot.get()
        wv = t[:, 0:8 * 256].rearrange("p (k n) -> p k n", k=8)
        K.dma("pool", wv[:, :, :ncols], w_ap.rearrange("(k p) n -> p k n", p=128)[:, :, col0:col0 + ncols],
              writes=(tb,))
        return wv, tb

    plan_q = []; issued_q = []
    AHEAD = 3

    def plan(specs):
        assert not plan_q and not issued_q
        plan_q.extend(specs)
        while plan_q and len(issued_q) < AHEAD:
            sp = plan_q.pop(0); issued_q.append((sp, _issue256(*sp)))

    def load256(w_ap, col0, ncols=256):
        if not issued_q and not plan_q:
            return _issue256(w_ap, col0, ncols)
        while plan_q and len(issued_q) < 1 + AHEAD:
            sp = plan_q.pop(0); issued_q.append((sp, _issue256(*sp)))
        sp, tile = issued_q.pop(0)
        assert sp[1] == col0 and sp[2] == ncols, (sp[1:], col0, ncols)
        return tile

    def wstream(loads, ahead):
        issued = []
        def get(i):
            while len(issued) < min(len(loads), i + 1 + ahead):
                issued.append(loads[len(issued)]())
            return issued[i]
        return get

    def load_w(dram_ap, view):
        t, b = wrot.get()
        K.dma("pool", view(t), dram_ap, reads=(), writes=(b,))
        return t, b

    K.dma("sp", gains[:], gains_d, writes=(gB,))
    K.dma("sp", convw[:], cw_d, writes=(cB,)); K.dma("sp", convb[:], cb_d, writes=(cB,))
    K.dma("sp", c_ident[:], ident_d, writes=(cB,))
    K.dma("sp", c_tri32[:], tri_d, writes=(cB,))
    K.dma("sp", c_sel127[:], sel127_d, writes=(cB,))
    K.dma("pool", c_tri[:], tri_d, writes=(cB,)); K.dma("pool", c_tric[:], tric_d, writes=(cB,))
    K.dma("sp", bfg[:], bfg_d, writes=(cB,))
    K.dma("sp", overlap[:], overlap_d, writes=(cB,)); pass
    K.dma("sp", rowidx[:], gsel_d, writes=(cB,)); K.dma("sp", agb[:], agb_d, writes=(cB,))
    K.op("dve", lambda e: e.tensor_scalar(agb[:], agb[:], -1.0, None, ALU.mult), reads=(cB,), writes=(cB,))
    K.dma("sp", cosc[:], cosc_d, writes=(cB,)); K.dma("sp", sinc[:], sinc_d, writes=(cB,))
    K.dma("pool", cpos[:], cpos_d, writes=(cB,)); K.dma("sp", cb1[:], cb1_d, writes=(cB,))
    K.dma("sp", cb2k[:], cb2k_d, writes=(cB,))
    K.dma("pool", cb2v[:], cb2v_d.rearrange("(o l) n -> o l n", o=1), writes=(cB,))
    K.op("dve", lambda e: e.memset(c_ones[:], 1.0), writes=(cB,))
    K.op("dve", lambda e: e.memset(c_onesm[:], 1.0 / 1024.0), writes=(cB,))
    K.op("dve", lambda e: e.memset(c_ones32[:], 1.0), writes=(cB,))
    K.op("dve", lambda e: e.memset(c_eps[:], EPS), writes=(cB,))
    K.op("dve", lambda e: e.memset(halo[:], 0.0), writes=tuple(haloB))
    for k in range(8):
        K.dma("sp", xT[:, k, :], xT_d[k * 128:(k + 1) * 128, :], writes=tuple(xB[k]))
    K.barrier()

    def rmsnorm_block(src, srcB, gidx, ncols, dst, dstB, col0=0, src_list=None):
        sl = (lambda k: src_list[k]) if src_list is not None else (lambda k: src[:, k, col0:col0 + ncols])
        for k in range(8):
            K.op("act", lambda e, k=k: e.activation(sq[:, k, :ncols], sl(k), AF.Square),
                 reads=(srcB[k],), writes=(sqB,))
        pt, pb = pA.get()
        for k in range(8):
            mm(pt[:, :ncols], c_onesm[:], sq[:, k, :ncols], k == 0, k == 7, (sqB, cB), (pb,), inc=(k == 7))
        act(rstd[:, :ncols], pt[:, :ncols], AF.Ln, (pb, cB), (rstdB,), bias=c_eps[:, 0:1])
        act(rstd[:, :ncols], rstd[:, :ncols], AF.Exp, (rstdB,), (rstdB,), scale=-0.5)
        for k in range(8):
            stt("dve", dst[:, k, :ncols], sl(k),
                gains[:, gidx, k:k + 1], rstd[:, :ncols], ALU.mult, ALU.mult, (srcB[k], rstdB, gB),
                (dstB[k] if isinstance(dstB, list) else dstB,))

    def proj_fm(wtile, wb, wcol0, ncol, rhsT, rhsB, ncols_tok):
        pt, pb = pA.get()
        for k in range(8):
            mm(pt[:ncol, :ncols_tok], wtile[:, k, wcol0:wcol0 + ncol], rhsT[:, k, :ncols_tok],
               k == 0, k == 7, (wb, rhsB[k] if isinstance(rhsB, list) else rhsB), (pb,), inc=(k == 7))
        return pt, pb

    def ffn_block(l, c):
        xcB = [xB[k][c] for k in range(8)]
        rmsnorm_block(xT, xcB, 4 + l, TQ, hT, hB, col0=c * TQ)
        srcu = wup_d[l].rearrange("(k p) f -> p k f", p=128)
        srcd = wdn_d[l].rearrange("(i p) n -> p i n", p=128)
        loads = []
        def mk_up(c0, npair):
            def f():
                t, tb = wrot.get()
                wv = t[:, 0:8 * 256].rearrange("p (k n) -> p k n", k=8)
                K.dma("pool", wv[:, :, 0:128 * npair], srcu[:, :, c0:c0 + 128 * npair], writes=(tb,))
                return wv, tb
            return f
        def mk_dn(half, f0, nf, nn):
            def f():
                t, tb = wrot.get()
                wv = t[:, 0:nf * 256].rearrange("p (i n) -> p i n", i=nf)
                K.dma("pool", wv, srcd[:, half * 11 + f0:half * 11 + f0 + nf, nn * 256:(nn + 1) * 256], writes=(tb,))
                return wv, tb
            return f
        for half in range(2):
            for pi in range(0, 11, 2):
                npair = min(2, 11 - pi)
                for ab in range(2):
                    loads.append(mk_up(ab * DFF + (half * 11 + pi) * 128, npair))
            for nn in range(4):
                for (f0, nf) in ((0, 6), (6, 5)):
                    loads.append(mk_dn(half, f0, nf, nn))
        wget = wstream(loads, 2)
        li = 0
        for half in range(2):
            for pi in range(0, 11, 2):
                npair = min(2, 11 - pi)
                i0 = half * 11 + pi
                tiles = [wget(li), wget(li + 1)]; li += 2
                for j in range(npair):
                    i = i0 + j
                    accs = []
                    par = (pi + j) % 2
                    for ab in range(2):
                        ch = i + 22 * ab
                        wv, tb = tiles[ab]
                        pt, pb = proj_fm(wv, tb, j * 128, 128, hT, hB, TQ)
                        ui = ab + 2 * par
                        ub, ubb = ubuf[ui], ubB[ui]
                        if c > 0:
                            cp("pool", ub[:, 0:2], halo[:, ch, :], (haloB[ch],), (ubb,))
                        else:
                            K.op("pool", lambda e, ub=ub: e.memset(ub[:, 0:2], 0.0), writes=(ubb,))
                        act(ub[:, 2:TQ + 2], pt[:, :], AF.Copy, (pb,), (ubb,))
                        cp("pool", halo[:, ch, :], ub[:, TQ:TQ + 2], (ubb,), (haloB[ch],))
                        ca, cab = (cacc[ab], caccB[ab]) if par == 0 else den_sb_items[ab]
                        eng = "dve"
                        K.op("act", lambda e, ca=ca, pt=pt, ch=ch: e.activation(
                            ca[:], pt[:, :], AF.Identity, bias=convb[:, l, ch:ch + 1], scale=convw[:, l, 2, ch:ch + 1]),
                            reads=(pb, cB), writes=(cab,))
                        stt(eng, ca[:], ub[:, 1:TQ + 1], convw[:, l, 1, ch:ch + 1], ca[:], ALU.mult, ALU.add,
                            (ubb, cB, cab), (cab,))
                        stt(eng, ca[:], ub[:, 0:TQ], convw[:, l, 0, ch:ch + 1], ca[:], ALU.mult, ALU.add,
                            (ubb, cB, cab), (cab,))
                        accs.append((ca, cab))
                    sl_t, sl_b = (sil, silB) if par == 0 else (oacc1, oaccB)
                    act(sl_t[:], accs[0][0][:], AF.Silu, (accs[0][1],), (sl_b,))
                    tt("dve", gated[:, pi + j, :], sl_t[:], accs[1][0][:], ALU.mult, (sl_b, accs[1][1]), (gatedB,))
            for nn in range(4):
                tiles = [wget(li), wget(li + 1)]; li += 2
                for n2 in range(2):
                    n = nn * 2 + n2
                    pt, pb = pA.get()
                    for i in range(11):
                        wv, tb = tiles[0] if i < 6 else tiles[1]
                        ii = i if i < 6 else i - 6
                        mm(pt[:, :], wv[:, ii, n2 * 128:(n2 + 1) * 128], gated[:, i, :], i == 0, i == 10,
                           (tb, gatedB), (pb,), inc=(i == 10))
                    tt("dve", xT[:, n, c * TQ:(c + 1) * TQ], xT[:, n, c * TQ:(c + 1) * TQ], pt[:, :], ALU.add,
                       (pb, xB[n][c]), (xB[n][c],))

    def mem_kv(l):
        hold = [(tmpf[0], tmprot.items[0][1]), (tmpf[1], tmprot.items[1][1]), den_sb_items[0], den_sb_items[1]]
        srcs = []; srcBs = []
        for k in range(8):
            t_, b_ = hold[k // 2]
            ap_ = t_[:, (k % 2) * 256:(k % 2) * 256 + 256]
            K.dma("sp", ap_, memT_d[k * 128:(k + 1) * 128, :], writes=(b_,))
            srcs.append(ap_); srcBs.append(b_)
        plan([(wmem_d[l], 0, 256), (wmem_d[l], 256, 256)])
        rmsnorm_block(None, srcBs, 8 + l, NMEM, mhT, mhB, src_list=srcs)
        wv, tb = load256(wmem_d[l], 0)
        for ch in range(2):
            pt, pb = proj_fm(wv, tb, ch * 128, 128, mhT, mhB, NMEM)
            act(kmT[:, ch, :], pt[:, :NMEM], AF.Copy, (pb,), (kmB,))
        wv, tb = load256(wmem_d[l], 256)
        for mt in range(2):
            pt, pb = pA.get()
            for k in range(8):
                mm(pt[:, :256], mhT[:, k, mt * 128:(mt + 1) * 128], wv[:, k, 0:256], k == 0, k == 7,
                   (tb, mhB), (pb,), inc=(k == 7))
            act(vm[:, mt, :], pt[:, :256], AF.Copy, (pb,), (vmB,))

    class Pipe:
        def __init__(self, depth=1):
            self.depth = depth; self.pending = []
        def push(self, A, B):
            r = A()
            self.pending.append((B, r))
            while len(self.pending) > self.depth:
                b, rr = self.pending.pop(0); b(rr)
        def flush(self):
            while self.pending:
                b, rr = self.pending.pop(0); b(rr)

    pipe = Pipe(2)

    def attn_tile(kT_ap, kreads, q_ap, qreads, ncol, bias, post, V_ap, vreads, pn_t, pn_b, pd_t, pd_b,
                  first, last, col0, after=None, krows=128, extra=None, preB=None, preB_late=None):
        def A():
            st_t, st_b = pS.get()
            mm(st_t[:krows, :ncol], kT_ap, q_ap, True, extra is None, kreads + qreads, (st_b,))
            if extra is not None:
                mm(st_t[:krows, :ncol], extra[0], extra[1], False, True, extra[2], (st_b,))
            e_t, e_b = Erot.get()
            K.op("act", lambda e: e.activation(e_t[:krows, :ncol], st_t[:krows, :ncol], AF.Exp, bias=bias[0],
                                               scale=SCALE),
                 reads=(st_b,) + bias[1], writes=(e_b,))
            if post is not None:
                post(e_t, e_b)
            if preB is not None:
                preB()
            return e_t, e_b
        def B(r):
            e_t, e_b = r
            if preB_late is not None:
                preB_late()
            mm(pn_t[:, col0:col0 + ncol], V_ap, e_t[:krows, :ncol], first, last, vreads + (e_b,), (pn_b,))
            mm(pd_t[:, col0:col0 + ncol], c_ones[:krows, :], e_t[:krows, :ncol], first, last, (e_b, cB), (pd_b,))
            if after is not None:
                after()
        pipe.push(A, B)

    def mask_sub(mask_ap):
        def post(e_t, e_b):
            tt("pool", e_t[:, 0:128], e_t[:, 0:128], mask_ap, ALU.mult, (e_b, cB), (e_b,))
        return post

    def causal_attention(kT_fn, V_fn, q_ap_fn, c, bias_fn, pr, njt=None, sel_fn=None, after_fn=None, preB=None):
        pn_t, pn_b = pN.get(); pd_t, pd_b = pD.get()
        tiles = list(range(4 * c + 4))
        nt = len(tiles)
        order = [4 * c] + list(range(4 * c)) + [4 * c + 1, 4 * c + 2, 4 * c + 3]
        for idx, j in enumerate(order):
            i = j - 4 * c
            if i < 0:
                col0, ncol, post = 0, TQ, None
            else:
                col0, ncol = 128 * i, TQ - 128 * i
                post = mask_sub(c_tri[:, :])
            extra = None
            if sel_fn is not None:
                post, extra = sel_fn(j, col0, ncol, post)
            kap, kr = kT_fn(j); vap, vr = V_fn(j)
            qap, qr = q_ap_fn(col0, ncol)
            aft = None
            if idx == nt - 1 and after_fn is not None:
                aft = (lambda: after_fn(pn_t, pn_b, pd_t, pd_b))
            attn_tile(kap, kr, qap, qr, ncol, bias_fn(j), post, vap, vr, pn_t, pn_b, pd_t, pd_b,
                      idx == 0, idx == nt - 1, col0, after=aft, extra=extra,
                      preB=(preB if idx == nt - 1 else None))
        return pn_t, pn_b, pd_t, pd_b

    def finish_head_plain(pn_t, pn_b, pd_t, pd_b, pr, dst_ap):
        d_t, d_b = denrot.get()
        act(d_t[pr, :], pd_t[pr, :], AF.Ln, (pd_b,), (d_b,))
        act(d_t[pr, :], d_t[pr, :], AF.Exp, (d_b,), (d_b,), scale=-1.0)
        tt("dve", dst_ap, pn_t[pr, :], d_t[pr, :], ALU.mult, (pn_b, d_b), (oB,))

    def mem_attention(c):
        for hm in range(4):
            ch, off = hm // 2, 64 * (hm % 2)
            pr = slice(off, off + 64)
            pn_t, pn_b = pN.get(); pd_t, pd_b = pD.get()
            for mt in range(2):
                aft = None
                if mt == 1:
                    aft = (lambda pn_t=pn_t, pn_b=pn_b, pd_t=pd_t, pd_b=pd_b, pr=pr, ch=ch:
                           finish_head_plain(pn_t, pn_b, pd_t, pd_b, pr, oT[pr, 6 + ch, :]))
                attn_tile(kmT[pr, ch, mt * 128:(mt + 1) * 128], (kmB,), qmT[pr, ch, :], (qmB,), TQ, (0.0, ()),
                          None, vm[:, mt, ch * 128:(ch + 1) * 128], (vmB,), pn_t, pn_b, pd_t, pd_b,
                          mt == 0, mt == 1, 0, after=aft)
        pipe.flush()

    def wo_block(l, c):
        plan([(wo_d[l], nn * 256, 256) for nn in range(4)])
        for nn in range(4):
            wv, tb = load256(wo_d[l], nn * 256)
            for n2 in range(2):
                n = nn * 2 + n2
                pt, pb = proj_fm(wv, tb, n2 * 128, 128, oT, oB, TQ)
                tt("dve", xT[:, n, c * TQ:(c + 1) * TQ], xT[:, n, c * TQ:(c + 1) * TQ], pt[:, :], ALU.add,
                   (pb, xB[n][c]), (xB[n][c],))

    def fox_shared_kv(c):
        plan([(wkv_d, cc * 256, 256) for cc in range(3)]
             + [(wkv_d, 768 + cc * 256, 256 if cc < 3 else 128) for cc in range(4)])
        xcB = [xB[k][c] for k in range(8)]
        rmsnorm_block(xT, xcB, 12, TQ, hT, hB, col0=c * TQ)
        for cc in range(3):
            wv, tb = load256(wkv_d, cc * 256)
            for j in range(2):
                ch = cc * 2 + j
                pt, pb = proj_fm(wv, tb, j * 128, 128, hT, hB, TQ)
                act(fkT[:, ch, c * TQ:(c + 1) * TQ], pt[:, :], AF.Copy, (pb,), (fkB[c],))
        for cc in range(4):
            ncols = 256 if cc < 3 else 128
            wv, tb = load256(wkv_d, 768 + cc * 256, ncols)
            for jt in range(4):
                j = 4 * c + jt
                pt, pb = pA.get()
                for k in range(8):
                    mm(pt[:, :ncols], hT[:, k, jt * 128:(jt + 1) * 128], wv[:, k, :ncols], k == 0, k == 7,
                       (tb, hB[k]), (pb,), inc=(k == 7))
                if cc < 3:
                    act(fV[:, j, cc * 256:(cc + 1) * 256], pt[:, :256], AF.Copy, (pb,), (fVB[c],))
                else:
                    tt("dve", logf[:, j, :], pt[:, 0:12], bfg[:], ALU.add, (pb, cB), (logfB[j],))
                    if cfg.get("dbg", 0) == 3:
                        continue
                    act(logf[:, j, :], logf[:, j, :], AF.Exp, (logfB[j],), (logfB[j],), scale=-1.0)
                    if cfg.get("dbg", 0) == 4:
                        continue
                    act(logf[:, j, :], logf[:, j, :], AF.Ln, (logfB[j], cB), (logfB[j],), bias=c_ones32[:, 0:1])
                    if cfg.get("dbg", 0) == 5:
                        continue
                    ts("dve", logf[:, j, :], logf[:, j, :], -1.0, None, ALU.mult, None, (logfB[j],), (logfB[j],))
        for jt in range(4):
            if cfg.get("dbg", 0) == 1:
                break
            j = 4 * c + jt
            pt, pb = pA.get()
            for jj in range(j + 1):
                lhs = c_tri32[:] if jj == j else c_ones32[:]
                mm(pt[:, :12], lhs, logf[:, jj, :], jj == 0, jj == j, (cB, logfB[jj]), (pb,), inc=(jj == j))
            act(dcum[:, j, :], pt[:, :12], AF.Copy, (pb,), (dcumB[j],))

    def fox_attention(l, c):
        pt, pb = pA2.get()
        mm(pt[:, :12], c_sel127[:], dcum[:, 4 * c + 1, :], True, True, (cB, dcumB[4 * c + 1]), (pb,))
        act(dref[:], pt[:, :12], AF.Copy, (pb,), (drefB,))
        for j in range(4 * c + 4):
            tt("pool", fbias[:, j, :], dref[:], dcum[:, j, :], ALU.subtract, (drefB, dcumB[j]), (logfB[j],))
        for h in range(NH):
            ch, off = h // 2, 64 * (h % 2)
            pr = slice(off, off + 64)
            causal_attention(
                lambda j, pr=pr, ch=ch: (fkT[pr, ch, j * 128:(j + 1) * 128], (fkB[j // 4],)),
                lambda j, ch=ch: (fV[:, j, ch * 128:(ch + 1) * 128], (fVB[j // 4],)),
                lambda col0, ncol, pr=pr, ch=ch: (qT[pr, ch, col0:col0 + ncol], (qB,)),
                c, lambda j, h=h: (fbias[:, j, h:h + 1], (logfB[j],)), pr,
                after_fn=(lambda a, b, c_, d, pr=pr, ch=ch: finish_head_plain(a, b, c_, d, pr, oT[pr, ch, :])))
        pipe.flush()

    def fox_q_proj(l, c):
        plan([(bwin_d[l - NA], cc * 256, 256) for cc in range(4)])
        xcB = [xB[k][c] for k in range(8)]
        rmsnorm_block(xT, xcB, l, TQ, hT, hB, col0=c * TQ)
        for cc in range(4):
            wv, tb = load256(bwin_d[l - NA], cc * 256)
            for j in range(2):
                ch = cc * 2 + j
                pt, pb = proj_fm(wv, tb, j * 128, 128, hT, hB, TQ)
                if ch < 6:
                    act(qT[:, ch, :], pt[:, :], AF.Copy, (pb,), (qB,))
                else:
                    act(qmT[:, ch - 6, :], pt[:, :], AF.Copy, (pb,), (qmB,))

    def rope_evac(pt, pb, pts, pbs, dst_ap, dstB, cs, sn, rB, ncols):
        t1, t1b = tmprot.get()
        tt("dve", t1[:, :ncols], pt[:, :ncols], cs, ALU.mult, (pb,) + rB, (t1b,))
        t2, t2b = tmprot.get()
        tt("dve", t2[:, :ncols], pts[:, :ncols], sn, ALU.mult, (pbs,) + rB, (t2b,))
        tt("pool", dst_ap, t1[:, :ncols], t2[:, :ncols], ALU.add, (t1b, t2b), (dstB,))

    def nsa_load_w(l, col0, ncols):
        return load256(awin_d[l], col0, ncols)

    def nsa_rope_tables(c):
        rc, rcb = denrot.items[0]; rs, rsb = denrot.items[1]
        K.dma("sp", rc[:], cos_d[:, c * TQ:(c + 1) * TQ], writes=(rcb,))
        K.dma("sp", rs[:], sin_d[:, c * TQ:(c + 1) * TQ], writes=(rsb,))
        return rc, rs, (rcb, rsb)

    def nsa_kv_proj(l, c):
        plan([(awin_d[l], (6 + 2 * bi + g) * 256, 256) for bi in range(2) for g in range(2)]
             + [(awin_d[l], 20 * 128, 256)] + [(awin_d[l], 3072 + 128 + vi * 256, 256) for vi in range(2)])
        xcB = [xB[k][c] for k in range(8)]
        rmsnorm_block(xT, xcB, l, TQ, hT, hB, col0=c * TQ)
        rc, rs, rB = nsa_rope_tables(c)
        tsl = slice(c * TQ, (c + 1) * TQ)
        for bi, dstT in ((0, kslcT), (1, kwinT)):
            for g in range(2):
                wv, tb = nsa_load_w(l, (6 + 2 * bi + g) * 256, 256)
                pt, pb = proj_fm(wv, tb, 0, 128, hT, hB, TQ)
                pts, pbs = proj_fm(wv, tb, 128, 128, hT, hB, TQ)
                rope_evac(pt, pb, pts, pbs, dstT[:, g, tsl], nsaKB[c], rc[:], rs[:], rB, TQ)
        wv, tb = nsa_load_w(l, 20 * 128, 256)
        for kv in range(2):
            pt, pb = proj_fm(wv, tb, kv * 128, 128, hT, hB, TQ)
            act(ucmp[:, kv, tsl], pt[:, :], AF.Copy, (pb,), (nsaKB[c],))
        for vi, vdst in ((0, vslc), (1, vwin)):
            wv, tb = nsa_load_w(l, 3072 + 128 + vi * 256, 256)
            for jt in range(4):
                j = 4 * c + jt
                pt, pb = pA.get()
                for k in range(8):
                    mm(pt[:, :256], hT[:, k, jt * 128:(jt + 1) * 128], wv[:, k, :], k == 0, k == 7, (tb, hB[k]), (pb,),
                       inc=(k == 7))
                act(vdst[:, j, :], pt[:, 0:256], AF.Copy, (pb,), (nsaKB[c],))

    def gelu_tanh(dst_ap, dstB, pt, pb, bias_ap, npart, ncols):
        x, xb = g_x[0], g_xB[0]; x2, x2b = g_x[1], g_xB[1]; th, thb = g_x[2], g_xB[2]
        act(x[:npart, :ncols], pt[:npart, :ncols], AF.Identity, (pb, cB), (xb,), bias=bias_ap)
        tt("dve", x2[:npart, :ncols], x[:npart, :ncols], x[:npart, :ncols], ALU.mult, (xb,), (x2b,))
        ts("dve", x2[:npart, :ncols], x2[:npart, :ncols], 0.044715, 1.0, ALU.mult, ALU.add, (x2b,), (x2b,))
        tt("dve", x2[:npart, :ncols], x2[:npart, :ncols], x[:npart, :ncols], ALU.mult, (x2b, xb), (x2b,))
        act(th[:npart, :ncols], x2[:npart, :ncols], AF.Tanh, (x2b,), (thb,), scale=0.7978845608028654)
        ts("dve", th[:npart, :ncols], th[:npart, :ncols], 1.0, 0.5, ALU.add, ALU.mult, (thb,), (thb,))
        tt("dve", dst_ap, th[:npart, :ncols], x[:npart, :ncols], ALU.mult, (thb, xb), (dstB,))

    def nsa_compress(l):
        allk = tuple(nsaKB)
        K.op("dve", lambda e: e.memset(expand, 0.0), writes=(cmpB,))
        K.op("dve", lambda e: e.memset(selT, 0.0), writes=(selB,))
        K.dma("pool", expand[0:32], expand_d, writes=(cmpB,))
        for kv in range(2):
            halves = []
            for hf in range(4):
                t, tb = wrot.get()
                wv = t[:, 0:8 * 256].rearrange("p (l n) -> p l n", l=8)
                K.dma("pool", wv, cw1_d[l, kv][:, hf * 8:(hf + 1) * 8, :], writes=(tb,))
                halves.append((wv, tb))
            for hc in range(2):
                pt, pb = pA.get()
                for li in range(32):
                    wv, tb = halves[li // 8]
                    mm(pt[:, 0:1], wv[0:64, li % 8, hc * 128:(hc + 1) * 128], cpos[0:64, l, kv, li:li + 1],
                       li == 0, li == 31, (tb, cB), (pb,), inc=(li == 31))
                tt("dve", hb[:, hc:hc + 1], pt[:, 0:1], cb1[:, l, kv, hc:hc + 1], ALU.add, (pb, cB), (hbB,))
                for g in range(2):
                    pr = slice(64 * g, 64 * g + 64)
                    pt2, pb2 = pA.get()
                    for li in range(32):
                        wv, tb = halves[li // 8]
                        rhs = ucmp[pr, kv, li:li + 16 * (NCMP - 1) + 1:16]
                        mm(pt2[:, :NCMP], wv[pr, li % 8, hc * 128:(hc + 1) * 128], rhs, li == 0, li == 31,
                           (tb,) + allk, (pb2,), inc=(li == 31))
                    gelu_tanh(hid_g[g][:, hc, :NCMP], hidB, pt2, pb2, hb[:, hc:hc + 1], 128, NCMP)
            if kv == 0:
                t, tb = wrot.get()
                wv = t[:, 0:2 * 256].rearrange("p (k n) -> p k n", k=2)
                K.dma("pool", wv, cw2k_d[l].rearrange("(k p) n -> p k n", p=128), writes=(tb,))
                for g in range(2):
                    pt, pb = pA.get(); pts, pbs = pA.get()
                    for hc in range(2):
                        mm(pt[:, :NCMP], wv[:, hc, 0:128], hid_g[g][:, hc, :NCMP], hc == 0, hc == 1, (tb, hidB),
                           (pb,))
                    for hc in range(2):
                        mm(pts[:, :NCMP], wv[:, hc, 128:256], hid_g[g][:, hc, :NCMP], hc == 0, hc == 1, (tb, hidB),
                           (pbs,))
                    a, ab_ = g_x[0], g_xB[0]; b, bb_ = g_x[1], g_xB[1]
                    act(a[:, :NCMP], pt[:, :NCMP], AF.Identity, (pb, cB), (ab_,), bias=cb2k[:, l, 0:1])
                    act(b[:, :NCMP], pts[:, :NCMP], AF.Identity, (pbs, cB), (bb_,), bias=cb2k[:, l, 1:2])
                    tt("dve", a[:, :NCMP], a[:, :NCMP], cosc[:, :NCMP], ALU.mult, (ab_, cB), (ab_,))
                    tt("dve", b[:, :NCMP], b[:, :NCMP], sinc[:, :NCMP], ALU.mult, (bb_, cB), (bb_,))
                    tt("dve", kcT[:, g, :NCMP], a[:, :NCMP], b[:, :NCMP], ALU.add, (ab_, bb_), (cmpB,))
            else:
                t, tb = wrot.get()
                wv = t[:, 0:2 * 128].rearrange("p (k n) -> p k n", k=2)
                K.dma("pool", wv, cw2v_d[l].rearrange("(k p) n -> p k n", p=128), writes=(tb,))
                for g in range(2):
                    pt, pb = pA.get()
                    for hc in range(2):
                        mm(pt[:NCMP, :128], hid_g[g][:, hc, :NCMP], wv[:, hc, :], hc == 0, False, (tb, hidB), (pb,),
                           inc=False)
                    mm(pt[:NCMP, :128], c_ones[0:1, :NCMP], cb2v[0:1, l, :], False, True, (cB,), (pb,))
                    act(vc[:NCMP, g, :], pt[:NCMP, :128], AF.Copy, (pb,), (cmpB,))

    hid_g = [sb(f"hid_g{g}", [128, 2, 128], BF16) for g in range(2)]

    def nsa_q_proj(l, c):
        plan([(awin_d[l], ch * 256, 256) for ch in range(6)] + [(awin_d[l], 22 * 128, 256), (awin_d[l], 3072, 128)])
        xcB = [xB[k][c] for k in range(8)]
        rmsnorm_block(xT, xcB, l, TQ, hT, hB, col0=c * TQ)
        rc, rs, rB = nsa_rope_tables(c)
        for ch in range(6):
            wv, tb = nsa_load_w(l, ch * 256, 256)
            pt, pb = proj_fm(wv, tb, 0, 128, hT, hB, TQ)
            pts, pbs = proj_fm(wv, tb, 128, 128, hT, hB, TQ)
            rope_evac(pt, pb, pts, pbs, qT[:, ch, :], qB, rc[:], rs[:], rB, TQ)
        wv, tb = nsa_load_w(l, 22 * 128, 256)
        for j in range(2):
            pt, pb = proj_fm(wv, tb, j * 128, 128, hT, hB, TQ)
            act(qmT[:, j, :], pt[:, :], AF.Copy, (pb,), (qmB,))
        wv, tb = nsa_load_w(l, 3072, 128)
        pt, pb = proj_fm(wv, tb, 0, 36, hT, hB, TQ)
        act(e36[:, :], pt[:36, :], AF.Exp, (pb, cB), (e36B,), bias=agb[:, l:l + 1], scale=-1.0)

    def nsa_attention(l, c):
        K.dma("pool", cmpmask, cmpmask_d[:, c * TQ:(c + 1) * TQ], writes=(cmB,))
        K.dma("sp", bonus[:], bonus_d[:, 4 * c:4 * c + 4, :], writes=(bonusB,))
        use_sel = c >= 2

        def cmp_scores(h, g, pr, ch):
            st_t, st_b = pS.get()
            mm(st_t[:NCMP, :], kcT[pr, g, :NCMP], qT[pr, ch, :], True, True, (cmpB, qB), (st_b,))
            e_t, e_b = Erot.get()
            act(e_t[:NCMP, :], st_t[:NCMP, :], AF.Exp, (st_b,), (e_b,), scale=SCALE)
            tt("dve", e_t[:NCMP, :], e_t[:NCMP, :], cmpmask[:NCMP, :], ALU.mult, (e_b, cmB), (e_b,))
            pd_t, pd_b = pD.get()
            mm(pd_t[:, :], c_ones[:NCMP, :], e_t[:NCMP, :], True, True, (cB, e_b), (pd_b,))
            d_t, d_b = denrot.get()
            ts("dve", d_t[:, :], pd_t[:, :], 1e-30, None, ALU.max, None, (pd_b,), (d_b,))
            act(d_t[:, :], d_t[:, :], AF.Ln, (d_b,), (d_b,))
            act(d_t[:, :], d_t[:, :], AF.Exp, (d_b,), (d_b,), scale=-1.0)
            return e_t, e_b, d_t, d_b

        for g in range(2):
            heads = list(range(6 * g, 6 * g + 6))
            if use_sel:
                pipe.flush()
                ip_t, ip_b = pA2.get()
                for h in heads:
                    ch, off = h // 2, 64 * (h % 2)
                    pr = slice(off, off + 64)
                    e_t, e_b, d_t, d_b = cmp_scores(h, g, pr, ch)
                    pn, pnB = tmprot.get()
                    tt("dve", pn[:NCMP, :], e_t[:NCMP, :], d_t[:NCMP, :], ALU.mult, (e_b, d_b), (pnB,))
                    for tt_i in range(4):
                        mm(ip_t[:, tt_i * 32:(tt_i + 1) * 32], pn[:NCMP, tt_i * 128:(tt_i + 1) * 128],
                           overlap[:NCMP, :], h == heads[0] and tt_i == 0, h == heads[-1] and tt_i == 3,
                           (pnB, cB), (ip_b,))
                for tt_i in range(4):
                    tt("dve", imp_s4[:, tt_i, :], ip_t[:, tt_i * 32:(tt_i + 1) * 32], bonus[:, tt_i, :], ALU.add,
                       (ip_b, bonusB), (impB,))
                for tt_i in range(4):
                    imp_s = imp_s4[:, tt_i, :]
                    K.op("dve", lambda e: e.max(out=mx8[:, 0:8], in_=imp_s), reads=(impB,), writes=(mx8B,))
                    K.op("dve", lambda e: e.match_replace(out=sc2[:, :], in_to_replace=mx8[:, 0:8],
                                                          in_values=imp_s, imm_value=-3.0e38),
                         reads=(impB, mx8B), writes=(sc2B,))
                    K.op("dve", lambda e: e.max(out=mx8[:, 8:16], in_=sc2[:, :]), reads=(sc2B,), writes=(mx8B,))
                    ts("dve", sel_s[:, :], imp_s, mx8[:, 15:16], None, ALU.is_ge, None, (impB, mx8B),
                       (selsB,))
                    ts("dve", sel_s[:, :], sel_s[:, :], -1.0, 30000.0, ALU.add, ALU.mult, (selsB,), (selsB,))
                    tp_t, tp_b = pA2.get()
                    K.op("pe", lambda e, tp_t=tp_t: e.transpose(tp_t[:32, :128], sel_s[:, :], c_ident[:, :]),
                         reads=(selsB, cB), writes=(tp_b,))
                    act(selT[:32, g, tt_i * 128:(tt_i + 1) * 128], tp_t[:32, :128], AF.Copy, (tp_b,), (selB,))
            for h in heads:
                ch, off = h // 2, 64 * (h % 2)
                pr = slice(off, off + 64)
                qfn = lambda col0, ncol, pr=pr, ch=ch: (qT[pr, ch, col0:col0 + ncol], (qB,))
                head_gates(l, h)

                def fin(br, last, h=h, pr=pr, ch=ch):
                    def f(pn_t, pn_b, pd_t, pd_b):
                        d_t, d_b = denrot.get()
                        ts("dve", d_t[pr, :], pd_t[pr, :], 1e-30, None, ALU.max, None, (pd_b,), (d_b,))
                        finish_gated(h, br, pn_t, pn_b, d_t, d_b, pr, first=(br == 0), last=last,
                                     dst=oT[pr, ch, :])
                    return f

                pn_t, pn_b = pN.get(); pd_t, pd_b = pD.get()
                f0 = fin(0, False)
                attn_tile(kcT[pr, g, :NCMP], (cmpB,), qT[pr, ch, :], (qB,), TQ, (0.0, ()),
                          (lambda e_t, e_b: tt("dve", e_t[:NCMP, :], e_t[:NCMP, :], cmpmask[:NCMP, :], ALU.mult,
                                               (e_b, cmB), (e_b,))),
                          vc[:NCMP, g, :], (cmpB,), pn_t, pn_b, pd_t, pd_b, True, True, 0,
                          after=(lambda f0=f0, a=pn_t, b=pn_b, c_=pd_t, d=pd_b: f0(a, b, c_, d)), krows=NCMP,
                          preB_late=(None if cfg.get("br_only") is not None else (lambda h=h: prep_gate(h, 0))))

                def sel_fn(j, col0, ncol, post0, g=g):
                    if not use_sel:
                        return post0, None
                    return post0, (expand[:, j, :], selT[:, g, col0:col0 + ncol], (cmpB, selB))
                causal_attention(
                    lambda j, pr=pr, g=g: (kslcT[pr, g, j * 128:(j + 1) * 128], (nsaKB[j // 4],)),
                    lambda j, g=g: (vslc[:, j, g * 128:(g + 1) * 128], (nsaKB[j // 4],)),
                    qfn, c, lambda j: (0.0, ()), pr, sel_fn=sel_fn, after_fn=fin(1, False),
                    preB=(None if cfg.get("br_only") is not None else (lambda h=h: prep_gate(h, 1))))
                pn_t, pn_b = pN.get(); pd_t, pd_b = pD.get()
                order = [4 * c] + [j for j in range(4 * c - 4, 4 * c) if j >= 0] + [4 * c + 1, 4 * c + 2, 4 * c + 3]
                f2 = fin(2, True)
                for idx, j in enumerate(order):
                    i = j - 4 * c
                    if i >= 0:
                        col0, ncol = 128 * i, TQ - 128 * i
                        post = mask_sub(c_tri[:, :])
                    else:
                        ii = i + 4
                        col0, ncol = 0, 128 * (ii + 1)
                        def post(e_t, e_b, ii=ii):
                            tt("pool", e_t[:, 128 * ii:128 * ii + 128], e_t[:, 128 * ii:128 * ii + 128],
                               c_tric[:, :], ALU.mult, (e_b, cB), (e_b,))
                    aft = None
                    if idx == len(order) - 1:
                        aft = (lambda f2=f2, a=pn_t, b=pn_b, c_=pd_t, d=pd_b: f2(a, b, c_, d))
                    attn_tile(kwinT[pr, g, j * 128:(j + 1) * 128], (nsaKB[j // 4],),
                              qT[pr, ch, col0:col0 + ncol], (qB,), ncol, (0.0, ()), post,
                              vwin[:, j, g * 128:(g + 1) * 128], (nsaKB[j // 4],), pn_t, pn_b, pd_t, pd_b,
                              idx == 0, idx == len(order) - 1, col0, after=aft,
                              preB=((lambda h=h: prep_gate(h, 2))
                                    if (idx == len(order) - 1 and cfg.get("br_only") is None) else None))
        pipe.flush()

    def head_gates(l, h):
        return

    gate_bc = {}

    def prep_gate(h, br):
        s_t, s_b = selhrot.get()
        ts("dve", s_t[:, :], rowidx[:, :], float(3 * h + br), None, ALU.is_equal, None, (cB,), (s_b,))
        gp_t, gp_b = pA2.get()
        mm(gp_t[:, :], s_t[:36, :], e36[:36, :], True, True, (s_b, e36B), (gp_b,))
        gate_bc[(h, br)] = (gp_t, gp_b)

    def finish_gated(h, br, pn_t, pn_b, d_t, d_b, pr, first, last=False, dst=None):
        if cfg.get("br_only") is not None:
            if br == cfg["br_only"]:
                ch_ = h // 2
                K.op("dve", lambda e: e.reciprocal(d_t[pr, :], d_t[pr, :]), reads=(d_b,), writes=(d_b,))
                tt("dve", oT[pr, ch_, :], pn_t[pr, :], d_t[pr, :], ALU.mult, (pn_b, d_b), (oB,))
            return
        gp_t, gp_b = gate_bc.pop((h, br))
        oacc = oacc1
        w_t, w_b = tmprot.get()
        stt("dve", w_t[pr, :], gp_t[pr, :], 1.0, d_t[pr, :], ALU.add, ALU.mult, (gp_b, d_b), (w_b,))
        act(w_t[pr, :], w_t[pr, :], AF.Ln, (w_b,), (w_b,))
        act(w_t[pr, :], w_t[pr, :], AF.Exp, (w_b,), (w_b,), scale=-1.0)
        if first:
            tt("dve", oacc[pr, :], pn_t[pr, :], w_t[pr, :], ALU.mult, (pn_b, w_b), (oaccB,))
            return
        tt("dve", w_t[pr, :], pn_t[pr, :], w_t[pr, :], ALU.mult, (pn_b, w_b), (w_b,))
        if last:
            tt("dve", dst, oacc[pr, :], w_t[pr, :], ALU.add, (oaccB, w_b), (oB,))
        else:
            tt("dve", oacc[pr, :], oacc[pr, :], w_t[pr, :], ALU.add, (oaccB, w_b), (oaccB,))

    skip = set(cfg.get("skip", ()))
    def maybe(fn):
        def w(*a):
            if fn.__name__ in skip:
                return
            K.phases.append((fn.__name__, a, dict(K.nops)))
            return fn(*a)
        return w
    mem_kv = maybe(mem_kv); nsa_kv_proj = maybe(nsa_kv_proj); nsa_compress = maybe(nsa_compress)
    nsa_q_proj = maybe(nsa_q_proj); nsa_attention = maybe(nsa_attention); mem_attention = maybe(mem_attention)
    wo_block = maybe(wo_block); ffn_block = maybe(ffn_block); fox_shared_kv = maybe(fox_shared_kv)
    fox_q_proj = maybe(fox_q_proj); fox_attention = maybe(fox_attention)
    for l in layers:
        if mode == "ffn":
            for c in range(NQB):
                ffn_block(l, c)
            continue
        mem_kv(l)
        if l < NA:
            for c in range(NQB):
                nsa_kv_proj(l, c)
            nsa_compress(l)
            for c in range(NQB):
                nsa_q_proj(l, c)
                nsa_attention(l, c)
                mem_attention(c)
                wo_block(l, c)
                if mode != "attn":
                    ffn_block(l, c)
            K.barrier()
        else:
            if l == NA:
                K.barrier()
                for c in range(NQB):
                    fox_shared_kv(c)
            for c in range(NQB):
                fox_q_proj(l, c)
                fox_attention(l, c)
                mem_attention(c)
                wo_block(l, c)
                if mode != "attn":
                    ffn_block(l, c)
    if do_final:
        for c in range(NQB):
            xcB = [xB[k][c] for k in range(8)]
            for k in range(8):
                K.op("act", lambda e, k=k: e.activation(sq[:, k, :], xT[:, k, c * TQ:(c + 1) * TQ], AF.Square),
                     reads=(xcB[k],), writes=(sqB,))
            pt, pb = pA.get()
            for k in range(8):
                mm(pt[:, :], c_onesm[:], sq[:, k, :], k == 0, k == 7, (sqB, cB), (pb,), inc=(k == 7))
            act(rstd[:, :], pt[:, :], AF.Ln, (pb, cB), (rstdB,), bias=c_eps[:, 0:1])
            act(rstd[:, :], rstd[:, :], AF.Exp, (rstdB,), (rstdB,), scale=-0.5)
            for k in range(8):
                stt("dve", xT[:, k, c * TQ:(c + 1) * TQ], xT[:, k, c * TQ:(c + 1) * TQ], gains[:, 13, k:k + 1],
                    rstd[:, :], ALU.mult, ALU.mult, (xcB[k], rstdB, gB), (xcB[k],))
    for k in range(8):
        K.dma("sp", out_d[k * 128:(k + 1) * 128, :], xT[:, k, :], reads=tuple(xB[k]))
    if cfg.get("dump"):
        K.barrier()
        dl = {"oT": (oT, [128, 8, TQ], BF16), "qT": (qT, [128, 6, TQ], BF16), "qmT": (qmT, [128, 2, TQ], BF16),
              "kslcT": (kslcT, [128, 2, S], BF16), "kwinT": (kwinT, [128, 2, S], BF16),
              "vslc": (vslc, [128, 16, 256], BF16), "vwin": (vwin, [128, 16, 256], BF16),
              "ucmp": (ucmp, [128, 2, S], BF16), "kcT": (kcT, [128, 2, 128], BF16), "vc": (vc, [128, 2, 128], BF16),
              "selT": (selT, [128, 2, TQ], BF16), "kmT": (kmT, [128, 2, NMEM], BF16), "vm": (vm, [128, 2, 256], BF16),
              "hT": (hT, [128, 8, TQ], BF16), "hid0": (hid_g[0], [128, 2, 128], BF16)}
        for nm in cfg["dump"]:
            t_, shp, dt_ = dl[nm]
            dd = nc.dram_tensor("dump_" + nm, shp, dt_, kind="ExternalOutput").ap()
            K.dma("sp", dd, t_ if not hasattr(t_, "ap") else t_[:], reads=())
    K.finish()
    st.close()
    return nc, K


def _consts():
    c = {}
    c["ident"] = np.eye(128, dtype=np.float32)
    s = np.arange(128)[:, None]; t = np.arange(128)[None, :]
    c["tri"] = (s <= t).astype(np.float32)
    c["tric"] = (t < s).astype(np.float32)
    n = np.arange(128)[:, None]; tt = np.arange(S)[None, :]
    c["cmpmask"] = ((16 * n + 31 <= tt) & (n < NCMP)).astype(np.float32)
    cs = np.arange(128) * 16
    ss = np.arange(32) * 64
    ov = ((cs[:, None] < ss[None, :] + 64) & (cs[:, None] + 32 > ss[None, :])).astype(np.float32)
    ov[NCMP:] = 0
    c["overlap"] = ov
    ex = np.zeros((32, 16, 128), np.float32)
    for jt in range(16):
        for s_ in range(128):
            ex[2 * jt + s_ // 64, jt, s_] = 1.0
    c["expand"] = ex
    bn = np.zeros((128, 16, 32), np.float32)
    for tix in range(16):
        tpos = tix * 128 + np.arange(128)
        blk = tpos // 64
        j = np.arange(32)[None, :]
        forced = (j == 0) | (j == blk[:, None]) | (j == blk[:, None] - 1)
        valid = j <= blk[:, None]
        bn[:, tix, :] = np.where(valid, 1e4 * forced, -1e30)
    c["bonus"] = bn
    c["rowidx"] = np.broadcast_to(np.arange(36, dtype=np.float32)[:, None], (36, 128)).copy()
    s127 = np.zeros((128, 128), np.float32); s127[127, :] = 1.0
    c["sel127"] = s127
    half = 32
    inv = (10000.0 ** (-np.arange(half, dtype=np.float32) / half)).astype(np.float32)
    def tables(pos):
        ang = pos.astype(np.float32)[None, :] * inv[:, None]
        co = np.cos(ang).astype(np.float32); si = np.sin(ang).astype(np.float32)
        cos64 = np.concatenate([co, co], 0); sin64 = np.concatenate([-si, si], 0)
        return np.concatenate([cos64, cos64], 0), np.concatenate([sin64, sin64], 0)
    c["ropecos"], c["ropesin"] = tables(np.arange(S))
    pc = np.arange(128) * 16 + 31
    c["ropecosc"], c["ropesinc"] = tables(pc)
    return {k: np.ascontiguousarray(v, dtype=np.float32) for k, v in c.items()}


def _fm(vec_list):
    a = np.stack(vec_list, 0).reshape(len(vec_list), 8, 128)
    return np.ascontiguousarray(a.transpose(2, 0, 1))


def prep_shared(inp):
    f = lambda a: np.ascontiguousarray(np.asarray(a, dtype=np.float32))
    sh = dict(_consts())
    gl = [inp["attn_norm"][i] for i in range(4)] + [inp["ffn_norm"][i] for i in range(4)] + \
         [inp["mem_norm"][i] for i in range(4)] + [inp["kv_norm"], inp["final_norm"]]
    sh["gains"] = _fm([np.asarray(g) for g in gl])
    sh["w_mem_kv"] = f(inp["w_mem_kv"]); sh["w_o"] = f(inp["w_o"]); sh["w_up"] = f(inp["w_up"])
    sh["w_down"] = f(inp["w_down"])
    cw = np.asarray(inp["conv_w"]).reshape(DEPTH, 3, NCH, 128)
    sh["convw"] = f(cw.transpose(3, 0, 1, 2))
    sh["convb"] = f(np.asarray(inp["conv_b"]).reshape(DEPTH, NCH, 128).transpose(2, 0, 1))
    aw = np.asarray(inp["a_w_in"])
    aug = np.zeros((NA, D, NSA_NCOL), np.float32)
    aug[:, :, :3072] = aw[:, :, NSA_FM]
    aug[:, :, 3072:3072 + 36] = aw[:, :, NSA_GATE]
    aug[:, :, 3200:] = aw[:, :, NSA_TM]
    sh["a_w_aug"] = aug
    sh["a_gate_b"] = f(np.asarray(inp["a_gate_b"]).T)
    w1 = np.asarray(inp["a_cmp_w1"]).reshape(NA, 2, 32, 64, 256).transpose(0, 1, 3, 2, 4)
    sh["cmp_w1"] = f(np.concatenate([w1, w1], axis=2))
    pos = np.asarray(inp["a_cmp_pos"]).transpose(3, 0, 1, 2)
    sh["cmp_pos"] = f(np.concatenate([pos, pos], 0))
    sh["cmp_b1"] = f(np.asarray(inp["a_cmp_b1"]).reshape(NA, 2, 2, 128).transpose(3, 0, 1, 2))
    w2 = np.asarray(inp["a_cmp_w2"]); b2 = np.asarray(inp["a_cmp_b2"])
    sw = _swap64(np.arange(64))
    w2k = w2[:, 0]
    sh["cmp_w2k"] = f(np.concatenate([w2k, w2k, w2k[:, :, sw], w2k[:, :, sw]], axis=2))
    b2k = b2[:, 0]
    b2kp = np.concatenate([b2k, b2k], 1); b2ks = np.concatenate([b2k[:, sw], b2k[:, sw]], 1)
    sh["cmp_b2k"] = f(np.stack([b2kp, b2ks], -1).transpose(1, 0, 2))
    w2v = w2[:, 1]
    sh["cmp_w2v"] = f(np.concatenate([w2v, w2v], axis=2))
    sh["cmp_b2v"] = f(np.concatenate([b2[:, 1], b2[:, 1]], 1))
    sh["b_w_in"] = f(inp["b_w_in"])
    wkv = np.asarray(inp["w_kv_shared"])
    wa = np.zeros((D, 768 + 768 + 128), np.float32)
    wa[:, :1536] = wkv[:, :1536]; wa[:, 1536:1548] = wkv[:, 1536:1548]
    sh["w_kv_aug"] = wa
    sh["b_fgate_bc"] = f(np.broadcast_to(np.asarray(inp["b_fgate"])[None, :], (128, 12)))
    return sh


_CACHE = {}

def run(inputs, cfg, n_cores=8, x_override=None):
    key = repr(sorted(cfg.items()))
    if key not in _CACHE:
        _CACHE[key] = build(cfg)
    nc, K = _CACHE[key]
    sh = prep_shared(inputs)
    x = np.asarray(inputs["x"], dtype=np.float32) if x_override is None else x_override
    mem = np.asarray(inputs["mem"], dtype=np.float32)
    in_maps = []
    for b in range(n_cores):
        m = dict(sh)
        m["xT"] = np.ascontiguousarray(x[b].T)
        m["memT"] = np.ascontiguousarray(mem[b].T)
        in_maps.append(m)
    res = run_bass_kernel_spmd(nc, in_maps, core_ids=list(range(n_cores)))
    global LAST_RES
    LAST_RES = res.results
    return np.stack([np.ascontiguousarray(r["outT"].T) for r in res.results], 0)


def kernel(**inputs):
    return run(inputs, {"layers": (0, 1, 2, 3), "final": True}).astype(np.float32)
```

```python
import numpy as np
from contextlib import ExitStack
import concourse.bass as bass
import concourse.mybir as mybir
from concourse.bass_utils import run_bass_kernel_spmd

F32 = mybir.dt.float32
BF16 = mybir.dt.bfloat16
AF = mybir.ActivationFunctionType
ALU = mybir.AluOpType

D = 1024; S = 2048; DEPTH = 4; NA = 2
DH = 64; NH = 12; NMH = 4; NMEM = 256
DFF = 2816; NCH = 44
TQ = 512; NQB = 4
NCMP = 127
EPS = 1e-6
SCALE = 0.125

def _swap64(cols):
    cols = np.asarray(cols).reshape(-1, 64)
    return np.concatenate([cols[:, 32:], cols[:, :32]], axis=1).reshape(-1)

def nsa_cols():
    q = np.arange(768)
    kv0 = 768
    def kvc(i, g):
        return kv0 + (i * 2 + g) * 64 + np.arange(64)
    fm = []
    qs = _swap64(q)
    for i in range(6):
        fm += [q[128 * i:128 * i + 128], qs[128 * i:128 * i + 128]]
    for i in (2, 4):
        for g in range(2):
            fm += [np.concatenate([kvc(i, g), kvc(i, g)])]
            fm += [np.concatenate([_swap64(kvc(i, g)), _swap64(kvc(i, g))])]
    fm += [np.concatenate([kvc(0, 0), kvc(0, 1)])]
    fm += [np.concatenate([kvc(1, 0), kvc(1, 1)])]
    qm = 768 + 768 + 36 + np.arange(256)
    fm += [qm]
    fmc = np.concatenate(fm)
    gates = 768 + 768 + np.arange(36)
    tm = np.concatenate([kvc(3, 0), kvc(3, 0), kvc(3, 1), kvc(3, 1),
                         kvc(5, 0), kvc(5, 0), kvc(5, 1), kvc(5, 1)])
    return fmc, gates, tm

NSA_FM, NSA_GATE, NSA_TM = nsa_cols()
NSA_NCOL = len(NSA_FM) + 128 + len(NSA_TM)


class Buf:
    __slots__ = ("name", "w", "readers", "excl")
    def __init__(self, name, excl=False):
        self.name = name; self.w = None; self.readers = {}; self.excl = excl


class Ker:
    def __init__(self, nc, stack):
        self.nc = nc
        self.eng = {"pe": nc.tensor, "act": nc.scalar, "dve": nc.vector, "pool": nc.gpsimd, "sp": nc.sync}
        self.sem = {e: stack.enter_context(nc.semaphore("s_" + e)) for e in self.eng}
        self.cnt = {e: 0 for e in self.eng}
        self.seen = {e: {} for e in self.eng}
        self.nds = 8
        self.dsem = {q: [stack.enter_context(nc.semaphore(f"d_{q}{i}")) for i in range(self.nds)]
                     for q in ("sp", "pool")}
        self.dval = {q: [0] * self.nds for q in ("sp", "pool")}
        self.dnext = {"sp": 0, "pool": 0}
        self.nwait = 0
        self.nops = {e: 0 for e in self.eng}
        self.phases = []

    def _need(self, e, tok):
        kind, a, v = tok
        key = (kind, a)
        if self.seen[e].get(key, 0) >= v:
            return
        if kind == "e":
            assert v <= self.cnt[a], f"wait on pending (non-incrementing) instruction of {a}"
            self.eng[e].wait_ge(self.sem[a], v)
        else:
            self.eng[e].wait_ge(self.dsem[a[0]][a[1]], v)
        self.nwait += 1
        self.seen[e][key] = v

    def _deps(self, e, reads, writes):
        for b in reads:
            if b.w is not None:
                if not (b.w[0] == "e" and b.w[1] == e and e == "pe"):
                    self._need(e, b.w)
            if b.excl:
                for (k, a), v in b.readers.items():
                    if k == "e" and a == e:
                        continue
                    self._need(e, (k, a, v))
        for b in writes:
            if b.w is not None and not (b.w[0] == "e" and b.w[1] == e):
                self._need(e, b.w)
            for (k, a), v in b.readers.items():
                if k == "e" and a == e:
                    continue
                self._need(e, (k, a, v))

    def op(self, e, fn, reads=(), writes=(), inc=True):
        self._deps(e, reads, writes)
        ins = fn(self.eng[e])
        self.nops[e] += 1
        if inc:
            ins.then_inc(self.sem[e], 1)
            self.cnt[e] += 1
            c = self.cnt[e]
        else:
            c = self.cnt[e] + 1
        for b in reads:
            k = ("e", e)
            if b.readers.get(k, 0) < c:
                b.readers[k] = c
        for b in writes:
            b.w = ("e", e, c); b.readers = {}
        return ins

    def dma(self, q, out_ap, in_ap, reads=(), writes=()):
        i = self.dnext[q]; self.dnext[q] = (i + 1) % self.nds
        if self.dval[q][i] > 0:
            self._need(q, ("d", (q, i), self.dval[q][i]))
        self._deps(q, reads, writes)
        self.dval[q][i] += 16
        v = self.dval[q][i]
        self.eng[q].dma_start(out=out_ap, in_=in_ap).then_inc(self.dsem[q][i], 16)
        for b in reads:
            b.readers[("d", (q, i))] = v
        for b in writes:
            b.w = ("d", (q, i), v); b.readers = {}

    def barrier(self):
        for e in ("pe", "act", "dve", "pool", "sp"):
            for f in ("pe", "act", "dve", "pool"):
                if f != e and self.cnt[f] > 0:
                    self._need(e, ("e", f, self.cnt[f]))
            for q in ("sp", "pool"):
                for i in range(self.nds):
                    if self.dval[q][i] > 0:
                        self._need(e, ("d", (q, i), self.dval[q][i]))

    def finish(self):
        for q in ("sp", "pool"):
            for i in range(self.nds):
                if self.dval[q][i] > 0:
                    self._need("sp", ("d", (q, i), self.dval[q][i]))


class Rot:
    def __init__(self, items):
        self.items = items; self.i = 0
    def get(self):
        it = self.items[self.i]; self.i = (self.i + 1) % len(self.items)
        return it


def build(cfg):
    layers = cfg.get("layers", [0, 1, 2, 3])
    do_final = cfg.get("final", True)
    mode = cfg.get("mode", "full")
    nc = bass.Bass("TRN2", target_bir_lowering=False)
    st = ExitStack()

    def din(name, shape, dt=F32):
        return nc.dram_tensor(name, list(shape), dt, kind="ExternalInput").ap()

    xT_d = din("xT", [D, S])
    memT_d = din("memT", [D, NMEM])
    gains_d = din("gains", [128, 14, 8])
    wmem_d = din("w_mem_kv", [DEPTH, D, 512])
    wo_d = din("w_o", [DEPTH, D, D])
    wup_d = din("w_up", [DEPTH, D, 2 * DFF])
    wdn_d = din("w_down", [DEPTH, DFF, D])
    cw_d = din("convw", [128, DEPTH, 3, NCH])
    cb_d = din("convb", [128, DEPTH, NCH])
    awin_d = din("a_w_aug", [NA, D, NSA_NCOL])
    agb_d = din("a_gate_b", [36, NA])
    cw1_d = din("cmp_w1", [NA, 2, 128, 32, 256])
    cpos_d = din("cmp_pos", [128, NA, 2, 32])
    cb1_d = din("cmp_b1", [128, NA, 2, 2])
    cw2k_d = din("cmp_w2k", [NA, 256, 256])
    cb2k_d = din("cmp_b2k", [128, NA, 2])
    cw2v_d = din("cmp_w2v", [NA, 256, 128])
    cb2v_d = din("cmp_b2v", [NA, 128])
    bwin_d = din("b_w_in", [NA, D, D])
    wkv_d = din("w_kv_aug", [D, 768 + 768 + 128])
    bfg_d = din("b_fgate_bc", [128, 12])
    cos_d = din("ropecos", [128, S]); sin_d = din("ropesin", [128, S])
    cosc_d = din("ropecosc", [128, 128]); sinc_d = din("ropesinc", [128, 128])
    ident_d = din("ident", [128, 128])
    tri_d = din("tri", [128, 128]); tric_d = din("tric", [128, 128])
    cmpmask_d = din("cmpmask", [128, S])
    overlap_d = din("overlap", [128, 32])
    expand_d = din("expand", [32, 16, 128])
    bonus_d = din("bonus", [128, 16, 32])
    gsel_d = din("rowidx", [36, 128])
    sel127_d = din("sel127", [128, 128])
    out_d = nc.dram_tensor("outT", [D, S], F32, kind="ExternalOutput").ap()
    scr_up = nc.dram_tensor("scr_up", [DEPTH, 24, 128, 2048], BF16, kind="Internal").ap()
    scr_dn = nc.dram_tensor("scr_dn", [DEPTH, 16, 128, 1536], BF16, kind="Internal").ap()
    cast_layers = set(cfg.get("cast_layers", (1, 2, 3)))
    dumps = {}

    K = Ker(nc, st)

    def sb(name, shape, dt):
        return st.enter_context(nc.sbuf_tensor("sb_" + name, list(shape), dt))

    def ps(name, shape=(128, 512), dt=F32):
        return st.enter_context(nc.psum_tensor("ps_" + name, list(shape), dt))

    xT = sb("xT", [128, 8, S], F32)
    xB = [[Buf(f"x{k}_{c}") for c in range(NQB)] for k in range(8)]
    hT = sb("hT", [128, 8, TQ], BF16); hB = [Buf(f"hT{k}") for k in range(8)]
    rstd = sb("rstd", [128, TQ], F32); rstdB = Buf("rstd")
    gains = sb("gains", [128, 14, 8], F32); gB = Buf("gains")
    NW = 4
    wt = [sb(f"wt{i}", [128, 2048], BF16) for i in range(NW)]
    wrot = Rot([(wt[i], Buf(f"wt{i}")) for i in range(NW)])
    c_ones = sb("c_ones", [128, 128], BF16)
    c_onesm = sb("c_onesm", [128, 128], BF16)
    c_ones32 = sb("c_ones32", [128, 128], F32)
    c_eps = sb("c_eps", [128, 1], F32)
    c_tri = sb("c_tri", [128, 128], BF16); c_tric = sb("c_tric", [128, 128], BF16)
    c_tri32 = sb("c_tri32", [128, 128], F32)
    c_ident = sb("c_ident", [128, 128], F32)
    c_sel127 = sb("c_sel127", [128, 128], F32)
    cB = Buf("consts")
    convw = sb("convw", [128, DEPTH, 3, NCH], F32); convb = sb("convb", [128, DEPTH, NCH], F32)
    qT = sb("qT", [128, 6, TQ], BF16); qB = Buf("qT")
    qmT = sb("qmT", [128, 2, TQ], BF16); qmB = Buf("qmT")
    oT = sb("oT", [128, 8, TQ], BF16); oB = Buf("oT")
    gated = sb("gated", [128, 11, TQ], BF16); gatedB = Buf("gated")
    sq = gated; sqB = gatedB
    ubuf = [sb(f"ubuf{i}", [128, TQ + 2], F32) for i in range(4)]
    ubB = [Buf(f"ubuf{i}") for i in range(4)]
    halo = sb("halo", [128, NCH, 2], F32); haloB = [Buf(f"halo{i}") for i in range(NCH)]
    sil = rstd; silB = rstdB
    E_t = [sb(f"E{i}", [128, TQ], BF16) for i in range(4)]
    Erot = Rot([(E_t[i], Buf(f"E{i}")) for i in range(4)])
    den_sb = [sb(f"den{i}", [128, TQ], F32) for i in range(2)]
    denrot = Rot([(den_sb[i], Buf(f"den{i}")) for i in range(2)])
    den_sb_items = denrot.items
    oacc1 = sb("oacc1", [128, TQ], F32); oaccB = Buf("oacc")
    tmpf = [sb(f"tmpf{i}", [128, TQ], F32) for i in range(2)]
    tmprot = Rot([(tmpf[i], Buf(f"tmpf{i}")) for i in range(2)])
    cacc = tmpf; caccB = [tmprot.items[i][1] for i in range(2)]
    mhT = oT; mhB = oB
    kmT = sb("kmT", [128, 2, NMEM], BF16); kmB = Buf("kmT")
    vm = sb("vm", [128, 2, 256], BF16); vmB = Buf("vm")
    KVBYTES = 48 * 1024
    kvraw = sb("kvraw", [128, KVBYTES // 2], BF16)
    fkT = kvraw[:, 0:6 * S].rearrange("p (k t) -> p k t", k=6); fkB = [Buf(f"fk{c}") for c in range(NQB)]
    fV = kvraw[:, 6 * S:12 * S].rearrange("p (j n) -> p j n", j=16); fVB = [Buf(f"fv{c}") for c in range(NQB)]
    dcum = sb("dcum", [128, 16, 12], F32); dcumB = [Buf(f"dcum{j}") for j in range(16)]
    logf = sb("logf", [128, 16, 12], F32); logfB = [Buf(f"logf{j}") for j in range(16)]
    fbias = logf
    dref = sb("dref", [128, 12], F32); drefB = Buf("dref")
    bfg = sb("bfg", [128, 12], F32)
    o = 0
    def carve(n):
        nonlocal o
        v = kvraw[:, o:o + n]; o += n
        return v
    kslcT = carve(2 * S).rearrange("p (g t) -> p g t", g=2); kwinT = carve(2 * S).rearrange("p (g t) -> p g t", g=2)
    vslc = carve(16 * 256).rearrange("p (j n) -> p j n", j=16); vwin = carve(16 * 256).rearrange("p (j n) -> p j n", j=16)
    ucmp = carve(2 * S).rearrange("p (k t) -> p k t", k=2)
    cmpmask = carve(TQ)
    kcT = carve(2 * 128).rearrange("p (g n) -> p g n", g=2)
    vc = carve(2 * 128).rearrange("p (g n) -> p g n", g=2)
    selT = carve(2 * TQ).rearrange("p (g t) -> p g t", g=2)
    expand = carve(16 * 128).rearrange("p (j s) -> p j s", j=16)
    assert o * 2 <= KVBYTES
    nsaKB = [Buf(f"nsak{c}") for c in range(NQB)]
    cmpB = Buf("cmp"); selB = Buf("selT"); hidB = Buf("hid"); cmB = Buf("cmpmask")
    overlap = sb("overlap", [128, 32], F32)
    bonus = sb("bonus", [128, 4, 32], F32); bonusB = Buf("bonus")
    e36 = sb("e36", [36, TQ], F32); e36B = Buf("e36")
    agb = sb("agb", [36, NA], F32)
    rowidx = sb("rowidx", [36, 128], F32)
    selh = [sb(f"selh{i}", [36, 128], F32) for i in range(2)]
    selhrot = Rot([(selh[i], Buf(f"selh{i}")) for i in range(2)])
    ropeB = Buf("rope")
    cosc = sb("cosc", [128, 128], F32); sinc = sb("sinc", [128, 128], F32)
    cpos = sb("cpos", [128, NA, 2, 32], BF16); cb1 = sb("cb1", [128, NA, 2, 2], F32)
    cb2k = sb("cb2k", [128, NA, 2], F32); cb2v = sb("cb2v", [1, NA, 128], BF16)
    hb = sb("hb", [128, 8], F32); hbB = Buf("hb")
    g_x = [sb(f"g_x{i}", [128, 128], F32) for i in range(4)]; g_xB = [Buf(f"g_x{i}") for i in range(4)]
    imp_s4 = sb("imp_s4", [128, 4, 32], F32); impB = Buf("imp_s")
    sc2 = sb("sc2", [128, 32], F32); sc2B = Buf("sc2")
    mx8 = sb("mx8", [128, 16], F32); mx8B = Buf("mx8")
    sel_s = sb("sel_s", [128, 32], F32); selsB = Buf("sel_s")

    pA = Rot([(ps(f"pA{i}"), Buf(f"pA{i}", True)) for i in range(2)])
    pA2 = Rot([pA.items[0]])
    pS = Rot([(ps(f"pS{i}"), Buf(f"pS{i}", True)) for i in range(2)] + [pA.items[1]])
    pN = Rot([(ps(f"pN{i}"), Buf(f"pN{i}", True)) for i in range(2)])
    pD = Rot([(ps(f"pD{i}"), Buf(f"pD{i}", True)) for i in range(2)])

    def mm(out, lhsT, rhs, start, stop, reads, writes, inc=True):
        return K.op("pe", lambda e: e.matmul(out, lhsT, rhs, start=start, stop=stop), reads, writes, inc)

    def act(out, in_, func, reads, writes, bias=0.0, scale=1.0):
        return K.op("act", lambda e: e.activation(out, in_, func, bias=bias, scale=scale), reads, writes)

    def tt(eng, out, in0, in1, op, reads, writes):
        return K.op(eng, lambda e: e.tensor_tensor(out, in0, in1, op), reads, writes)

    def ts(eng, out, in0, s1, s2, op0, op1, reads, writes):
        if op1 is None:
            return K.op(eng, lambda e: e.tensor_scalar(out, in0, s1, None, op0), reads, writes)
        return K.op(eng, lambda e: e.tensor_scalar(out, in0, s1, s2, op0, op1), reads, writes)

    def stt(eng, out, in0, scalar, in1, op0, op1, reads, writes):
        return K.op(eng, lambda e: e.scalar_tensor_tensor(out, in0, scalar, in1, op0, op1), reads, writes)

    def cp(eng, out, in_, reads, writes):
        return K.op(eng, lambda e: e.tensor_copy(out, in_), reads, writes)

    def _issue256(w_ap, col0, ncols):
        t, tb = wrot.get()
        wv = t[:, 0:8 * 256].rearrange("p (k n) -> p k n", k=8)
        K.dma("pool", wv[:, :, :ncols], w_ap.rearrange("(k p) n -> p k n", p=128)[:, :, col0:col0 + ncols],
              writes=(tb,))
        return wv, tb

    plan_q = []; issued_q = []
    AHEAD = 3

    def plan(specs):
        assert not plan_q and not issued_q
        plan_q.extend(specs)
        while plan_q and len(issued_q) < AHEAD:
            sp = plan_q.pop(0); issued_q.append((sp, _issue256(*sp)))

    def load256(w_ap, col0, ncols=256):
        if not issued_q and not plan_q:
            return _issue256(w_ap, col0, ncols)
        while plan_q and len(issued_q) < 1 + AHEAD:
            sp = plan_q.pop(0); issued_q.append((sp, _issue256(*sp)))
        sp, tile = issued_q.pop(0)
        assert sp[1] == col0 and sp[2] == ncols, (sp[1:], col0, ncols)
        return tile

    def wstream(loads, ahead):
        issued = []
        def get(i):
            while len(issued) < min(len(loads), i + 1 + ahead):
                issued.append(loads[len(issued)]())
            return issued[i]
        return get

    def load_w(dram_ap, view):
        t, b = wrot.get()
        K.dma("pool", view(t), dram_ap, reads=(), writes=(b,))
        return t, b

    K.dma("sp", gains[:], gains_d, writes=(gB,))
    K.dma("sp", convw[:], cw_d, writes=(cB,)); K.dma("sp", convb[:], cb_d, writes=(cB,))
    K.dma("sp", c_ident[:], ident_d, writes=(cB,))
    K.dma("sp", c_tri32[:], tri_d, writes=(cB,))
    K.dma("sp", c_sel127[:], sel127_d, writes=(cB,))
    K.dma("pool", c_tri[:], tri_d, writes=(cB,)); K.dma("pool", c_tric[:], tric_d, writes=(cB,))
    K.dma("sp", bfg[:], bfg_d, writes=(cB,))
    K.dma("sp", overlap[:], overlap_d, writes=(cB,)); pass
    K.dma("sp", rowidx[:], gsel_d, writes=(cB,)); K.dma("sp", agb[:], agb_d, writes=(cB,))
    K.op("dve", lambda e: e.tensor_scalar(agb[:], agb[:], -1.0, None, ALU.mult), reads=(cB,), writes=(cB,))
    K.dma("sp", cosc[:], cosc_d, writes=(cB,)); K.dma("sp", sinc[:], sinc_d, writes=(cB,))
    K.dma("pool", cpos[:], cpos_d, writes=(cB,)); K.dma("sp", cb1[:], cb1_d, writes=(cB,))
    K.dma("sp", cb2k[:], cb2k_d, writes=(cB,))
    K.dma("pool", cb2v[:], cb2v_d.rearrange("(o l) n -> o l n", o=1), writes=(cB,))
    K.op("dve", lambda e: e.memset(c_ones[:], 1.0), writes=(cB,))
    K.op("dve", lambda e: e.memset(c_onesm[:], 1.0 / 1024.0), writes=(cB,))
    K.op("dve", lambda e: e.memset(c_ones32[:], 1.0), writes=(cB,))
    K.op("dve", lambda e: e.memset(c_eps[:], EPS), writes=(cB,))
    K.op("dve", lambda e: e.memset(halo[:], 0.0), writes=tuple(haloB))
    for k in range(8):
        K.dma("sp", xT[:, k, :], xT_d[k * 128:(k + 1) * 128, :], writes=tuple(xB[k]))
    K.barrier()

    def rmsnorm_block(src, srcB, gidx, ncols, dst, dstB, col0=0, src_list=None):
        sl = (lambda k: src_list[k]) if src_list is not None else (lambda k: src[:, k, col0:col0 + ncols])
        for k in range(8):
            K.op("act", lambda e, k=k: e.activation(sq[:, k, :ncols], sl(k), AF.Square),
                 reads=(srcB[k],), writes=(sqB,))
        pt, pb = pA.get()
        for k in range(8):
            mm(pt[:, :ncols], c_onesm[:], sq[:, k, :ncols], k == 0, k == 7, (sqB, cB), (pb,), inc=(k == 7))
        act(rstd[:, :ncols], pt[:, :ncols], AF.Ln, (pb, cB), (rstdB,), bias=c_eps[:, 0:1])
        act(rstd[:, :ncols], rstd[:, :ncols], AF.Exp, (rstdB,), (rstdB,), scale=-0.5)
        for k in range(8):
            stt("dve", dst[:, k, :ncols], sl(k),
                gains[:, gidx, k:k + 1], rstd[:, :ncols], ALU.mult, ALU.mult, (srcB[k], rstdB, gB),
                (dstB[k] if isinstance(dstB, list) else dstB,))

    def proj_fm(wtile, wb, wcol0, ncol, rhsT, rhsB, ncols_tok):
        pt, pb = pA.get()
        for k in range(8):
            mm(pt[:ncol, :ncols_tok], wtile[:, k, wcol0:wcol0 + ncol], rhsT[:, k, :ncols_tok],
               k == 0, k == 7, (wb, rhsB[k] if isinstance(rhsB, list) else rhsB), (pb,), inc=(k == 7))
        return pt, pb

    def ffn_tiles(l):
        srcu = wup_d[l].rearrange("(k p) f -> p k f", p=128)
        srcd = wdn_d[l].rearrange("(i p) n -> p i n", p=128)
        tiles = []
        iu = 0; idn = 0
        for half in range(2):
            for pi in range(0, 11, 2):
                npair = min(2, 11 - pi)
                for ab in range(2):
                    c0 = ab * DFF + (half * 11 + pi) * 128
                    tiles.append(("u", iu, srcu[:, :, c0:c0 + 128 * npair], scr_up[l, iu], 8, 128 * npair)); iu += 1
            for nn in range(4):
                for (f0, nf) in ((0, 6), (6, 5)):
                    tiles.append(("d", idn, srcd[:, half * 11 + f0:half * 11 + f0 + nf, nn * 256:(nn + 1) * 256],
                                  scr_dn[l, idn], nf, 256)); idn += 1
        return tiles

    castB = {l: [Buf(f"cast{l}_{i}") for i in range(40)] for l in range(DEPTH)}

    def cast_piece(l, piece):
        tl = ffn_tiles(l)
        for i in range(piece * 10, piece * 10 + 10):
            kind, idx, src, scr, a, n = tl[i]
            dst = scr[:, 0:a * 256].rearrange("p (a n) -> p a n", a=a)[:, :, 0:n]
            K.dma("pool", dst, src, writes=(castB[l][i],))

    def ffn_block(l, c):
        xcB = [xB[k][c] for k in range(8)]
        rmsnorm_block(xT, xcB, 4 + l, TQ, hT, hB, col0=c * TQ)
        srcu = wup_d[l].rearrange("(k p) f -> p k f", p=128)
        srcd = wdn_d[l].rearrange("(i p) n -> p i n", p=128)
        loads = []
        def mk_up(c0, npair):
            def f():
                t, tb = wrot.get()
                wv = t[:, 0:8 * 256].rearrange("p (k n) -> p k n", k=8)
                K.dma("pool", wv[:, :, 0:128 * npair], srcu[:, :, c0:c0 + 128 * npair], writes=(tb,))
                return wv, tb
            return f
        def mk_dn(half, f0, nf, nn):
            def f():
                t, tb = wrot.get()
                wv = t[:, 0:nf * 256].rearrange("p (i n) -> p i n", i=nf)
                K.dma("pool", wv, srcd[:, half * 11 + f0:half * 11 + f0 + nf, nn * 256:(nn + 1) * 256], writes=(tb,))
                return wv, tb
            return f
        for half in range(2):
            for pi in range(0, 11, 2):
                npair = min(2, 11 - pi)
                for ab in range(2):
                    loads.append(mk_up(ab * DFF + (half * 11 + pi) * 128, npair))
            for nn in range(4):
                for (f0, nf) in ((0, 6), (6, 5)):
                    loads.append(mk_dn(half, f0, nf, nn))
        if l in cast_layers:
            tl = ffn_tiles(l)
            def mk_scr(i):
                kind, idx, src, scr, a, n = tl[i]
                def f():
                    t, tb = wrot.get()
                    wid = 2048 if kind == "u" else a * 256
                    K.dma("sp", t[:, 0:wid], scr[:, 0:wid], reads=(castB[l][i],), writes=(tb,))
                    if kind == "u":
                        return t[:, 0:2048].rearrange("p (k n) -> p k n", k=8), tb
                    return t[:, 0:a * 256].rearrange("p (i n) -> p i n", i=a), tb
                return f
            loads = [mk_scr(i) for i in range(40)]
        wget = wstream(loads, 2)
        li = 0
        for half in range(2):
            for pi in range(0, 11, 2):
                npair = min(2, 11 - pi)
                i0 = half * 11 + pi
                tiles = [wget(li), wget(li + 1)]; li += 2
                for j in range(npair):
                    i = i0 + j
                    accs = []
                    par = (pi + j) % 2
                    for ab in range(2):
                        ch = i + 22 * ab
                        wv, tb = tiles[ab]
                        pt, pb = proj_fm(wv, tb, j * 128, 128, hT, hB, TQ)
                        ui = ab + 2 * par
                        ub, ubb = ubuf[ui], ubB[ui]
                        if c > 0:
                            cp("pool", ub[:, 0:2], halo[:, ch, :], (haloB[ch],), (ubb,))
                        else:
                            K.op("pool", lambda e, ub=ub: e.memset(ub[:, 0:2], 0.0), writes=(ubb,))
                        act(ub[:, 2:TQ + 2], pt[:, :], AF.Copy, (pb,), (ubb,))
                        cp("pool", halo[:, ch, :], ub[:, TQ:TQ + 2], (ubb,), (haloB[ch],))
                        ca, cab = (cacc[ab], caccB[ab]) if par == 0 else den_sb_items[ab]
                        eng = "dve"
                        K.op("act", lambda e, ca=ca, pt=pt, ch=ch: e.activation(
                            ca[:], pt[:, :], AF.Identity, bias=convb[:, l, ch:ch + 1], scale=convw[:, l, 2, ch:ch + 1]),
                            reads=(pb, cB), writes=(cab,))
                        stt(eng, ca[:], ub[:, 1:TQ + 1], convw[:, l, 1, ch:ch + 1], ca[:], ALU.mult, ALU.add,
                            (ubb, cB, cab), (cab,))
                        stt(eng, ca[:], ub[:, 0:TQ], convw[:, l, 0, ch:ch + 1], ca[:], ALU.mult, ALU.add,
                            (ubb, cB, cab), (cab,))
                        accs.append((ca, cab))
                    sl_t, sl_b = (sil, silB) if par == 0 else (oacc1, oaccB)
                    act(sl_t[:], accs[0][0][:], AF.Silu, (accs[0][1],), (sl_b,))
                    tt("dve", gated[:, pi + j, :], sl_t[:], accs[1][0][:], ALU.mult, (sl_b, accs[1][1]), (gatedB,))
            for nn in range(4):
                tiles = [wget(li), wget(li + 1)]; li += 2
                for n2 in range(2):
                    n = nn * 2 + n2
                    pt, pb = pA.get()
                    for i in range(11):
                        wv, tb = tiles[0] if i < 6 else tiles[1]
                        ii = i if i < 6 else i - 6
                        mm(pt[:, :], wv[:, ii, n2 * 128:(n2 + 1) * 128], gated[:, i, :], i == 0, i == 10,
                           (tb, gatedB), (pb,), inc=(i == 10))
                    tt("dve", xT[:, n, c * TQ:(c + 1) * TQ], xT[:, n, c * TQ:(c + 1) * TQ], pt[:, :], ALU.add,
                       (pb, xB[n][c]), (xB[n][c],))

    def mem_kv(l):
        hold = [(tmpf[0], tmprot.items[0][1]), (tmpf[1], tmprot.items[1][1]), den_sb_items[0], den_sb_items[1]]
        srcs = []; srcBs = []
        for k in range(8):
            t_, b_ = hold[k // 2]
            ap_ = t_[:, (k % 2) * 256:(k % 2) * 256 + 256]
            K.dma("sp", ap_, memT_d[k * 128:(k + 1) * 128, :], writes=(b_,))
            srcs.append(ap_); srcBs.append(b_)
        plan([(wmem_d[l], 0, 256), (wmem_d[l], 256, 256)])
        rmsnorm_block(None, srcBs, 8 + l, NMEM, mhT, mhB, src_list=srcs)
        wv, tb = load256(wmem_d[l], 0)
        for ch in range(2):
            pt, pb = proj_fm(wv, tb, ch * 128, 128, mhT, mhB, NMEM)
            act(kmT[:, ch, :], pt[:, :NMEM], AF.Copy, (pb,), (kmB,))
        wv, tb = load256(wmem_d[l], 256)
        for mt in range(2):
            pt, pb = pA.get()
            for k in range(8):
                mm(pt[:, :256], mhT[:, k, mt * 128:(mt + 1) * 128], wv[:, k, 0:256], k == 0, k == 7,
                   (tb, mhB), (pb,), inc=(k == 7))
            act(vm[:, mt, :], pt[:, :256], AF.Copy, (pb,), (vmB,))

    class Pipe:
        def __init__(self, depth=1):
            self.depth = depth; self.pending = []
        def push(self, A, B):
            r = A()
            self.pending.append((B, r))
            while len(self.pending) > self.depth:
                b, rr = self.pending.pop(0); b(rr)
        def flush(self):
            while self.pending:
                b, rr = self.pending.pop(0); b(rr)

    pipe = Pipe(2)

    def attn_tile(kT_ap, kreads, q_ap, qreads, ncol, bias, post, V_ap, vreads, pn_t, pn_b, pd_t, pd_b,
                  first, last, col0, after=None, krows=128, extra=None, preB=None, preB_late=None):
        def A():
            st_t, st_b = pS.get()
            mm(st_t[:krows, :ncol], kT_ap, q_ap, True, extra is None, kreads + qreads, (st_b,))
            if extra is not None:
                mm(st_t[:krows, :ncol], extra[0], extra[1], False, True, extra[2], (st_b,))
            e_t, e_b = Erot.get()
            K.op("act", lambda e: e.activation(e_t[:krows, :ncol], st_t[:krows, :ncol], AF.Exp, bias=bias[0],
                                               scale=SCALE),
                 reads=(st_b,) + bias[1], writes=(e_b,))
            if post is not None:
                post(e_t, e_b)
            if preB is not None:
                preB()
            return e_t, e_b
        def B(r):
            e_t, e_b = r
            if preB_late is not None:
                preB_late()
            mm(pn_t[:, col0:col0 + ncol], V_ap, e_t[:krows, :ncol], first, last, vreads + (e_b,), (pn_b,))
            mm(pd_t[:, col0:col0 + ncol], c_ones[:krows, :], e_t[:krows, :ncol], first, last, (e_b, cB), (pd_b,))
            if after is not None:
                after()
        pipe.push(A, B)

    def mask_sub(mask_ap):
        def post(e_t, e_b):
            tt("pool", e_t[:, 0:128], e_t[:, 0:128], mask_ap, ALU.mult, (e_b, cB), (e_b,))
        return post

    def causal_attention(kT_fn, V_fn, q_ap_fn, c, bias_fn, pr, njt=None, sel_fn=None, after_fn=None, preB=None):
        pn_t, pn_b = pN.get(); pd_t, pd_b = pD.get()
        tiles = list(range(4 * c + 4))
        nt = len(tiles)
        order = [4 * c] + list(range(4 * c)) + [4 * c + 1, 4 * c + 2, 4 * c + 3]
        for idx, j in enumerate(order):
            i = j - 4 * c
            if i < 0:
                col0, ncol, post = 0, TQ, None
            else:
                col0, ncol = 128 * i, TQ - 128 * i
                post = mask_sub(c_tri[:, :])
            extra = None
            if sel_fn is not None:
                post, extra = sel_fn(j, col0, ncol, post)
            kap, kr = kT_fn(j); vap, vr = V_fn(j)
            qap, qr = q_ap_fn(col0, ncol)
            aft = None
            if idx == nt - 1 and after_fn is not None:
                aft = (lambda: after_fn(pn_t, pn_b, pd_t, pd_b))
            attn_tile(kap, kr, qap, qr, ncol, bias_fn(j), post, vap, vr, pn_t, pn_b, pd_t, pd_b,
                      idx == 0, idx == nt - 1, col0, after=aft, extra=extra,
                      preB=(preB if idx == nt - 1 else None))
        return pn_t, pn_b, pd_t, pd_b

    def finish_head_plain(pn_t, pn_b, pd_t, pd_b, pr, dst_ap):
        d_t, d_b = denrot.get()
        act(d_t[pr, :], pd_t[pr, :], AF.Ln, (pd_b,), (d_b,))
        act(d_t[pr, :], d_t[pr, :], AF.Exp, (d_b,), (d_b,), scale=-1.0)
        tt("dve", dst_ap, pn_t[pr, :], d_t[pr, :], ALU.mult, (pn_b, d_b), (oB,))

    def mem_attention(c):
        for hm in range(4):
            ch, off = hm // 2, 64 * (hm % 2)
            pr = slice(off, off + 64)
            pn_t, pn_b = pN.get(); pd_t, pd_b = pD.get()
            for mt in range(2):
                aft = None
                if mt == 1:
                    aft = (lambda pn_t=pn_t, pn_b=pn_b, pd_t=pd_t, pd_b=pd_b, pr=pr, ch=ch:
                           finish_head_plain(pn_t, pn_b, pd_t, pd_b, pr, oT[pr, 6 + ch, :]))
                attn_tile(kmT[pr, ch, mt * 128:(mt + 1) * 128], (kmB,), qmT[pr, ch, :], (qmB,), TQ, (0.0, ()),
                          None, vm[:, mt, ch * 128:(ch + 1) * 128], (vmB,), pn_t, pn_b, pd_t, pd_b,
                          mt == 0, mt == 1, 0, after=aft)
        pipe.flush()

    def wo_block(l, c):
        plan([(wo_d[l], nn * 256, 256) for nn in range(4)])
        for nn in range(4):
            wv, tb = load256(wo_d[l], nn * 256)
            for n2 in range(2):
                n = nn * 2 + n2
                pt, pb = proj_fm(wv, tb, n2 * 128, 128, oT, oB, TQ)
                tt("dve", xT[:, n, c * TQ:(c + 1) * TQ], xT[:, n, c * TQ:(c + 1) * TQ], pt[:, :], ALU.add,
                   (pb, xB[n][c]), (xB[n][c],))

    def fox_shared_kv(c):
        plan([(wkv_d, cc * 256, 256) for cc in range(3)]
             + [(wkv_d, 768 + cc * 256, 256 if cc < 3 else 128) for cc in range(4)])
        xcB = [xB[k][c] for k in range(8)]
        rmsnorm_block(xT, xcB, 12, TQ, hT, hB, col0=c * TQ)
        for cc in range(3):
            wv, tb = load256(wkv_d, cc * 256)
            for j in range(2):
                ch = cc * 2 + j
                pt, pb = proj_fm(wv, tb, j * 128, 128, hT, hB, TQ)
                act(fkT[:, ch, c * TQ:(c + 1) * TQ], pt[:, :], AF.Copy, (pb,), (fkB[c],))
        for cc in range(4):
            ncols = 256 if cc < 3 else 128
            wv, tb = load256(wkv_d, 768 + cc * 256, ncols)
            for jt in range(4):
                j = 4 * c + jt
                pt, pb = pA.get()
                for k in range(8):
                    mm(pt[:, :ncols], hT[:, k, jt * 128:(jt + 1) * 128], wv[:, k, :ncols], k == 0, k == 7,
                       (tb, hB[k]), (pb,), inc=(k == 7))
                if cc < 3:
                    act(fV[:, j, cc * 256:(cc + 1) * 256], pt[:, :256], AF.Copy, (pb,), (fVB[c],))
                else:
                    tt("dve", logf[:, j, :], pt[:, 0:12], bfg[:], ALU.add, (pb, cB), (logfB[j],))
                    if cfg.get("dbg", 0) == 3:
                        continue
                    act(logf[:, j, :], logf[:, j, :], AF.Exp, (logfB[j],), (logfB[j],), scale=-1.0)
                    if cfg.get("dbg", 0) == 4:
                        continue
                    act(logf[:, j, :], logf[:, j, :], AF.Ln, (logfB[j], cB), (logfB[j],), bias=c_ones32[:, 0:1])
                    if cfg.get("dbg", 0) == 5:
                        continue
                    ts("dve", logf[:, j, :], logf[:, j, :], -1.0, None, ALU.mult, None, (logfB[j],), (logfB[j],))
        for jt in range(4):
            if cfg.get("dbg", 0) == 1:
                break
            j = 4 * c + jt
            pt, pb = pA.get()
            for jj in range(j + 1):
                lhs = c_tri32[:] if jj == j else c_ones32[:]
                mm(pt[:, :12], lhs, logf[:, jj, :], jj == 0, jj == j, (cB, logfB[jj]), (pb,), inc=(jj == j))
            act(dcum[:, j, :], pt[:, :12], AF.Copy, (pb,), (dcumB[j],))

    def fox_attention(l, c):
        if (l + 1) in cast_layers and l + 1 < DEPTH:
            cast_piece(l + 1, c)
        pt, pb = pA2.get()
        mm(pt[:, :12], c_sel127[:], dcum[:, 4 * c + 1, :], True, True, (cB, dcumB[4 * c + 1]), (pb,))
        act(dref[:], pt[:, :12], AF.Copy, (pb,), (drefB,))
        for j in range(4 * c + 4):
            tt("pool", fbias[:, j, :], dref[:], dcum[:, j, :], ALU.subtract, (drefB, dcumB[j]), (logfB[j],))
        for h in range(NH):
            ch, off = h // 2, 64 * (h % 2)
            pr = slice(off, off + 64)
            causal_attention(
                lambda j, pr=pr, ch=ch: (fkT[pr, ch, j * 128:(j + 1) * 128], (fkB[j // 4],)),
                lambda j, ch=ch: (fV[:, j, ch * 128:(ch + 1) * 128], (fVB[j // 4],)),
                lambda col0, ncol, pr=pr, ch=ch: (qT[pr, ch, col0:col0 + ncol], (qB,)),
                c, lambda j, h=h: (fbias[:, j, h:h + 1], (logfB[j],)), pr,
                after_fn=(lambda a, b, c_, d, pr=pr, ch=ch: finish_head_plain(a, b, c_, d, pr, oT[pr, ch, :])))
        pipe.flush()

    def fox_q_proj(l, c):
        plan([(bwin_d[l - NA], cc * 256, 256) for cc in range(4)])
        xcB = [xB[k][c] for k in range(8)]
        rmsnorm_block(xT, xcB, l, TQ, hT, hB, col0=c * TQ)
        for cc in range(4):
            wv, tb = load256(bwin_d[l - NA], cc * 256)
            for j in range(2):
                ch = cc * 2 + j
                pt, pb = proj_fm(wv, tb, j * 128, 128, hT, hB, TQ)
                if ch < 6:
                    act(qT[:, ch, :], pt[:, :], AF.Copy, (pb,), (qB,))
                else:
                    act(qmT[:, ch - 6, :], pt[:, :], AF.Copy, (pb,), (qmB,))

    def rope_evac(pt, pb, pts, pbs, dst_ap, dstB, cs, sn, rB, ncols):
        t1, t1b = tmprot.get()
        tt("dve", t1[:, :ncols], pt[:, :ncols], cs, ALU.mult, (pb,) + rB, (t1b,))
        t2, t2b = tmprot.get()
        tt("dve", t2[:, :ncols], pts[:, :ncols], sn, ALU.mult, (pbs,) + rB, (t2b,))
        tt("pool", dst_ap, t1[:, :ncols], t2[:, :ncols], ALU.add, (t1b, t2b), (dstB,))

    def nsa_load_w(l, col0, ncols):
        return load256(awin_d[l], col0, ncols)

    def nsa_rope_tables(c):
        rc, rcb = denrot.items[0]; rs, rsb = denrot.items[1]
        K.dma("sp", rc[:], cos_d[:, c * TQ:(c + 1) * TQ], writes=(rcb,))
        K.dma("sp", rs[:], sin_d[:, c * TQ:(c + 1) * TQ], writes=(rsb,))
        return rc, rs, (rcb, rsb)

    def nsa_kv_proj(l, c):
        plan([(awin_d[l], (6 + 2 * bi + g) * 256, 256) for bi in range(2) for g in range(2)]
             + [(awin_d[l], 20 * 128, 256)] + [(awin_d[l], 3072 + 128 + vi * 256, 256) for vi in range(2)])
        xcB = [xB[k][c] for k in range(8)]
        rmsnorm_block(xT, xcB, l, TQ, hT, hB, col0=c * TQ)
        rc, rs, rB = nsa_rope_tables(c)
        tsl = slice(c * TQ, (c + 1) * TQ)
        for bi, dstT in ((0, kslcT), (1, kwinT)):
            for g in range(2):
                wv, tb = nsa_load_w(l, (6 + 2 * bi + g) * 256, 256)
                pt, pb = proj_fm(wv, tb, 0, 128, hT, hB, TQ)
                pts, pbs = proj_fm(wv, tb, 128, 128, hT, hB, TQ)
                rope_evac(pt, pb, pts, pbs, dstT[:, g, tsl], nsaKB[c], rc[:], rs[:], rB, TQ)
        wv, tb = nsa_load_w(l, 20 * 128, 256)
        for kv in range(2):
            pt, pb = proj_fm(wv, tb, kv * 128, 128, hT, hB, TQ)
            act(ucmp[:, kv, tsl], pt[:, :], AF.Copy, (pb,), (nsaKB[c],))
        for vi, vdst in ((0, vslc), (1, vwin)):
            wv, tb = nsa_load_w(l, 3072 + 128 + vi * 256, 256)
            for jt in range(4):
                j = 4 * c + jt
                pt, pb = pA.get()
                for k in range(8):
                    mm(pt[:, :256], hT[:, k, jt * 128:(jt + 1) * 128], wv[:, k, :], k == 0, k == 7, (tb, hB[k]), (pb,),
                       inc=(k == 7))
                act(vdst[:, j, :], pt[:, 0:256], AF.Copy, (pb,), (nsaKB[c],))

    def gelu_tanh(dst_ap, dstB, pt, pb, bias_ap, npart, ncols):
        x, xb = g_x[0], g_xB[0]; x2, x2b = g_x[1], g_xB[1]; th, thb = g_x[2], g_xB[2]
        act(x[:npart, :ncols], pt[:npart, :ncols], AF.Identity, (pb, cB), (xb,), bias=bias_ap)
        tt("dve", x2[:npart, :ncols], x[:npart, :ncols], x[:npart, :ncols], ALU.mult, (xb,), (x2b,))
        ts("dve", x2[:npart, :ncols], x2[:npart, :ncols], 0.044715, 1.0, ALU.mult, ALU.add, (x2b,), (x2b,))
        tt("dve", x2[:npart, :ncols], x2[:npart, :ncols], x[:npart, :ncols], ALU.mult, (x2b, xb), (x2b,))
        act(th[:npart, :ncols], x2[:npart, :ncols], AF.Tanh, (x2b,), (thb,), scale=0.7978845608028654)
        ts("dve", th[:npart, :ncols], th[:npart, :ncols], 1.0, 0.5, ALU.add, ALU.mult, (thb,), (thb,))
        tt("dve", dst_ap, th[:npart, :ncols], x[:npart, :ncols], ALU.mult, (thb, xb), (dstB,))

    def nsa_compress(l):
        allk = tuple(nsaKB)
        K.op("dve", lambda e: e.memset(expand, 0.0), writes=(cmpB,))
        K.op("dve", lambda e: e.memset(selT, 0.0), writes=(selB,))
        K.dma("pool", expand[0:32], expand_d, writes=(cmpB,))
        for kv in range(2):
            halves = []
            for hf in range(4):
                t, tb = wrot.get()
                wv = t[:, 0:8 * 256].rearrange("p (l n) -> p l n", l=8)
                K.dma("pool", wv, cw1_d[l, kv][:, hf * 8:(hf + 1) * 8, :], writes=(tb,))
                halves.append((wv, tb))
            for hc in range(2):
                pt, pb = pA.get()
                for li in range(32):
                    wv, tb = halves[li // 8]
                    mm(pt[:, 0:1], wv[0:64, li % 8, hc * 128:(hc + 1) * 128], cpos[0:64, l, kv, li:li + 1],
                       li == 0, li == 31, (tb, cB), (pb,), inc=(li == 31))
                tt("dve", hb[:, hc:hc + 1], pt[:, 0:1], cb1[:, l, kv, hc:hc + 1], ALU.add, (pb, cB), (hbB,))
                for g in range(2):
                    pr = slice(64 * g, 64 * g + 64)
                    pt2, pb2 = pA.get()
                    for li in range(32):
                        wv, tb = halves[li // 8]
                        rhs = ucmp[pr, kv, li:li + 16 * (NCMP - 1) + 1:16]
                        mm(pt2[:, :NCMP], wv[pr, li % 8, hc * 128:(hc + 1) * 128], rhs, li == 0, li == 31,
                           (tb,) + allk, (pb2,), inc=(li == 31))
                    gelu_tanh(hid_g[g][:, hc, :NCMP], hidB, pt2, pb2, hb[:, hc:hc + 1], 128, NCMP)
            if kv == 0:
                t, tb = wrot.get()
                wv = t[:, 0:2 * 256].rearrange("p (k n) -> p k n", k=2)
                K.dma("pool", wv, cw2k_d[l].rearrange("(k p) n -> p k n", p=128), writes=(tb,))
                for g in range(2):
                    pt, pb = pA.get(); pts, pbs = pA.get()
                    for hc in range(2):
                        mm(pt[:, :NCMP], wv[:, hc, 0:128], hid_g[g][:, hc, :NCMP], hc == 0, hc == 1, (tb, hidB),
                           (pb,))
                    for hc in range(2):
                        mm(pts[:, :NCMP], wv[:, hc, 128:256], hid_g[g][:, hc, :NCMP], hc == 0, hc == 1, (tb, hidB),
                           (pbs,))
                    a, ab_ = g_x[0], g_xB[0]; b, bb_ = g_x[1], g_xB[1]
                    act(a[:, :NCMP], pt[:, :NCMP], AF.Identity, (pb, cB), (ab_,), bias=cb2k[:, l, 0:1])
                    act(b[:, :NCMP], pts[:, :NCMP], AF.Identity, (pbs, cB), (bb_,), bias=cb2k[:, l, 1:2])
                    tt("dve", a[:, :NCMP], a[:, :NCMP], cosc[:, :NCMP], ALU.mult, (ab_, cB), (ab_,))
                    tt("dve", b[:, :NCMP], b[:, :NCMP], sinc[:, :NCMP], ALU.mult, (bb_, cB), (bb_,))
                    tt("dve", kcT[:, g, :NCMP], a[:, :NCMP], b[:, :NCMP], ALU.add, (ab_, bb_), (cmpB,))
            else:
                t, tb = wrot.get()
                wv = t[:, 0:2 * 128].rearrange("p (k n) -> p k n", k=2)
                K.dma("pool", wv, cw2v_d[l].rearrange("(k p) n -> p k n", p=128), writes=(tb,))
                for g in range(2):
                    pt, pb = pA.get()
                    for hc in range(2):
                        mm(pt[:NCMP, :128], hid_g[g][:, hc, :NCMP], wv[:, hc, :], hc == 0, False, (tb, hidB), (pb,),
                           inc=False)
                    mm(pt[:NCMP, :128], c_ones[0:1, :NCMP], cb2v[0:1, l, :], False, True, (cB,), (pb,))
                    act(vc[:NCMP, g, :], pt[:NCMP, :128], AF.Copy, (pb,), (cmpB,))

    hid_g = [sb(f"hid_g{g}", [128, 2, 128], BF16) for g in range(2)]

    def nsa_q_proj(l, c):
        plan([(awin_d[l], ch * 256, 256) for ch in range(6)] + [(awin_d[l], 22 * 128, 256), (awin_d[l], 3072, 128)])
        xcB = [xB[k][c] for k in range(8)]
        rmsnorm_block(xT, xcB, l, TQ, hT, hB, col0=c * TQ)
        rc, rs, rB = nsa_rope_tables(c)
        for ch in range(6):
            wv, tb = nsa_load_w(l, ch * 256, 256)
            pt, pb = proj_fm(wv, tb, 0, 128, hT, hB, TQ)
            pts, pbs = proj_fm(wv, tb, 128, 128, hT, hB, TQ)
            rope_evac(pt, pb, pts, pbs, qT[:, ch, :], qB, rc[:], rs[:], rB, TQ)
        wv, tb = nsa_load_w(l, 22 * 128, 256)
        for j in range(2):
            pt, pb = proj_fm(wv, tb, j * 128, 128, hT, hB, TQ)
            act(qmT[:, j, :], pt[:, :], AF.Copy, (pb,), (qmB,))
        wv, tb = nsa_load_w(l, 3072, 128)
        pt, pb = proj_fm(wv, tb, 0, 36, hT, hB, TQ)
        act(e36[:, :], pt[:36, :], AF.Exp, (pb, cB), (e36B,), bias=agb[:, l:l + 1], scale=-1.0)

    def nsa_attention(l, c):
        if (l + 1) in cast_layers:
            cast_piece(l + 1, c)
        K.dma("pool", cmpmask, cmpmask_d[:, c * TQ:(c + 1) * TQ], writes=(cmB,))
        K.dma("sp", bonus[:], bonus_d[:, 4 * c:4 * c + 4, :], writes=(bonusB,))
        use_sel = c >= 2

        def cmp_scores(h, g, pr, ch):
            st_t, st_b = pS.get()
            mm(st_t[:NCMP, :], kcT[pr, g, :NCMP], qT[pr, ch, :], True, True, (cmpB, qB), (st_b,))
            e_t, e_b = Erot.get()
            act(e_t[:NCMP, :], st_t[:NCMP, :], AF.Exp, (st_b,), (e_b,), scale=SCALE)
            tt("dve", e_t[:NCMP, :], e_t[:NCMP, :], cmpmask[:NCMP, :], ALU.mult, (e_b, cmB), (e_b,))
            pd_t, pd_b = pD.get()
            mm(pd_t[:, :], c_ones[:NCMP, :], e_t[:NCMP, :], True, True, (cB, e_b), (pd_b,))
            d_t, d_b = denrot.get()
            ts("dve", d_t[:, :], pd_t[:, :], 1e-30, None, ALU.max, None, (pd_b,), (d_b,))
            act(d_t[:, :], d_t[:, :], AF.Ln, (d_b,), (d_b,))
            act(d_t[:, :], d_t[:, :], AF.Exp, (d_b,), (d_b,), scale=-1.0)
            return e_t, e_b, d_t, d_b

        for g in range(2):
            heads = list(range(6 * g, 6 * g + 6))
            if use_sel:
                pipe.flush()
                ip_t, ip_b = pA2.get()
                for h in heads:
                    ch, off = h // 2, 64 * (h % 2)
                    pr = slice(off, off + 64)
                    e_t, e_b, d_t, d_b = cmp_scores(h, g, pr, ch)
                    pn, pnB = tmprot.get()
                    tt("dve", pn[:NCMP, :], e_t[:NCMP, :], d_t[:NCMP, :], ALU.mult, (e_b, d_b), (pnB,))
                    for tt_i in range(4):
                        mm(ip_t[:, tt_i * 32:(tt_i + 1) * 32], pn[:NCMP, tt_i * 128:(tt_i + 1) * 128],
                           overlap[:NCMP, :], h == heads[0] and tt_i == 0, h == heads[-1] and tt_i == 3,
                           (pnB, cB), (ip_b,))
                for tt_i in range(4):
                    tt("dve", imp_s4[:, tt_i, :], ip_t[:, tt_i * 32:(tt_i + 1) * 32], bonus[:, tt_i, :], ALU.add,
                       (ip_b, bonusB), (impB,))
                for tt_i in range(4):
                    imp_s = imp_s4[:, tt_i, :]
                    K.op("dve", lambda e: e.max(out=mx8[:, 0:8], in_=imp_s), reads=(impB,), writes=(mx8B,))
                    K.op("dve", lambda e: e.match_replace(out=sc2[:, :], in_to_replace=mx8[:, 0:8],
                                                          in_values=imp_s, imm_value=-3.0e38),
                         reads=(impB, mx8B), writes=(sc2B,))
                    K.op("dve", lambda e: e.max(out=mx8[:, 8:16], in_=sc2[:, :]), reads=(sc2B,), writes=(mx8B,))
                    ts("dve", sel_s[:, :], imp_s, mx8[:, 15:16], None, ALU.is_ge, None, (impB, mx8B),
                       (selsB,))
                    ts("dve", sel_s[:, :], sel_s[:, :], -1.0, 30000.0, ALU.add, ALU.mult, (selsB,), (selsB,))
                    tp_t, tp_b = pA2.get()
                    K.op("pe", lambda e, tp_t=tp_t: e.transpose(tp_t[:32, :128], sel_s[:, :], c_ident[:, :]),
                         reads=(selsB, cB), writes=(tp_b,))
                    act(selT[:32, g, tt_i * 128:(tt_i + 1) * 128], tp_t[:32, :128], AF.Copy, (tp_b,), (selB,))
            for h in heads:
                ch, off = h // 2, 64 * (h % 2)
                pr = slice(off, off + 64)
                qfn = lambda col0, ncol, pr=pr, ch=ch: (qT[pr, ch, col0:col0 + ncol], (qB,))
                head_gates(l, h)

                def fin(br, last, h=h, pr=pr, ch=ch):
                    def f(pn_t, pn_b, pd_t, pd_b):
                        d_t, d_b = denrot.get()
                        ts("dve", d_t[pr, :], pd_t[pr, :], 1e-30, None, ALU.max, None, (pd_b,), (d_b,))
                        finish_gated(h, br, pn_t, pn_b, d_t, d_b, pr, first=(br == 0), last=last,
                                     dst=oT[pr, ch, :])
                    return f

                pn_t, pn_b = pN.get(); pd_t, pd_b = pD.get()
                f0 = fin(0, False)
                attn_tile(kcT[pr, g, :NCMP], (cmpB,), qT[pr, ch, :], (qB,), TQ, (0.0, ()),
                          (lambda e_t, e_b: tt("dve", e_t[:NCMP, :], e_t[:NCMP, :], cmpmask[:NCMP, :], ALU.mult,
                                               (e_b, cmB), (e_b,))),
                          vc[:NCMP, g, :], (cmpB,), pn_t, pn_b, pd_t, pd_b, True, True, 0,
                          after=(lambda f0=f0, a=pn_t, b=pn_b, c_=pd_t, d=pd_b: f0(a, b, c_, d)), krows=NCMP,
                          preB_late=(None if cfg.get("br_only") is not None else (lambda h=h: prep_gate(h, 0))))

                def sel_fn(j, col0, ncol, post0, g=g):
                    if not use_sel:
                        return post0, None
                    return post0, (expand[:, j, :], selT[:, g, col0:col0 + ncol], (cmpB, selB))
                causal_attention(
                    lambda j, pr=pr, g=g: (kslcT[pr, g, j * 128:(j + 1) * 128], (nsaKB[j // 4],)),
                    lambda j, g=g: (vslc[:, j, g * 128:(g + 1) * 128], (nsaKB[j // 4],)),
                    qfn, c, lambda j: (0.0, ()), pr, sel_fn=sel_fn, after_fn=fin(1, False),
                    preB=(None if cfg.get("br_only") is not None else (lambda h=h: prep_gate(h, 1))))
                pn_t, pn_b = pN.get(); pd_t, pd_b = pD.get()
                order = [4 * c] + [j for j in range(4 * c - 4, 4 * c) if j >= 0] + [4 * c + 1, 4 * c + 2, 4 * c + 3]
                f2 = fin(2, True)
                for idx, j in enumerate(order):
                    i = j - 4 * c
                    if i >= 0:
                        col0, ncol = 128 * i, TQ - 128 * i
                        post = mask_sub(c_tri[:, :])
                    else:
                        ii = i + 4
                        col0, ncol = 0, 128 * (ii + 1)
                        def post(e_t, e_b, ii=ii):
                            tt("pool", e_t[:, 128 * ii:128 * ii + 128], e_t[:, 128 * ii:128 * ii + 128],
                               c_tric[:, :], ALU.mult, (e_b, cB), (e_b,))
                    aft = None
                    if idx == len(order) - 1:
                        aft = (lambda f2=f2, a=pn_t, b=pn_b, c_=pd_t, d=pd_b: f2(a, b, c_, d))
                    attn_tile(kwinT[pr, g, j * 128:(j + 1) * 128], (nsaKB[j // 4],),
                              qT[pr, ch, col0:col0 + ncol], (qB,), ncol, (0.0, ()), post,
                              vwin[:, j, g * 128:(g + 1) * 128], (nsaKB[j // 4],), pn_t, pn_b, pd_t, pd_b,
                              idx == 0, idx == len(order) - 1, col0, after=aft,
                              preB=((lambda h=h: prep_gate(h, 2))
                                    if (idx == len(order) - 1 and cfg.get("br_only") is None) else None))
        pipe.flush()

    def head_gates(l, h):
        return

    gate_bc = {}

    def prep_gate(h, br):
        s_t, s_b = selhrot.get()
        ts("dve", s_t[:, :], rowidx[:, :], float(3 * h + br), None, ALU.is_equal, None, (cB,), (s_b,))
        gp_t, gp_b = pA2.get()
        mm(gp_t[:, :], s_t[:36, :], e36[:36, :], True, True, (s_b, e36B), (gp_b,))
        gate_bc[(h, br)] = (gp_t, gp_b)

    def finish_gated(h, br, pn_t, pn_b, d_t, d_b, pr, first, last=False, dst=None):
        if cfg.get("br_only") is not None:
            if br == cfg["br_only"]:
                ch_ = h // 2
                K.op("dve", lambda e: e.reciprocal(d_t[pr, :], d_t[pr, :]), reads=(d_b,), writes=(d_b,))
                tt("dve", oT[pr, ch_, :], pn_t[pr, :], d_t[pr, :], ALU.mult, (pn_b, d_b), (oB,))
            return
        gp_t, gp_b = gate_bc.pop((h, br))
        oacc = oacc1
        w_t, w_b = tmprot.get()
        stt("dve", w_t[pr, :], gp_t[pr, :], 1.0, d_t[pr, :], ALU.add, ALU.mult, (gp_b, d_b), (w_b,))
        act(w_t[pr, :], w_t[pr, :], AF.Ln, (w_b,), (w_b,))
        act(w_t[pr, :], w_t[pr, :], AF.Exp, (w_b,), (w_b,), scale=-1.0)
        if first:
            tt("dve", oacc[pr, :], pn_t[pr, :], w_t[pr, :], ALU.mult, (pn_b, w_b), (oaccB,))
            return
        tt("dve", w_t[pr, :], pn_t[pr, :], w_t[pr, :], ALU.mult, (pn_b, w_b), (w_b,))
        if last:
            tt("dve", dst, oacc[pr, :], w_t[pr, :], ALU.add, (oaccB, w_b), (oB,))
        else:
            tt("dve", oacc[pr, :], oacc[pr, :], w_t[pr, :], ALU.add, (oaccB, w_b), (oaccB,))

    skip = set(cfg.get("skip", ()))
    def maybe(fn):
        def w(*a):
            if fn.__name__ in skip:
                return
            K.phases.append((fn.__name__, a, dict(K.nops)))
            return fn(*a)
        return w
    mem_kv = maybe(mem_kv); nsa_kv_proj = maybe(nsa_kv_proj); nsa_compress = maybe(nsa_compress)
    nsa_q_proj = maybe(nsa_q_proj); nsa_attention = maybe(nsa_attention); mem_attention = maybe(mem_attention)
    wo_block = maybe(wo_block); ffn_block = maybe(ffn_block); fox_shared_kv = maybe(fox_shared_kv)
    fox_q_proj = maybe(fox_q_proj); fox_attention = maybe(fox_attention)
    for l in layers:
        if mode == "ffn":
            for c in range(NQB):
                ffn_block(l, c)
            continue
        mem_kv(l)
        if l < NA:
            for c in range(NQB):
                nsa_kv_proj(l, c)
            nsa_compress(l)
            for c in range(NQB):
                nsa_q_proj(l, c)
                nsa_attention(l, c)
                mem_attention(c)
                wo_block(l, c)
                if mode != "attn":
                    ffn_block(l, c)
            K.barrier()
        else:
            if l == NA:
                K.barrier()
                for c in range(NQB):
                    fox_shared_kv(c)
            for c in range(NQB):
                fox_q_proj(l, c)
                fox_attention(l, c)
                mem_attention(c)
                wo_block(l, c)
                if mode != "attn":
                    ffn_block(l, c)
    if do_final:
        for c in range(NQB):
            xcB = [xB[k][c] for k in range(8)]
            for k in range(8):
                K.op("act", lambda e, k=k: e.activation(sq[:, k, :], xT[:, k, c * TQ:(c + 1) * TQ], AF.Square),
                     reads=(xcB[k],), writes=(sqB,))
            pt, pb = pA.get()
            for k in range(8):
                mm(pt[:, :], c_onesm[:], sq[:, k, :], k == 0, k == 7, (sqB, cB), (pb,), inc=(k == 7))
            act(rstd[:, :], pt[:, :], AF.Ln, (pb, cB), (rstdB,), bias=c_eps[:, 0:1])
            act(rstd[:, :], rstd[:, :], AF.Exp, (rstdB,), (rstdB,), scale=-0.5)
            for k in range(8):
                stt("dve", xT[:, k, c * TQ:(c + 1) * TQ], xT[:, k, c * TQ:(c + 1) * TQ], gains[:, 13, k:k + 1],
                    rstd[:, :], ALU.mult, ALU.mult, (xcB[k], rstdB, gB), (xcB[k],))
    for k in range(8):
        K.dma("sp", out_d[k * 128:(k + 1) * 128, :], xT[:, k, :], reads=tuple(xB[k]))
    if cfg.get("dump"):
        K.barrier()
        dl = {"oT": (oT, [128, 8, TQ], BF16), "qT": (qT, [128, 6, TQ], BF16), "qmT": (qmT, [128, 2, TQ], BF16),
              "kslcT": (kslcT, [128, 2, S], BF16), "kwinT": (kwinT, [128, 2, S], BF16),
              "vslc": (vslc, [128, 16, 256], BF16), "vwin": (vwin, [128, 16, 256], BF16),
              "ucmp": (ucmp, [128, 2, S], BF16), "kcT": (kcT, [128, 2, 128], BF16), "vc": (vc, [128, 2, 128], BF16),
              "selT": (selT, [128, 2, TQ], BF16), "kmT": (kmT, [128, 2, NMEM], BF16), "vm": (vm, [128, 2, 256], BF16),
              "hT": (hT, [128, 8, TQ], BF16), "hid0": (hid_g[0], [128, 2, 128], BF16)}
        for nm in cfg["dump"]:
            t_, shp, dt_ = dl[nm]
            dd = nc.dram_tensor("dump_" + nm, shp, dt_, kind="ExternalOutput").ap()
            K.dma("sp", dd, t_ if not hasattr(t_, "ap") else t_[:], reads=())
    K.finish()
    st.close()
    return nc, K


def _consts():
    c = {}
    c["ident"] = np.eye(128, dtype=np.float32)
    s = np.arange(128)[:, None]; t = np.arange(128)[None, :]
    c["tri"] = (s <= t).astype(np.float32)
    c["tric"] = (t < s).astype(np.float32)
    n = np.arange(128)[:, None]; tt = np.arange(S)[None, :]
    c["cmpmask"] = ((16 * n + 31 <= tt) & (n < NCMP)).astype(np.float32)
    cs = np.arange(128) * 16
    ss = np.arange(32) * 64
    ov = ((cs[:, None] < ss[None, :] + 64) & (cs[:, None] + 32 > ss[None, :])).astype(np.float32)
    ov[NCMP:] = 0
    c["overlap"] = ov
    ex = np.zeros((32, 16, 128), np.float32)
    for jt in range(16):
        for s_ in range(128):
            ex[2 * jt + s_ // 64, jt, s_] = 1.0
    c["expand"] = ex
    bn = np.zeros((128, 16, 32), np.float32)
    for tix in range(16):
        tpos = tix * 128 + np.arange(128)
        blk = tpos // 64
        j = np.arange(32)[None, :]
        forced = (j == 0) | (j == blk[:, None]) | (j == blk[:, None] - 1)
        valid = j <= blk[:, None]
        bn[:, tix, :] = np.where(valid, 1e4 * forced, -1e30)
    c["bonus"] = bn
    c["rowidx"] = np.broadcast_to(np.arange(36, dtype=np.float32)[:, None], (36, 128)).copy()
    s127 = np.zeros((128, 128), np.float32); s127[127, :] = 1.0
    c["sel127"] = s127
    half = 32
    inv = (10000.0 ** (-np.arange(half, dtype=np.float32) / half)).astype(np.float32)
    def tables(pos):
        ang = pos.astype(np.float32)[None, :] * inv[:, None]
        co = np.cos(ang).astype(np.float32); si = np.sin(ang).astype(np.float32)
        cos64 = np.concatenate([co, co], 0); sin64 = np.concatenate([-si, si], 0)
        return np.concatenate([cos64, cos64], 0), np.concatenate([sin64, sin64], 0)
    c["ropecos"], c["ropesin"] = tables(np.arange(S))
    pc = np.arange(128) * 16 + 31
    c["ropecosc"], c["ropesinc"] = tables(pc)
    return {k: np.ascontiguousarray(v, dtype=np.float32) for k, v in c.items()}


def _fm(vec_list):
    a = np.stack(vec_list, 0).reshape(len(vec_list), 8, 128)
    return np.ascontiguousarray(a.transpose(2, 0, 1))


def prep_shared(inp):
    f = lambda a: np.ascontiguousarray(np.asarray(a, dtype=np.float32))
    sh = dict(_consts())
    gl = [inp["attn_norm"][i] for i in range(4)] + [inp["ffn_norm"][i] for i in range(4)] + \
         [inp["mem_norm"][i] for i in range(4)] + [inp["kv_norm"], inp["final_norm"]]
    sh["gains"] = _fm([np.asarray(g) for g in gl])
    sh["w_mem_kv"] = f(inp["w_mem_kv"]); sh["w_o"] = f(inp["w_o"]); sh["w_up"] = f(inp["w_up"])
    sh["w_down"] = f(inp["w_down"])
    cw = np.asarray(inp["conv_w"]).reshape(DEPTH, 3, NCH, 128)
    sh["convw"] = f(cw.transpose(3, 0, 1, 2))
    sh["convb"] = f(np.asarray(inp["conv_b"]).reshape(DEPTH, NCH, 128).transpose(2, 0, 1))
    aw = np.asarray(inp["a_w_in"])
    aug = np.zeros((NA, D, NSA_NCOL), np.float32)
    aug[:, :, :3072] = aw[:, :, NSA_FM]
    aug[:, :, 3072:3072 + 36] = aw[:, :, NSA_GATE]
    aug[:, :, 3200:] = aw[:, :, NSA_TM]
    sh["a_w_aug"] = aug
    sh["a_gate_b"] = f(np.asarray(inp["a_gate_b"]).T)
    w1 = np.asarray(inp["a_cmp_w1"]).reshape(NA, 2, 32, 64, 256).transpose(0, 1, 3, 2, 4)
    sh["cmp_w1"] = f(np.concatenate([w1, w1], axis=2))
    pos = np.asarray(inp["a_cmp_pos"]).transpose(3, 0, 1, 2)
    sh["cmp_pos"] = f(np.concatenate([pos, pos], 0))
    sh["cmp_b1"] = f(np.asarray(inp["a_cmp_b1"]).reshape(NA, 2, 2, 128).transpose(3, 0, 1, 2))
    w2 = np.asarray(inp["a_cmp_w2"]); b2 = np.asarray(inp["a_cmp_b2"])
    sw = _swap64(np.arange(64))
    w2k = w2[:, 0]
    sh["cmp_w2k"] = f(np.concatenate([w2k, w2k, w2k[:, :, sw], w2k[:, :, sw]], axis=2))
    b2k = b2[:, 0]
    b2kp = np.concatenate([b2k, b2k], 1); b2ks = np.concatenate([b2k[:, sw], b2k[:, sw]], 1)
    sh["cmp_b2k"] = f(np.stack([b2kp, b2ks], -1).transpose(1, 0, 2))
    w2v = w2[:, 1]
    sh["cmp_w2v"] = f(np.concatenate([w2v, w2v], axis=2))
    sh["cmp_b2v"] = f(np.concatenate([b2[:, 1], b2[:, 1]], 1))
    sh["b_w_in"] = f(inp["b_w_in"])
    wkv = np.asarray(inp["w_kv_shared"])
    wa = np.zeros((D, 768 + 768 + 128), np.float32)
    wa[:, :1536] = wkv[:, :1536]; wa[:, 1536:1548] = wkv[:, 1536:1548]
    sh["w_kv_aug"] = wa
    sh["b_fgate_bc"] = f(np.broadcast_to(np.asarray(inp["b_fgate"])[None, :], (128, 12)))
    return sh


_CACHE = {}

def run(inputs, cfg, n_cores=8, x_override=None):
    key = repr(sorted(cfg.items()))
    if key not in _CACHE:
        _CACHE[key] = build(cfg)
    nc, K = _CACHE[key]
    sh = prep_shared(inputs)
    x = np.asarray(inputs["x"], dtype=np.float32) if x_override is None else x_override
    mem = np.asarray(inputs["mem"], dtype=np.float32)
    in_maps = []
    for b in range(n_cores):
        m = dict(sh)
        m["xT"] = np.ascontiguousarray(x[b].T)
        m["memT"] = np.ascontiguousarray(mem[b].T)
        in_maps.append(m)
    res = run_bass_kernel_spmd(nc, in_maps, core_ids=list(range(n_cores)))
    global LAST_RES
    LAST_RES = res.results
    return np.stack([np.ascontiguousarray(r["outT"].T) for r in res.results], 0)


def kernel(**inputs):
    return run(inputs, {"layers": (0, 1, 2, 3), "final": True}).astype(np.float32)
```

```python
import numpy as np
from contextlib import ExitStack
import concourse.bass as bass
import concourse.mybir as mybir
from concourse.bass_utils import run_bass_kernel_spmd

F32 = mybir.dt.float32
BF16 = mybir.dt.bfloat16
AF = mybir.ActivationFunctionType
ALU = mybir.AluOpType

D = 1024; S = 2048; DEPTH = 4; NA = 2
DH = 64; NH = 12; NMH = 4; NMEM = 256
DFF = 2816; NCH = 44
TQ = 512; NQB = 4
NCMP = 127
EPS = 1e-6
SCALE = 0.125

def _swap64(cols):
    cols = np.asarray(cols).reshape(-1, 64)
    return np.concatenate([cols[:, 32:], cols[:, :32]], axis=1).reshape(-1)

def nsa_cols():
    q = np.arange(768)
    kv0 = 768
    def kvc(i, g):
        return kv0 + (i * 2 + g) * 64 + np.arange(64)
    fm = []
    qs = _swap64(q)
    for i in range(6):
        fm += [q[128 * i:128 * i + 128], qs[128 * i:128 * i + 128]]
    for i in (2, 4):
        for g in range(2):
            fm += [np.concatenate([kvc(i, g), kvc(i, g)])]
            fm += [np.concatenate([_swap64(kvc(i, g)), _swap64(kvc(i, g))])]
    fm += [np.concatenate([kvc(0, 0), kvc(0, 1)])]
    fm += [np.concatenate([kvc(1, 0), kvc(1, 1)])]
    qm = 768 + 768 + 36 + np.arange(256)
    fm += [qm]
    fmc = np.concatenate(fm)
    gates = 768 + 768 + np.arange(36)
    tm = np.concatenate([kvc(3, 0), kvc(3, 0), kvc(3, 1), kvc(3, 1),
                         kvc(5, 0), kvc(5, 0), kvc(5, 1), kvc(5, 1)])
    return fmc, gates, tm

NSA_FM, NSA_GATE, NSA_TM = nsa_cols()
NSA_NCOL = len(NSA_FM) + 128 + len(NSA_TM)


class Buf:
    __slots__ = ("name", "w", "readers", "excl")
    def __init__(self, name, excl=False):
        self.name = name; self.w = None; self.readers = {}; self.excl = excl


class Ker:
    def __init__(self, nc, stack):
        self.nc = nc
        self.eng = {"pe": nc.tensor, "act": nc.scalar, "dve": nc.vector, "pool": nc.gpsimd, "sp": nc.sync}
        self.sem = {e: stack.enter_context(nc.semaphore("s_" + e)) for e in self.eng}
        self.cnt = {e: 0 for e in self.eng}
        self.seen = {e: {} for e in self.eng}
        self.nds = 8
        self.dsem = {q: [stack.enter_context(nc.semaphore(f"d_{q}{i}")) for i in range(self.nds)]
                     for q in ("sp", "pool")}
        self.dval = {q: [0] * self.nds for q in ("sp", "pool")}
        self.dnext = {"sp": 0, "pool": 0}
        self.nwait = 0
        self.nops = {e: 0 for e in self.eng}
        self.phases = []

    def _need(self, e, tok):
        kind, a, v = tok
        key = (kind, a)
        if self.seen[e].get(key, 0) >= v:
            return
        if kind == "e":
            assert v <= self.cnt[a], f"wait on pending (non-incrementing) instruction of {a}"
            self.eng[e].wait_ge(self.sem[a], v)
        else:
            self.eng[e].wait_ge(self.dsem[a[0]][a[1]], v)
        self.nwait += 1
        self.seen[e][key] = v

    def _deps(self, e, reads, writes):
        for b in reads:
            if b.w is not None:
                if not (b.w[0] == "e" and b.w[1] == e and e == "pe"):
                    self._need(e, b.w)
            if b.excl:
                for (k, a), v in b.readers.items():
                    if k == "e" and a == e:
                        continue
                    self._need(e, (k, a, v))
        for b in writes:
            if b.w is not None and not (b.w[0] == "e" and b.w[1] == e):
                self._need(e, b.w)
            for (k, a), v in b.readers.items():
                if k == "e" and a == e:
                    continue
                self._need(e, (k, a, v))

    def op(self, e, fn, reads=(), writes=(), inc=True):
        self._deps(e, reads, writes)
        ins = fn(self.eng[e])
        self.nops[e] += 1
        if inc:
            ins.then_inc(self.sem[e], 1)
            self.cnt[e] += 1
            c = self.cnt[e]
        else:
            c = self.cnt[e] + 1
        for b in reads:
            k = ("e", e)
            if b.readers.get(k, 0) < c:
                b.readers[k] = c
        for b in writes:
            b.w = ("e", e, c); b.readers = {}
        return ins

    def dma(self, q, out_ap, in_ap, reads=(), writes=()):
        i = self.dnext[q]; self.dnext[q] = (i + 1) % self.nds
        if self.dval[q][i] > 0:
            self._need(q, ("d", (q, i), self.dval[q][i]))
        self._deps(q, reads, writes)
        self.dval[q][i] += 16
        v = self.dval[q][i]
        self.eng[q].dma_start(out=out_ap, in_=in_ap).then_inc(self.dsem[q][i], 16)
        for b in reads:
            b.readers[("d", (q, i))] = v
        for b in writes:
            b.w = ("d", (q, i), v); b.readers = {}

    def barrier(self):
        for e in ("pe", "act", "dve", "pool", "sp"):
            for f in ("pe", "act", "dve", "pool"):
                if f != e and self.cnt[f] > 0:
                    self._need(e, ("e", f, self.cnt[f]))
            for q in ("sp", "pool"):
                for i in range(self.nds):
                    if self.dval[q][i] > 0:
                        self._need(e, ("d", (q, i), self.dval[q][i]))

    def finish(self):
        for q in ("sp", "pool"):
            for i in range(self.nds):
                if self.dval[q][i] > 0:
                    self._need("sp", ("d", (q, i), self.dval[q][i]))


class Rot:
    def __init__(self, items):
        self.items = items; self.i = 0
    def get(self):
        it = self.items[self.i]; self.i = (self.i + 1) % len(self.items)
        return it


def build(cfg):
    layers = cfg.get("layers", [0, 1, 2, 3])
    do_final = cfg.get("final", True)
    mode = cfg.get("mode", "full")
    nc = bass.Bass("TRN2", target_bir_lowering=False)
    st = ExitStack()

    def din(name, shape, dt=F32):
        return nc.dram_tensor(name, list(shape), dt, kind="ExternalInput").ap()

    xT_d = din("xT", [D, S])
    memT_d = din("memT", [D, NMEM])
    gains_d = din("gains", [128, 14, 8])
    wmem_d = din("w_mem_kv", [DEPTH, D, 512])
    wo_d = din("w_o", [DEPTH, D, D])
    wup_d = din("w_up", [DEPTH, D, 2 * DFF])
    wdn_d = din("w_down", [DEPTH, DFF, D])
    cw_d = din("convw", [128, DEPTH, 3, NCH])
    cb_d = din("convb", [128, DEPTH, NCH])
    awin_d = din("a_w_aug", [NA, D, NSA_NCOL])
    agb_d = din("a_gate_b", [36, NA])
    cw1_d = din("cmp_w1", [NA, 2, 128, 32, 256])
    cpos_d = din("cmp_pos", [128, NA, 2, 32])
    cb1_d = din("cmp_b1", [128, NA, 2, 2])
    cw2k_d = din("cmp_w2k", [NA, 256, 256])
    cb2k_d = din("cmp_b2k", [128, NA, 2])
    cw2v_d = din("cmp_w2v", [NA, 256, 128])
    cb2v_d = din("cmp_b2v", [NA, 128])
    bwin_d = din("b_w_in", [NA, D, D])
    wkv_d = din("w_kv_aug", [D, 768 + 768 + 128])
    bfg_d = din("b_fgate_bc", [128, 12])
    cos_d = din("ropecos", [128, S]); sin_d = din("ropesin", [128, S])
    cosc_d = din("ropecosc", [128, 128]); sinc_d = din("ropesinc", [128, 128])
    ident_d = din("ident", [128, 128])
    tri_d = din("tri", [128, 128]); tric_d = din("tric", [128, 128])
    cmpmask_d = din("cmpmask", [128, S])
    overlap_d = din("overlap", [128, 32])
    ov33_d = din("overlap33", [128, 33])
    expand_d = din("expand", [32, 16, 128])
    bonus_d = din("bonus", [128, 16, 32])
    gsel_d = din("rowidx", [36, 128])
    sel127_d = din("sel127", [128, 128])
    out_d = nc.dram_tensor("outT", [D, S], F32, kind="ExternalOutput").ap()
    dumps = {}

    K = Ker(nc, st)

    def sb(name, shape, dt):
        return st.enter_context(nc.sbuf_tensor("sb_" + name, list(shape), dt))

    def ps(name, shape=(128, 512), dt=F32):
        return st.enter_context(nc.psum_tensor("ps_" + name, list(shape), dt))

    xT = sb("xT", [128, 8, S], F32)
    xB = [[Buf(f"x{k}_{c}") for c in range(NQB)] for k in range(8)]
    hT = sb("hT", [128, 8, TQ], BF16); hB = [Buf(f"hT{k}") for k in range(8)]
    rstd = sb("rstd", [128, TQ], F32); rstdB = Buf("rstd")
    gains = sb("gains", [128, 14, 8], F32); gB = Buf("gains")
    NW = 4
    wt = [sb(f"wt{i}", [128, 2048], BF16) for i in range(NW)]
    wrot = Rot([(wt[i], Buf(f"wt{i}")) for i in range(NW)])
    c_ones = sb("c_ones", [128, 128], BF16)
    c_onesm = sb("c_onesm", [128, 128], BF16)
    c_ones32 = sb("c_ones32", [128, 128], F32)
    c_eps = sb("c_eps", [128, 1], F32)
    c_tri = sb("c_tri", [128, 128], BF16); c_tric = sb("c_tric", [128, 128], BF16)
    c_tri32 = sb("c_tri32", [128, 128], F32)
    c_ident = sb("c_ident", [128, 128], F32)
    c_sel127 = sb("c_sel127", [128, 128], F32)
    cB = Buf("consts")
    convw = sb("convw", [128, DEPTH, 3, NCH], F32); convb = sb("convb", [128, DEPTH, NCH], F32)
    qT = sb("qT", [128, 6, TQ], BF16); qB = Buf("qT")
    qmT = sb("qmT", [128, 2, TQ], BF16); qmB = Buf("qmT")
    oT = sb("oT", [128, 8, TQ], BF16); oB = Buf("oT")
    gated = sb("gated", [128, 11, TQ], BF16); gatedB = Buf("gated")
    sq = gated; sqB = gatedB
    ubuf = [sb(f"ubuf{i}", [128, TQ + 2], F32) for i in range(2)]
    ubB = [Buf(f"ubuf{i}") for i in range(2)]
    halo = sb("halo", [128, NCH, 2], F32); haloB = [Buf(f"halo{i}") for i in range(NCH)]
    sil = rstd; silB = rstdB
    E_t = [sb(f"E{i}", [128, TQ], BF16) for i in range(4)]
    Erot = Rot([(E_t[i], Buf(f"E{i}")) for i in range(4)])
    den_sb = [sb(f"den{i}", [128, TQ], F32) for i in range(2)]
    denrot = Rot([(den_sb[i], Buf(f"den{i}")) for i in range(2)])
    den_sb_items = denrot.items
    oacc1 = sb("oacc1", [128, TQ], F32); oaccB = Buf("oacc")
    tmpf = [sb(f"tmpf{i}", [128, TQ], F32) for i in range(2)]
    tmprot = Rot([(tmpf[i], Buf(f"tmpf{i}")) for i in range(2)])
    cacc = tmpf; caccB = [tmprot.items[i][1] for i in range(2)]
    mhT = oT; mhB = oB
    kmT = sb("kmT", [128, 2, NMEM], BF16); kmB = Buf("kmT")
    vm = sb("vm", [128, 2, 256], BF16); vmB = Buf("vm")
    KVBYTES = 48 * 1024
    kvraw = sb("kvraw", [128, KVBYTES // 2], BF16)
    fkT = kvraw[:, 0:6 * S].rearrange("p (k t) -> p k t", k=6); fkB = [Buf(f"fk{c}") for c in range(NQB)]
    fV = kvraw[:, 6 * S:12 * S].rearrange("p (j n) -> p j n", j=16); fVB = [Buf(f"fv{c}") for c in range(NQB)]
    dcum = sb("dcum", [128, 16, 12], F32); dcumB = [Buf(f"dcum{j}") for j in range(16)]
    logf = sb("logf", [128, 16, 12], F32); logfB = [Buf(f"logf{j}") for j in range(16)]
    fbias = logf
    dref = sb("dref", [128, 12], F32); drefB = Buf("dref")
    bfg = sb("bfg", [128, 12], F32)
    o = 0
    def carve(n):
        nonlocal o
        v = kvraw[:, o:o + n]; o += n
        return v
    kslcT = carve(2 * S).rearrange("p (g t) -> p g t", g=2); kwinT = carve(2 * S).rearrange("p (g t) -> p g t", g=2)
    vslc = carve(16 * 256).rearrange("p (j n) -> p j n", j=16); vwin = carve(16 * 256).rearrange("p (j n) -> p j n", j=16)
    ucmp = carve(2 * S).rearrange("p (k t) -> p k t", k=2)
    cmpmask = carve(TQ)
    kcT = carve(2 * 128).rearrange("p (g n) -> p g n", g=2)
    vc = carve(2 * 128).rearrange("p (g n) -> p g n", g=2)
    selT = carve(2 * TQ).rearrange("p (g t) -> p g t", g=2)
    expand = carve(16 * 128).rearrange("p (j s) -> p j s", j=16)
    assert o * 2 <= KVBYTES
    nsaKB = [Buf(f"nsak{c}") for c in range(NQB)]
    cmpB = Buf("cmp"); selB = Buf("selT"); hidB = Buf("hid"); cmB = Buf("cmpmask")
    overlap = sb("overlap", [128, 32], F32)
    ov33 = sb("ov33", [128, 33], BF16)
    r4 = sb("r4", [128, 4, 1], F32); r4B = Buf("r4")
    imp_acc4 = sb("imp_acc4", [128, 4, 32], F32); impaB = Buf("imp_acc4")
    bonus = sb("bonus", [128, 4, 32], F32); bonusB = Buf("bonus")
    e36 = sb("e36", [36, TQ], F32); e36B = Buf("e36")
    e_hi = sb("e_hi", [36, TQ], BF16); e_lo = sb("e_lo", [36, TQ], BF16); ehlB = Buf("ehl")
    e_hf = sb("e_hf", [36, TQ], F32); ehfB = Buf("ehf")
    agb = sb("agb", [36, NA], F32)
    rowidx = sb("rowidx", [36, 128], F32)
    selh = [sb(f"selh{i}", [36, 128], BF16) for i in range(2)]
    selhrot = Rot([(selh[i], Buf(f"selh{i}")) for i in range(2)])
    ropeB = Buf("rope")
    cosc = sb("cosc", [128, 128], F32); sinc = sb("sinc", [128, 128], F32)
    cpos = sb("cpos", [128, NA, 2, 32], BF16); cb1 = sb("cb1", [128, NA, 2, 2], F32)
    cb2k = sb("cb2k", [128, NA, 2], F32); cb2v = sb("cb2v", [1, NA, 128], BF16)
    hb = sb("hb", [128, 8], F32); hbB = Buf("hb")
    g_x = [sb(f"g_x{i}", [128, 128], F32) for i in range(4)]; g_xB = [Buf(f"g_x{i}") for i in range(4)]
    imp_s4 = sb("imp_s4", [128, 4, 32], F32); impB = Buf("imp_s")
    sc2 = sb("sc2", [128, 32], F32); sc2B = Buf("sc2")
    mx8 = sb("mx8", [128, 16], F32); mx8B = Buf("mx8")
    sel_s = sb("sel_s", [128, 32], F32); selsB = Buf("sel_s")

    pA = Rot([(ps(f"pA{i}"), Buf(f"pA{i}", True)) for i in range(2)])
    pA2 = Rot([pA.items[0]])
    pS = Rot([(ps(f"pS{i}"), Buf(f"pS{i}", True)) for i in range(2)] + [pA.items[1]])
    pN = Rot([(ps(f"pN{i}"), Buf(f"pN{i}", True)) for i in range(2)])
    pD = Rot([(ps(f"pD{i}"), Buf(f"pD{i}", True)) for i in range(2)])

    def mm(out, lhsT, rhs, start, stop, reads, writes, inc=True):
        return K.op("pe", lambda e: e.matmul(out, lhsT, rhs, start=start, stop=stop), reads, writes, inc)

    def act(out, in_, func, reads, writes, bias=0.0, scale=1.0):
        return K.op("act", lambda e: e.activation(out, in_, func, bias=bias, scale=scale), reads, writes)

    def tt(eng, out, in0, in1, op, reads, writes):
        return K.op(eng, lambda e: e.tensor_tensor(out, in0, in1, op), reads, writes)

    def ts(eng, out, in0, s1, s2, op0, op1, reads, writes):
        if op1 is None:
            return K.op(eng, lambda e: e.tensor_scalar(out, in0, s1, None, op0), reads, writes)
        return K.op(eng, lambda e: e.tensor_scalar(out, in0, s1, s2, op0, op1), reads, writes)

    def stt(eng, out, in0, scalar, in1, op0, op1, reads, writes):
        return K.op(eng, lambda e: e.scalar_tensor_tensor(out, in0, scalar, in1, op0, op1), reads, writes)

    def cp(eng, out, in_, reads, writes):
        return K.op(eng, lambda e: e.tensor_copy(out, in_), reads, writes)

    def _issue256(w_ap, col0, ncols):
        t, tb = wrot.get()
        wv = t[:, 0:8 * 256].rearrange("p (k n) -> p k n", k=8)
        K.dma("pool", wv[:, :, :ncols], w_ap.rearrange("(k p) n -> p k n", p=128)[:, :, col0:col0 + ncols],
              writes=(tb,))
        return wv, tb

    plan_q = []; issued_q = []
    AHEAD = 3

    def plan(specs):
        assert not plan_q and not issued_q
        plan_q.extend(specs)
        while plan_q and len(issued_q) < AHEAD:
            sp = plan_q.pop(0); issued_q.append((sp, _issue256(*sp)))

    def load256(w_ap, col0, ncols=256):
        if not issued_q and not plan_q:
            return _issue256(w_ap, col0, ncols)
        while plan_q and len(issued_q) < 1 + AHEAD:
            sp = plan_q.pop(0); issued_q.append((sp, _issue256(*sp)))
        sp, tile = issued_q.pop(0)
        assert sp[1] == col0 and sp[2] == ncols, (sp[1:], col0, ncols)
        return tile

    def wstream(loads, ahead):
        issued = []
        def get(i):
            while len(issued) < min(len(loads), i + 1 + ahead):
                issued.append(loads[len(issued)]())
            return issued[i]
        return get

    def load_w(dram_ap, view):
        t, b = wrot.get()
        K.dma("pool", view(t), dram_ap, reads=(), writes=(b,))
        return t, b

    K.dma("sp", gains[:], gains_d, writes=(gB,))
    K.dma("sp", convw[:], cw_d, writes=(cB,)); K.dma("sp", convb[:], cb_d, writes=(cB,))
    K.dma("sp", c_ident[:], ident_d, writes=(cB,))
    K.dma("sp", c_tri32[:], tri_d, writes=(cB,))
    K.dma("sp", c_sel127[:], sel127_d, writes=(cB,))
    K.dma("pool", c_tri[:], tri_d, writes=(cB,)); K.dma("pool", c_tric[:], tric_d, writes=(cB,))
    K.dma("sp", bfg[:], bfg_d, writes=(cB,))
    K.dma("sp", overlap[:], overlap_d, writes=(cB,)); K.dma("pool", ov33[:], ov33_d, writes=(cB,))
    K.dma("sp", rowidx[:], gsel_d, writes=(cB,)); K.dma("sp", agb[:], agb_d, writes=(cB,))
    K.op("dve", lambda e: e.tensor_scalar(agb[:], agb[:], -1.0, None, ALU.mult), reads=(cB,), writes=(cB,))
    K.dma("sp", cosc[:], cosc_d, writes=(cB,)); K.dma("sp", sinc[:], sinc_d, writes=(cB,))
    K.dma("pool", cpos[:], cpos_d, writes=(cB,)); K.dma("sp", cb1[:], cb1_d, writes=(cB,))
    K.dma("sp", cb2k[:], cb2k_d, writes=(cB,))
    K.dma("pool", cb2v[:], cb2v_d.rearrange("(o l) n -> o l n", o=1), writes=(cB,))
    K.op("dve", lambda e: e.memset(c_ones[:], 1.0), writes=(cB,))
    K.op("dve", lambda e: e.memset(c_onesm[:], 1.0 / 1024.0), writes=(cB,))
    K.op("dve", lambda e: e.memset(c_ones32[:], 1.0), writes=(cB,))
    K.op("dve", lambda e: e.memset(c_eps[:], EPS), writes=(cB,))
    K.op("dve", lambda e: e.memset(halo[:], 0.0), writes=tuple(haloB))
    for k in range(8):
        K.dma("sp", xT[:, k, :], xT_d[k * 128:(k + 1) * 128, :], writes=tuple(xB[k]))
    K.barrier()

    def rmsnorm_block(src, srcB, gidx, ncols, dst, dstB, col0=0, src_list=None):
        sl = (lambda k: src_list[k]) if src_list is not None else (lambda k: src[:, k, col0:col0 + ncols])
        for k in range(8):
            K.op("act", lambda e, k=k: e.activation(sq[:, k, :ncols], sl(k), AF.Square),
                 reads=(srcB[k],), writes=(sqB,))
        pt, pb = pA.get()
        for k in range(8):
            mm(pt[:, :ncols], c_onesm[:], sq[:, k, :ncols], k == 0, k == 7, (sqB, cB), (pb,), inc=(k == 7))
        act(rstd[:, :ncols], pt[:, :ncols], AF.Ln, (pb, cB), (rstdB,), bias=c_eps[:, 0:1])
        act(rstd[:, :ncols], rstd[:, :ncols], AF.Exp, (rstdB,), (rstdB,), scale=-0.5)
        for k in range(8):
            stt("dve", dst[:, k, :ncols], sl(k),
                gains[:, gidx, k:k + 1], rstd[:, :ncols], ALU.mult, ALU.mult, (srcB[k], rstdB, gB),
                (dstB[k] if isinstance(dstB, list) else dstB,))

    def proj_fm(wtile, wb, wcol0, ncol, rhsT, rhsB, ncols_tok):
        pt, pb = pA.get()
        for k in range(8):
            mm(pt[:ncol, :ncols_tok], wtile[:, k, wcol0:wcol0 + ncol], rhsT[:, k, :ncols_tok],
               k == 0, k == 7, (wb, rhsB[k] if isinstance(rhsB, list) else rhsB), (pb,), inc=(k == 7))
        return pt, pb

    def ffn_block(l, c):
        xcB = [xB[k][c] for k in range(8)]
        rmsnorm_block(xT, xcB, 4 + l, TQ, hT, hB, col0=c * TQ)
        srcu = wup_d[l].rearrange("(k p) f -> p k f", p=128)
        srcd = wdn_d[l].rearrange("(i p) n -> p i n", p=128)
        loads = []
        def mk_up(c0, npair):
            def f():
                t, tb = wrot.get()
                wv = t[:, 0:8 * 256].rearrange("p (k n) -> p k n", k=8)
                K.dma("pool", wv[:, :, 0:128 * npair], srcu[:, :, c0:c0 + 128 * npair], writes=(tb,))
                return wv, tb
            return f
        def mk_dn(half, f0, nf, nn):
            def f():
                t, tb = wrot.get()
                wv = t[:, 0:nf * 256].rearrange("p (i n) -> p i n", i=nf)
                K.dma("pool", wv, srcd[:, half * 11 + f0:half * 11 + f0 + nf, nn * 256:(nn + 1) * 256], writes=(tb,))
                return wv, tb
            return f
        for half in range(2):
            for pi in range(0, 11, 2):
                npair = min(2, 11 - pi)
                for ab in range(2):
                    loads.append(mk_up(ab * DFF + (half * 11 + pi) * 128, npair))
            for nn in range(4):
                for (f0, nf) in ((0, 6), (6, 5)):
                    loads.append(mk_dn(half, f0, nf, nn))
        wget = wstream(loads, 2)
        li = 0
        for half in range(2):
            for pi in range(0, 11, 2):
                npair = min(2, 11 - pi)
                i0 = half * 11 + pi
                tiles = [wget(li), wget(li + 1)]; li += 2
                for j in range(npair):
                    i = i0 + j
                    accs = []
                    par = (pi + j) % 2
                    for ab in range(2):
                        ch = i + 22 * ab
                        wv, tb = tiles[ab]
                        pt, pb = proj_fm(wv, tb, j * 128, 128, hT, hB, TQ)
                        ui = ab
                        ub, ubb = ubuf[ui], ubB[ui]
                        if c > 0:
                            cp("pool", ub[:, 0:2], halo[:, ch, :], (haloB[ch],), (ubb,))
                        else:
                            K.op("pool", lambda e, ub=ub: e.memset(ub[:, 0:2], 0.0), writes=(ubb,))
                        act(ub[:, 2:TQ + 2], pt[:, :], AF.Copy, (pb,), (ubb,))
                        cp("pool", halo[:, ch, :], ub[:, TQ:TQ + 2], (ubb,), (haloB[ch],))
                        ca, cab = (cacc[ab], caccB[ab]) if par == 0 else den_sb_items[ab]
                        eng = "dve"
                        K.op("act", lambda e, ca=ca, pt=pt, ch=ch: e.activation(
                            ca[:], pt[:, :], AF.Identity, bias=convb[:, l, ch:ch + 1], scale=convw[:, l, 2, ch:ch + 1]),
                            reads=(pb, cB), writes=(cab,))
                        stt(eng, ca[:], ub[:, 1:TQ + 1], convw[:, l, 1, ch:ch + 1], ca[:], ALU.mult, ALU.add,
                            (ubb, cB, cab), (cab,))
                        stt(eng, ca[:], ub[:, 0:TQ], convw[:, l, 0, ch:ch + 1], ca[:], ALU.mult, ALU.add,
                            (ubb, cB, cab), (cab,))
                        accs.append((ca, cab))
                    sl_t, sl_b = (sil, silB) if par == 0 else (oacc1, oaccB)
                    act(sl_t[:], accs[0][0][:], AF.Silu, (accs[0][1],), (sl_b,))
                    tt("dve", gated[:, pi + j, :], sl_t[:], accs[1][0][:], ALU.mult, (sl_b, accs[1][1]), (gatedB,))
            for nn in range(4):
                tiles = [wget(li), wget(li + 1)]; li += 2
                for n2 in range(2):
                    n = nn * 2 + n2
                    pt, pb = pA.get()
                    for i in range(11):
                        wv, tb = tiles[0] if i < 6 else tiles[1]
                        ii = i if i < 6 else i - 6
                        mm(pt[:, :], wv[:, ii, n2 * 128:(n2 + 1) * 128], gated[:, i, :], i == 0, i == 10,
                           (tb, gatedB), (pb,), inc=(i == 10))
                    tt("dve", xT[:, n, c * TQ:(c + 1) * TQ], xT[:, n, c * TQ:(c + 1) * TQ], pt[:, :], ALU.add,
                       (pb, xB[n][c]), (xB[n][c],))

    def mem_kv(l):
        hold = [(tmpf[0], tmprot.items[0][1]), (tmpf[1], tmprot.items[1][1]), den_sb_items[0], den_sb_items[1]]
        srcs = []; srcBs = []
        for k in range(8):
            t_, b_ = hold[k // 2]
            ap_ = t_[:, (k % 2) * 256:(k % 2) * 256 + 256]
            K.dma("sp", ap_, memT_d[k * 128:(k + 1) * 128, :], writes=(b_,))
            srcs.append(ap_); srcBs.append(b_)
        plan([(wmem_d[l], 0, 256), (wmem_d[l], 256, 256)])
        rmsnorm_block(None, srcBs, 8 + l, NMEM, mhT, mhB, src_list=srcs)
        wv, tb = load256(wmem_d[l], 0)
        for ch in range(2):
            pt, pb = proj_fm(wv, tb, ch * 128, 128, mhT, mhB, NMEM)
            act(kmT[:, ch, :], pt[:, :NMEM], AF.Copy, (pb,), (kmB,))
        wv, tb = load256(wmem_d[l], 256)
        for mt in range(2):
            pt, pb = pA.get()
            for k in range(8):
                mm(pt[:, :256], mhT[:, k, mt * 128:(mt + 1) * 128], wv[:, k, 0:256], k == 0, k == 7,
                   (tb, mhB), (pb,), inc=(k == 7))
            act(vm[:, mt, :], pt[:, :256], AF.Copy, (pb,), (vmB,))

    class Pipe:
        def __init__(self, depth=1):
            self.depth = depth; self.pending = []
        def push(self, A, B):
            r = A()
            self.pending.append((B, r))
            while len(self.pending) > self.depth:
                b, rr = self.pending.pop(0); b(rr)
        def flush(self):
            while self.pending:
                b, rr = self.pending.pop(0); b(rr)

    pipe = Pipe(2)

    def attn_tile(kT_ap, kreads, q_ap, qreads, ncol, bias, post, V_ap, vreads, pn_t, pn_b, pd_t, pd_b,
                  first, last, col0, after=None, krows=128, extra=None, preB=None, preB_late=None):
        def A():
            st_t, st_b = pS.get()
            mm(st_t[:krows, :ncol], kT_ap, q_ap, True, extra is None, kreads + qreads, (st_b,))
            if extra is not None:
                mm(st_t[:krows, :ncol], extra[0], extra[1], False, True, extra[2], (st_b,))
            e_t, e_b = Erot.get()
            K.op("act", lambda e: e.activation(e_t[:krows, :ncol], st_t[:krows, :ncol], AF.Exp, bias=bias[0],
                                               scale=SCALE),
                 reads=(st_b,) + bias[1], writes=(e_b,))
            if post is not None:
                post(e_t, e_b)
            if preB is not None:
                preB()
            return e_t, e_b
        def B(r):
            e_t, e_b = r
            if preB_late is not None:
                preB_late()
            mm(pn_t[:, col0:col0 + ncol], V_ap, e_t[:krows, :ncol], first, last, vreads + (e_b,), (pn_b,))
            mm(pd_t[:, col0:col0 + ncol], c_ones[:krows, :], e_t[:krows, :ncol], first, last, (e_b, cB), (pd_b,))
            if after is not None:
                after()
        pipe.push(A, B)

    def mask_sub(mask_ap):
        def post(e_t, e_b):
            tt("pool", e_t[:, 0:128], e_t[:, 0:128], mask_ap, ALU.mult, (e_b, cB), (e_b,))
        return post

    def causal_attention(kT_fn, V_fn, q_ap_fn, c, bias_fn, pr, njt=None, sel_fn=None, after_fn=None, preB=None):
        pn_t, pn_b = pN.get(); pd_t, pd_b = pD.get()
        tiles = list(range(4 * c + 4))
        nt = len(tiles)
        order = [4 * c] + list(range(4 * c)) + [4 * c + 1, 4 * c + 2, 4 * c + 3]
        for idx, j in enumerate(order):
            i = j - 4 * c
            if i < 0:
                col0, ncol, post = 0, TQ, None
            else:
                col0, ncol = 128 * i, TQ - 128 * i
                post = mask_sub(c_tri[:, :])
            extra = None
            if sel_fn is not None:
                post, extra = sel_fn(j, col0, ncol, post)
            kap, kr = kT_fn(j); vap, vr = V_fn(j)
            qap, qr = q_ap_fn(col0, ncol)
            aft = None
            if idx == nt - 1 and after_fn is not None:
                aft = (lambda: after_fn(pn_t, pn_b, pd_t, pd_b))
            attn_tile(kap, kr, qap, qr, ncol, bias_fn(j), post, vap, vr, pn_t, pn_b, pd_t, pd_b,
                      idx == 0, idx == nt - 1, col0, after=aft, extra=extra,
                      preB=(preB if idx == nt - 1 else None))
        return pn_t, pn_b, pd_t, pd_b

    def finish_head_plain(pn_t, pn_b, pd_t, pd_b, pr, dst_ap):
        d_t, d_b = denrot.get()
        act(d_t[pr, :], pd_t[pr, :], AF.Ln, (pd_b,), (d_b,))
        act(d_t[pr, :], d_t[pr, :], AF.Exp, (d_b,), (d_b,), scale=-1.0)
        tt("dve", dst_ap, pn_t[pr, :], d_t[pr, :], ALU.mult, (pn_b, d_b), (oB,))

    def mem_attention(c):
        for hm in range(4):
            ch, off = hm // 2, 64 * (hm % 2)
            pr = slice(off, off + 64)
            pn_t, pn_b = pN.get(); pd_t, pd_b = pD.get()
            for mt in range(2):
                aft = None
                if mt == 1:
                    aft = (lambda pn_t=pn_t, pn_b=pn_b, pd_t=pd_t, pd_b=pd_b, pr=pr, ch=ch:
                           finish_head_plain(pn_t, pn_b, pd_t, pd_b, pr, oT[pr, 6 + ch, :]))
                attn_tile(kmT[pr, ch, mt * 128:(mt + 1) * 128], (kmB,), qmT[pr, ch, :], (qmB,), TQ, (0.0, ()),
                          None, vm[:, mt, ch * 128:(ch + 1) * 128], (vmB,), pn_t, pn_b, pd_t, pd_b,
                          mt == 0, mt == 1, 0, after=aft)
        pipe.flush()

    def wo_block(l, c):
        plan([(wo_d[l], nn * 256, 256) for nn in range(4)])
        for nn in range(4):
            wv, tb = load256(wo_d[l], nn * 256)
            for n2 in range(2):
                n = nn * 2 + n2
                pt, pb = proj_fm(wv, tb, n2 * 128, 128, oT, oB, TQ)
                tt("dve", xT[:, n, c * TQ:(c + 1) * TQ], xT[:, n, c * TQ:(c + 1) * TQ], pt[:, :], ALU.add,
                   (pb, xB[n][c]), (xB[n][c],))

    def fox_shared_kv(c):
        plan([(wkv_d, cc * 256, 256) for cc in range(3)]
             + [(wkv_d, 768 + cc * 256, 256 if cc < 3 else 128) for cc in range(4)])
        xcB = [xB[k][c] for k in range(8)]
        rmsnorm_block(xT, xcB, 12, TQ, hT, hB, col0=c * TQ)
        for cc in range(3):
            wv, tb = load256(wkv_d, cc * 256)
            for j in range(2):
                ch = cc * 2 + j
                pt, pb = proj_fm(wv, tb, j * 128, 128, hT, hB, TQ)
                act(fkT[:, ch, c * TQ:(c + 1) * TQ], pt[:, :], AF.Copy, (pb,), (fkB[c],))
        for cc in range(4):
            ncols = 256 if cc < 3 else 128
            wv, tb = load256(wkv_d, 768 + cc * 256, ncols)
            for jt in range(4):
                j = 4 * c + jt
                pt, pb = pA.get()
                for k in range(8):
                    mm(pt[:, :ncols], hT[:, k, jt * 128:(jt + 1) * 128], wv[:, k, :ncols], k == 0, k == 7,
                       (tb, hB[k]), (pb,), inc=(k == 7))
                if cc < 3:
                    act(fV[:, j, cc * 256:(cc + 1) * 256], pt[:, :256], AF.Copy, (pb,), (fVB[c],))
                else:
                    tt("dve", logf[:, j, :], pt[:, 0:12], bfg[:], ALU.add, (pb, cB), (logfB[j],))
                    if cfg.get("dbg", 0) == 3:
                        continue
                    act(logf[:, j, :], logf[:, j, :], AF.Exp, (logfB[j],), (logfB[j],), scale=-1.0)
                    if cfg.get("dbg", 0) == 4:
                        continue
                    act(logf[:, j, :], logf[:, j, :], AF.Ln, (logfB[j], cB), (logfB[j],), bias=c_ones32[:, 0:1])
                    if cfg.get("dbg", 0) == 5:
                        continue
                    ts("dve", logf[:, j, :], logf[:, j, :], -1.0, None, ALU.mult, None, (logfB[j],), (logfB[j],))
        for jt in range(4):
            if cfg.get("dbg", 0) == 1:
                break
            j = 4 * c + jt
            pt, pb = pA.get()
            for jj in range(j + 1):
                lhs = c_tri32[:] if jj == j else c_ones32[:]
                mm(pt[:, :12], lhs, logf[:, jj, :], jj == 0, jj == j, (cB, logfB[jj]), (pb,), inc=(jj == j))
            act(dcum[:, j, :], pt[:, :12], AF.Copy, (pb,), (dcumB[j],))

    def fox_attention(l, c):
        pt, pb = pA2.get()
        mm(pt[:, :12], c_sel127[:], dcum[:, 4 * c + 1, :], True, True, (cB, dcumB[4 * c + 1]), (pb,))
        act(dref[:], pt[:, :12], AF.Copy, (pb,), (drefB,))
        for j in range(4 * c + 4):
            tt("pool", fbias[:, j, :], dref[:], dcum[:, j, :], ALU.subtract, (drefB, dcumB[j]), (logfB[j],))
        for h in range(NH):
            ch, off = h // 2, 64 * (h % 2)
            pr = slice(off, off + 64)
            causal_attention(
                lambda j, pr=pr, ch=ch: (fkT[pr, ch, j * 128:(j + 1) * 128], (fkB[j // 4],)),
                lambda j, ch=ch: (fV[:, j, ch * 128:(ch + 1) * 128], (fVB[j // 4],)),
                lambda col0, ncol, pr=pr, ch=ch: (qT[pr, ch, col0:col0 + ncol], (qB,)),
                c, lambda j, h=h: (fbias[:, j, h:h + 1], (logfB[j],)), pr,
                after_fn=(lambda a, b, c_, d, pr=pr, ch=ch: finish_head_plain(a, b, c_, d, pr, oT[pr, ch, :])))
        pipe.flush()

    def fox_q_proj(l, c):
        plan([(bwin_d[l - NA], cc * 256, 256) for cc in range(4)])
        xcB = [xB[k][c] for k in range(8)]
        rmsnorm_block(xT, xcB, l, TQ, hT, hB, col0=c * TQ)
        for cc in range(4):
            wv, tb = load256(bwin_d[l - NA], cc * 256)
            for j in range(2):
                ch = cc * 2 + j
                pt, pb = proj_fm(wv, tb, j * 128, 128, hT, hB, TQ)
                if ch < 6:
                    act(qT[:, ch, :], pt[:, :], AF.Copy, (pb,), (qB,))
                else:
                    act(qmT[:, ch - 6, :], pt[:, :], AF.Copy, (pb,), (qmB,))

    def rope_evac(pt, pb, pts, pbs, dst_ap, dstB, cs, sn, rB, ncols):
        t1, t1b = tmprot.get()
        tt("dve", t1[:, :ncols], pt[:, :ncols], cs, ALU.mult, (pb,) + rB, (t1b,))
        t2, t2b = tmprot.get()
        tt("dve", t2[:, :ncols], pts[:, :ncols], sn, ALU.mult, (pbs,) + rB, (t2b,))
        tt("pool", dst_ap, t1[:, :ncols], t2[:, :ncols], ALU.add, (t1b, t2b), (dstB,))

    def nsa_load_w(l, col0, ncols):
        return load256(awin_d[l], col0, ncols)

    def nsa_rope_tables(c):
        rc, rcb = denrot.items[0]; rs, rsb = denrot.items[1]
        K.dma("sp", rc[:], cos_d[:, c * TQ:(c + 1) * TQ], writes=(rcb,))
        K.dma("sp", rs[:], sin_d[:, c * TQ:(c + 1) * TQ], writes=(rsb,))
        return rc, rs, (rcb, rsb)

    def nsa_kv_proj(l, c):
        plan([(awin_d[l], (6 + 2 * bi + g) * 256, 256) for bi in range(2) for g in range(2)]
             + [(awin_d[l], 20 * 128, 256)] + [(awin_d[l], 3072 + 128 + vi * 256, 256) for vi in range(2)])
        xcB = [xB[k][c] for k in range(8)]
        rmsnorm_block(xT, xcB, l, TQ, hT, hB, col0=c * TQ)
        rc, rs, rB = nsa_rope_tables(c)
        tsl = slice(c * TQ, (c + 1) * TQ)
        for bi, dstT in ((0, kslcT), (1, kwinT)):
            for g in range(2):
                wv, tb = nsa_load_w(l, (6 + 2 * bi + g) * 256, 256)
                pt, pb = proj_fm(wv, tb, 0, 128, hT, hB, TQ)
                pts, pbs = proj_fm(wv, tb, 128, 128, hT, hB, TQ)
                rope_evac(pt, pb, pts, pbs, dstT[:, g, tsl], nsaKB[c], rc[:], rs[:], rB, TQ)
        wv, tb = nsa_load_w(l, 20 * 128, 256)
        for kv in range(2):
            pt, pb = proj_fm(wv, tb, kv * 128, 128, hT, hB, TQ)
            act(ucmp[:, kv, tsl], pt[:, :], AF.Copy, (pb,), (nsaKB[c],))
        for vi, vdst in ((0, vslc), (1, vwin)):
            wv, tb = nsa_load_w(l, 3072 + 128 + vi * 256, 256)
            for jt in range(4):
                j = 4 * c + jt
                pt, pb = pA.get()
                for k in range(8):
                    mm(pt[:, :256], hT[:, k, jt * 128:(jt + 1) * 128], wv[:, k, :], k == 0, k == 7, (tb, hB[k]), (pb,),
                       inc=(k == 7))
                act(vdst[:, j, :], pt[:, 0:256], AF.Copy, (pb,), (nsaKB[c],))

    def gelu_tanh(dst_ap, dstB, pt, pb, bias_ap, npart, ncols):
        x, xb = g_x[0], g_xB[0]; x2, x2b = g_x[1], g_xB[1]; th, thb = g_x[2], g_xB[2]
        act(x[:npart, :ncols], pt[:npart, :ncols], AF.Identity, (pb, cB), (xb,), bias=bias_ap)
        tt("dve", x2[:npart, :ncols], x[:npart, :ncols], x[:npart, :ncols], ALU.mult, (xb,), (x2b,))
        ts("dve", x2[:npart, :ncols], x2[:npart, :ncols], 0.044715, 1.0, ALU.mult, ALU.add, (x2b,), (x2b,))
        tt("dve", x2[:npart, :ncols], x2[:npart, :ncols], x[:npart, :ncols], ALU.mult, (x2b, xb), (x2b,))
        act(th[:npart, :ncols], x2[:npart, :ncols], AF.Tanh, (x2b,), (thb,), scale=0.7978845608028654)
        ts("dve", th[:npart, :ncols], th[:npart, :ncols], 1.0, 0.5, ALU.add, ALU.mult, (thb,), (thb,))
        tt("dve", dst_ap, th[:npart, :ncols], x[:npart, :ncols], ALU.mult, (thb, xb), (dstB,))

    def nsa_compress(l):
        allk = tuple(nsaKB)
        K.op("dve", lambda e: e.memset(expand, 0.0), writes=(cmpB,))
        K.op("dve", lambda e: e.memset(selT, 0.0), writes=(selB,))
        K.dma("pool", expand[0:32], expand_d, writes=(cmpB,))
        for kv in range(2):
            halves = []
            for hf in range(4):
                t, tb = wrot.get()
                wv = t[:, 0:8 * 256].rearrange("p (l n) -> p l n", l=8)
                K.dma("pool", wv, cw1_d[l, kv][:, hf * 8:(hf + 1) * 8, :], writes=(tb,))
                halves.append((wv, tb))
            for hc in range(2):
                pt, pb = pA.get()
                for li in range(32):
                    wv, tb = halves[li // 8]
                    mm(pt[:, 0:1], wv[0:64, li % 8, hc * 128:(hc + 1) * 128], cpos[0:64, l, kv, li:li + 1],
                       li == 0, li == 31, (tb, cB), (pb,), inc=(li == 31))
                tt("dve", hb[:, hc:hc + 1], pt[:, 0:1], cb1[:, l, kv, hc:hc + 1], ALU.add, (pb, cB), (hbB,))
                for g in range(2):
                    pr = slice(64 * g, 64 * g + 64)
                    pt2, pb2 = pA.get()
                    for li in range(32):
                        wv, tb = halves[li // 8]
                        rhs = ucmp[pr, kv, li:li + 16 * (NCMP - 1) + 1:16]
                        mm(pt2[:, :NCMP], wv[pr, li % 8, hc * 128:(hc + 1) * 128], rhs, li == 0, li == 31,
                           (tb,) + allk, (pb2,), inc=(li == 31))
                    gelu_tanh(hid_g[g][:, hc, :NCMP], hidB, pt2, pb2, hb[:, hc:hc + 1], 128, NCMP)
            if kv == 0:
                t, tb = wrot.get()
                wv = t[:, 0:2 * 256].rearrange("p (k n) -> p k n", k=2)
                K.dma("pool", wv, cw2k_d[l].rearrange("(k p) n -> p k n", p=128), writes=(tb,))
                for g in range(2):
                    pt, pb = pA.get(); pts, pbs = pA.get()
                    for hc in range(2):
                        mm(pt[:, :NCMP], wv[:, hc, 0:128], hid_g[g][:, hc, :NCMP], hc == 0, hc == 1, (tb, hidB),
                           (pb,))
                    for hc in range(2):
                        mm(pts[:, :NCMP], wv[:, hc, 128:256], hid_g[g][:, hc, :NCMP], hc == 0, hc == 1, (tb, hidB),
                           (pbs,))
                    a, ab_ = g_x[0], g_xB[0]; b, bb_ = g_x[1], g_xB[1]
                    act(a[:, :NCMP], pt[:, :NCMP], AF.Identity, (pb, cB), (ab_,), bias=cb2k[:, l, 0:1])
                    act(b[:, :NCMP], pts[:, :NCMP], AF.Identity, (pbs, cB), (bb_,), bias=cb2k[:, l, 1:2])
                    tt("dve", a[:, :NCMP], a[:, :NCMP], cosc[:, :NCMP], ALU.mult, (ab_, cB), (ab_,))
                    tt("dve", b[:, :NCMP], b[:, :NCMP], sinc[:, :NCMP], ALU.mult, (bb_, cB), (bb_,))
                    tt("dve", kcT[:, g, :NCMP], a[:, :NCMP], b[:, :NCMP], ALU.add, (ab_, bb_), (cmpB,))
            else:
                t, tb = wrot.get()
                wv = t[:, 0:2 * 128].rearrange("p (k n) -> p k n", k=2)
                K.dma("pool", wv, cw2v_d[l].rearrange("(k p) n -> p k n", p=128), writes=(tb,))
                for g in range(2):
                    pt, pb = pA.get()
                    for hc in range(2):
                        mm(pt[:NCMP, :128], hid_g[g][:, hc, :NCMP], wv[:, hc, :], hc == 0, False, (tb, hidB), (pb,),
                           inc=False)
                    mm(pt[:NCMP, :128], c_ones[0:1, :NCMP], cb2v[0:1, l, :], False, True, (cB,), (pb,))
                    act(vc[:NCMP, g, :], pt[:NCMP, :128], AF.Copy, (pb,), (cmpB,))

    hid_g = [sb(f"hid_g{g}", [128, 2, 128], BF16) for g in range(2)]

    def nsa_q_proj(l, c):
        plan([(awin_d[l], ch * 256, 256) for ch in range(6)] + [(awin_d[l], 22 * 128, 256), (awin_d[l], 3072, 128)])
        xcB = [xB[k][c] for k in range(8)]
        rmsnorm_block(xT, xcB, l, TQ, hT, hB, col0=c * TQ)
        rc, rs, rB = nsa_rope_tables(c)
        for ch in range(6):
            wv, tb = nsa_load_w(l, ch * 256, 256)
            pt, pb = proj_fm(wv, tb, 0, 128, hT, hB, TQ)
            pts, pbs = proj_fm(wv, tb, 128, 128, hT, hB, TQ)
            rope_evac(pt, pb, pts, pbs, qT[:, ch, :], qB, rc[:], rs[:], rB, TQ)
        wv, tb = nsa_load_w(l, 22 * 128, 256)
        for j in range(2):
            pt, pb = proj_fm(wv, tb, j * 128, 128, hT, hB, TQ)
            act(qmT[:, j, :], pt[:, :], AF.Copy, (pb,), (qmB,))
        wv, tb = nsa_load_w(l, 3072, 128)
        pt, pb = proj_fm(wv, tb, 0, 36, hT, hB, TQ)
        act(e36[:, :], pt[:36, :], AF.Exp, (pb, cB), (e36B,), bias=agb[:, l:l + 1], scale=-1.0)
        cp("dve", e_hi[:, :], e36[:, :], (e36B,), (ehlB,))
        cp("dve", e_hf[:, :], e_hi[:, :], (ehlB,), (ehfB,))
        tt("dve", e_lo[:, :], e36[:, :], e_hf[:, :], ALU.subtract, (e36B, ehfB), (ehlB,))

    def nsa_attention(l, c):
        K.dma("pool", cmpmask, cmpmask_d[:, c * TQ:(c + 1) * TQ], writes=(cmB,))
        K.dma("sp", bonus[:], bonus_d[:, 4 * c:4 * c + 4, :], writes=(bonusB,))
        use_sel = c >= 2

        def cmp_scores(h, g, pr, ch):
            st_t, st_b = pS.get()
            mm(st_t[:NCMP, :], kcT[pr, g, :NCMP], qT[pr, ch, :], True, True, (cmpB, qB), (st_b,))
            e_t, e_b = Erot.get()
            act(e_t[:NCMP, :], st_t[:NCMP, :], AF.Exp, (st_b,), (e_b,), scale=SCALE)
            tt("dve", e_t[:NCMP, :], e_t[:NCMP, :], cmpmask[:NCMP, :], ALU.mult, (e_b, cmB), (e_b,))
            pd_t, pd_b = pD.get()
            mm(pd_t[:, :], c_ones[:NCMP, :], e_t[:NCMP, :], True, True, (cB, e_b), (pd_b,))
            d_t, d_b = denrot.get()
            ts("dve", d_t[:, :], pd_t[:, :], 1e-30, None, ALU.max, None, (pd_b,), (d_b,))
            act(d_t[:, :], d_t[:, :], AF.Ln, (d_b,), (d_b,))
            act(d_t[:, :], d_t[:, :], AF.Exp, (d_b,), (d_b,), scale=-1.0)
            return e_t, e_b, d_t, d_b

        for g in range(2):
            heads = list(range(6 * g, 6 * g + 6))
            if use_sel:
                pipe.flush()
                K.op("dve", lambda e: e.memset(imp_acc4[:], 0.0), writes=(impaB,))
                for h in heads:
                    ch, off = h // 2, 64 * (h % 2)
                    pr = slice(off, off + 64)
                    st_t, st_b = pS.get()
                    mm(st_t[:NCMP, :], kcT[pr, g, :NCMP], qT[pr, ch, :], True, True, (cmpB, qB), (st_b,))
                    e_t, e_b = Erot.get()
                    act(e_t[:NCMP, :], st_t[:NCMP, :], AF.Exp, (st_b,), (e_b,), scale=SCALE)
                    tt("dve", e_t[:NCMP, :], e_t[:NCMP, :], cmpmask[:NCMP, :], ALU.mult, (e_b, cmB), (e_b,))
                    rp_t, rp_b = pA2.get()
                    rv = rp_t[:, 0:132].rearrange("p (a n) -> p a n", n=33)
                    for tt_i in range(4):
                        mm(rv[:, tt_i, :], e_t[:NCMP, tt_i * 128:(tt_i + 1) * 128], ov33[:NCMP, :], tt_i == 0,
                           tt_i == 3, (e_b, cB), (rp_b,), inc=(tt_i == 3))
                    ts("dve", r4[:], rv[:, :, 32:33], 1e-30, None, ALU.max, None, (rp_b,), (r4B,))
                    K.op("dve", lambda e: e.reciprocal(r4[:], r4[:]), reads=(r4B,), writes=(r4B,))
                    for tt_i in range(4):
                        stt("dve", imp_acc4[:, tt_i, :], rv[:, tt_i, 0:32], r4[:, tt_i, :], imp_acc4[:, tt_i, :],
                            ALU.mult, ALU.add, (rp_b, r4B, impaB), (impaB,))
                for tt_i in range(4):
                    tt("dve", imp_s4[:, tt_i, :], imp_acc4[:, tt_i, :], bonus[:, tt_i, :], ALU.add,
                       (impaB, bonusB), (impB,))
                for tt_i in range(4):
                    imp_s = imp_s4[:, tt_i, :]
                    K.op("dve", lambda e: e.max(out=mx8[:, 0:8], in_=imp_s), reads=(impB,), writes=(mx8B,))
                    K.op("dve", lambda e: e.match_replace(out=sc2[:, :], in_to_replace=mx8[:, 0:8],
                                                          in_values=imp_s, imm_value=-3.0e38),
                         reads=(impB, mx8B), writes=(sc2B,))
                    K.op("dve", lambda e: e.max(out=mx8[:, 8:16], in_=sc2[:, :]), reads=(sc2B,), writes=(mx8B,))
                    ts("dve", sel_s[:, :], imp_s, mx8[:, 15:16], None, ALU.is_ge, None, (impB, mx8B),
                       (selsB,))
                    ts("dve", sel_s[:, :], sel_s[:, :], -1.0, 30000.0, ALU.add, ALU.mult, (selsB,), (selsB,))
                    tp_t, tp_b = pA2.get()
                    K.op("pe", lambda e, tp_t=tp_t: e.transpose(tp_t[:32, :128], sel_s[:, :], c_ident[:, :]),
                         reads=(selsB, cB), writes=(tp_b,))
                    act(selT[:32, g, tt_i * 128:(tt_i + 1) * 128], tp_t[:32, :128], AF.Copy, (tp_b,), (selB,))
            for h in heads:
                ch, off = h // 2, 64 * (h % 2)
                pr = slice(off, off + 64)
                qfn = lambda col0, ncol, pr=pr, ch=ch: (qT[pr, ch, col0:col0 + ncol], (qB,))
                head_gates(l, h)

                def fin(br, last, h=h, pr=pr, ch=ch):
                    def f(pn_t, pn_b, pd_t, pd_b):
                        d_t, d_b = denrot.get()
                        ts("dve", d_t[pr, :], pd_t[pr, :], 1e-30, None, ALU.max, None, (pd_b,), (d_b,))
                        finish_gated(h, br, pn_t, pn_b, d_t, d_b, pr, first=(br == 0), last=last,
                                     dst=oT[pr, ch, :])
                    return f

                pn_t, pn_b = pN.get(); pd_t, pd_b = pD.get()
                f0 = fin(0, False)
                attn_tile(kcT[pr, g, :NCMP], (cmpB,), qT[pr, ch, :], (qB,), TQ, (0.0, ()),
                          (lambda e_t, e_b: tt("dve", e_t[:NCMP, :], e_t[:NCMP, :], cmpmask[:NCMP, :], ALU.mult,
                                               (e_b, cmB), (e_b,))),
                          vc[:NCMP, g, :], (cmpB,), pn_t, pn_b, pd_t, pd_b, True, True, 0,
                          after=(lambda f0=f0, a=pn_t, b=pn_b, c_=pd_t, d=pd_b: f0(a, b, c_, d)), krows=NCMP,
                          preB_late=(None if cfg.get("br_only") is not None else (lambda h=h: prep_gate(h, 0))))

                def sel_fn(j, col0, ncol, post0, g=g):
                    if not use_sel:
                        return post0, None
                    return post0, (expand[:, j, :], selT[:, g, col0:col0 + ncol], (cmpB, selB))
                causal_attention(
                    lambda j, pr=pr, g=g: (kslcT[pr, g, j * 128:(j + 1) * 128], (nsaKB[j // 4],)),
                    lambda j, g=g: (vslc[:, j, g * 128:(g + 1) * 128], (nsaKB[j // 4],)),
                    qfn, c, lambda j: (0.0, ()), pr, sel_fn=sel_fn, after_fn=fin(1, False),
                    preB=(None if cfg.get("br_only") is not None else (lambda h=h: prep_gate(h, 1))))
                pn_t, pn_b = pN.get(); pd_t, pd_b = pD.get()
                order = [4 * c] + [j for j in range(4 * c - 4, 4 * c) if j >= 0] + [4 * c + 1, 4 * c + 2, 4 * c + 3]
                f2 = fin(2, True)
                for idx, j in enumerate(order):
                    i = j - 4 * c
                    if i >= 0:
                        col0, ncol = 128 * i, TQ - 128 * i
                        post = mask_sub(c_tri[:, :])
                    else:
                        ii = i + 4
                        col0, ncol = 0, 128 * (ii + 1)
                        def post(e_t, e_b, ii=ii):
                            tt("pool", e_t[:, 128 * ii:128 * ii + 128], e_t[:, 128 * ii:128 * ii + 128],
                               c_tric[:, :], ALU.mult, (e_b, cB), (e_b,))
                    aft = None
                    if idx == len(order) - 1:
                        aft = (lambda f2=f2, a=pn_t, b=pn_b, c_=pd_t, d=pd_b: f2(a, b, c_, d))
                    attn_tile(kwinT[pr, g, j * 128:(j + 1) * 128], (nsaKB[j // 4],),
                              qT[pr, ch, col0:col0 + ncol], (qB,), ncol, (0.0, ()), post,
                              vwin[:, j, g * 128:(g + 1) * 128], (nsaKB[j // 4],), pn_t, pn_b, pd_t, pd_b,
                              idx == 0, idx == len(order) - 1, col0, after=aft,
                              preB=((lambda h=h: prep_gate(h, 2))
                                    if (idx == len(order) - 1 and cfg.get("br_only") is None) else None))
        pipe.flush()

    def head_gates(l, h):
        return

    gate_bc = {}

    def prep_gate(h, br):
        s_t, s_b = selhrot.get()
        ts("dve", s_t[:, :], rowidx[:, :], float(3 * h + br), None, ALU.is_equal, None, (cB,), (s_b,))
        gp_t, gp_b = pA2.get()
        mm(gp_t[:, :], s_t[:36, :], e_hi[:36, :], True, False, (s_b, ehlB), (gp_b,), inc=False)
        mm(gp_t[:, :], s_t[:36, :], e_lo[:36, :], False, True, (s_b, ehlB), (gp_b,))
        gate_bc[(h, br)] = (gp_t, gp_b)

    def finish_gated(h, br, pn_t, pn_b, d_t, d_b, pr, first, last=False, dst=None):
        if cfg.get("br_only") is not None:
            if br == cfg["br_only"]:
                ch_ = h // 2
                K.op("dve", lambda e: e.reciprocal(d_t[pr, :], d_t[pr, :]), reads=(d_b,), writes=(d_b,))
                tt("dve", oT[pr, ch_, :], pn_t[pr, :], d_t[pr, :], ALU.mult, (pn_b, d_b), (oB,))
            return
        gp_t, gp_b = gate_bc.pop((h, br))
        oacc = oacc1
        w_t, w_b = tmprot.get()
        stt("dve", w_t[pr, :], gp_t[pr, :], 1.0, d_t[pr, :], ALU.add, ALU.mult, (gp_b, d_b), (w_b,))
        act(w_t[pr, :], w_t[pr, :], AF.Ln, (w_b,), (w_b,))
        act(w_t[pr, :], w_t[pr, :], AF.Exp, (w_b,), (w_b,), scale=-1.0)
        if first:
            tt("dve", oacc[pr, :], pn_t[pr, :], w_t[pr, :], ALU.mult, (pn_b, w_b), (oaccB,))
            return
        tt("dve", w_t[pr, :], pn_t[pr, :], w_t[pr, :], ALU.mult, (pn_b, w_b), (w_b,))
        if last:
            tt("dve", dst, oacc[pr, :], w_t[pr, :], ALU.add, (oaccB, w_b), (oB,))
        else:
            tt("dve", oacc[pr, :], oacc[pr, :], w_t[pr, :], ALU.add, (oaccB, w_b), (oaccB,))

    skip = set(cfg.get("skip", ()))
    def maybe(fn):
        def w(*a):
            if fn.__name__ in skip:
                return
            K.phases.append((fn.__name__, a, dict(K.nops)))
            return fn(*a)
        return w
    mem_kv = maybe(mem_kv); nsa_kv_proj = maybe(nsa_kv_proj); nsa_compress = maybe(nsa_compress)
    nsa_q_proj = maybe(nsa_q_proj); nsa_attention = maybe(nsa_attention); mem_attention = maybe(mem_attention)
    wo_block = maybe(wo_block); ffn_block = maybe(ffn_block); fox_shared_kv = maybe(fox_shared_kv)
    fox_q_proj = maybe(fox_q_proj); fox_attention = maybe(fox_attention)
    for l in layers:
        if mode == "ffn":
            for c in range(NQB):
                ffn_block(l, c)
            continue
        mem_kv(l)
        if l < NA:
            for c in range(NQB):
                nsa_kv_proj(l, c)
            nsa_compress(l)
            for c in range(NQB):
                nsa_q_proj(l, c)
                nsa_attention(l, c)
                mem_attention(c)
                wo_block(l, c)
                if mode != "attn":
                    ffn_block(l, c)
            K.barrier()
        else:
            if l == NA:
                K.barrier()
                for c in range(NQB):
                    fox_shared_kv(c)
            for c in range(NQB):
                fox_q_proj(l, c)
                fox_attention(l, c)
                mem_attention(c)
                wo_block(l, c)
                if mode != "attn":
                    ffn_block(l, c)
    if do_final:
        for c in range(NQB):
            xcB = [xB[k][c] for k in range(8)]
            for k in range(8):
                K.op("act", lambda e, k=k: e.activation(sq[:, k, :], xT[:, k, c * TQ:(c + 1) * TQ], AF.Square),
                     reads=(xcB[k],), writes=(sqB,))
            pt, pb = pA.get()
            for k in range(8):
                mm(pt[:, :], c_onesm[:], sq[:, k, :], k == 0, k == 7, (sqB, cB), (pb,), inc=(k == 7))
            act(rstd[:, :], pt[:, :], AF.Ln, (pb, cB), (rstdB,), bias=c_eps[:, 0:1])
            act(rstd[:, :], rstd[:, :], AF.Exp, (rstdB,), (rstdB,), scale=-0.5)
            for k in range(8):
                stt("dve", xT[:, k, c * TQ:(c + 1) * TQ], xT[:, k, c * TQ:(c + 1) * TQ], gains[:, 13, k:k + 1],
                    rstd[:, :], ALU.mult, ALU.mult, (xcB[k], rstdB, gB), (xcB[k],))
    for k in range(8):
        K.dma("sp", out_d[k * 128:(k + 1) * 128, :], xT[:, k, :], reads=tuple(xB[k]))
    if cfg.get("dump"):
        K.barrier()
        dl = {"oT": (oT, [128, 8, TQ], BF16), "qT": (qT, [128, 6, TQ], BF16), "qmT": (qmT, [128, 2, TQ], BF16),
              "kslcT": (kslcT, [128, 2, S], BF16), "kwinT": (kwinT, [128, 2, S], BF16),
              "vslc": (vslc, [128, 16, 256], BF16), "vwin": (vwin, [128, 16, 256], BF16),
              "ucmp": (ucmp, [128, 2, S], BF16), "kcT": (kcT, [128, 2, 128], BF16), "vc": (vc, [128, 2, 128], BF16),
              "selT": (selT, [128, 2, TQ], BF16), "kmT": (kmT, [128, 2, NMEM], BF16), "vm": (vm, [128, 2, 256], BF16),
              "hT": (hT, [128, 8, TQ], BF16), "hid0": (hid_g[0], [128, 2, 128], BF16)}
        for nm in cfg["dump"]:
            t_, shp, dt_ = dl[nm]
            dd = nc.dram_tensor("dump_" + nm, shp, dt_, kind="ExternalOutput").ap()
            K.dma("sp", dd, t_ if not hasattr(t_, "ap") else t_[:], reads=())
    K.finish()
    st.close()
    return nc, K


def _consts():
    c = {}
    c["ident"] = np.eye(128, dtype=np.float32)
    s = np.arange(128)[:, None]; t = np.arange(128)[None, :]
    c["tri"] = (s <= t).astype(np.float32)
    c["tric"] = (t < s).astype(np.float32)
    n = np.arange(128)[:, None]; tt = np.arange(S)[None, :]
    c["cmpmask"] = ((16 * n + 31 <= tt) & (n < NCMP)).astype(np.float32)
    cs = np.arange(128) * 16
    ss = np.arange(32) * 64
    ov = ((cs[:, None] < ss[None, :] + 64) & (cs[:, None] + 32 > ss[None, :])).astype(np.float32)
    ov[NCMP:] = 0
    c["overlap"] = ov
    ov33 = np.zeros((128, 33), np.float32); ov33[:, :32] = ov; ov33[:NCMP, 32] = 1.0
    c["overlap33"] = ov33
    ex = np.zeros((32, 16, 128), np.float32)
    for jt in range(16):
        for s_ in range(128):
            ex[2 * jt + s_ // 64, jt, s_] = 1.0
    c["expand"] = ex
    bn = np.zeros((128, 16, 32), np.float32)
    for tix in range(16):
        tpos = tix * 128 + np.arange(128)
        blk = tpos // 64
        j = np.arange(32)[None, :]
        forced = (j == 0) | (j == blk[:, None]) | (j == blk[:, None] - 1)
        valid = j <= blk[:, None]
        bn[:, tix, :] = np.where(valid, 1e4 * forced, -1e30)
    c["bonus"] = bn
    c["rowidx"] = np.broadcast_to(np.arange(36, dtype=np.float32)[:, None], (36, 128)).copy()
    s127 = np.zeros((128, 128), np.float32); s127[127, :] = 1.0
    c["sel127"] = s127
    half = 32
    inv = (10000.0 ** (-np.arange(half, dtype=np.float32) / half)).astype(np.float32)
    def tables(pos):
        ang = pos.astype(np.float32)[None, :] * inv[:, None]
        co = np.cos(ang).astype(np.float32); si = np.sin(ang).astype(np.float32)
        cos64 = np.concatenate([co, co], 0); sin64 = np.concatenate([-si, si], 0)
        return np.concatenate([cos64, cos64], 0), np.concatenate([sin64, sin64], 0)
    c["ropecos"], c["ropesin"] = tables(np.arange(S))
    pc = np.arange(128) * 16 + 31
    c["ropecosc"], c["ropesinc"] = tables(pc)
    return {k: np.ascontiguousarray(v, dtype=np.float32) for k, v in c.items()}


def _fm(vec_list):
    a = np.stack(vec_list, 0).reshape(len(vec_list), 8, 128)
    return np.ascontiguousarray(a.transpose(2, 0, 1))


def prep_shared(inp):
    f = lambda a: np.ascontiguousarray(np.asarray(a, dtype=np.float32))
    sh = dict(_consts())
    gl = [inp["attn_norm"][i] for i in range(4)] + [inp["ffn_norm"][i] for i in range(4)] + \
         [inp["mem_norm"][i] for i in range(4)] + [inp["kv_norm"], inp["final_norm"]]
    sh["gains"] = _fm([np.asarray(g) for g in gl])
    sh["w_mem_kv"] = f(inp["w_mem_kv"]); sh["w_o"] = f(inp["w_o"]); sh["w_up"] = f(inp["w_up"])
    sh["w_down"] = f(inp["w_down"])
    cw = np.asarray(inp["conv_w"]).reshape(DEPTH, 3, NCH, 128)
    sh["convw"] = f(cw.transpose(3, 0, 1, 2))
    sh["convb"] = f(np.asarray(inp["conv_b"]).reshape(DEPTH, NCH, 128).transpose(2, 0, 1))
    aw = np.asarray(inp["a_w_in"])
    aug = np.zeros((NA, D, NSA_NCOL), np.float32)
    aug[:, :, :3072] = aw[:, :, NSA_FM]
    aug[:, :, 3072:3072 + 36] = aw[:, :, NSA_GATE]
    aug[:, :, 3200:] = aw[:, :, NSA_TM]
    sh["a_w_aug"] = aug
    sh["a_gate_b"] = f(np.asarray(inp["a_gate_b"]).T)
    w1 = np.asarray(inp["a_cmp_w1"]).reshape(NA, 2, 32, 64, 256).transpose(0, 1, 3, 2, 4)
    sh["cmp_w1"] = f(np.concatenate([w1, w1], axis=2))
    pos = np.asarray(inp["a_cmp_pos"]).transpose(3, 0, 1, 2)
    sh["cmp_pos"] = f(np.concatenate([pos, pos], 0))
    sh["cmp_b1"] = f(np.asarray(inp["a_cmp_b1"]).reshape(NA, 2, 2, 128).transpose(3, 0, 1, 2))
    w2 = np.asarray(inp["a_cmp_w2"]); b2 = np.asarray(inp["a_cmp_b2"])
    sw = _swap64(np.arange(64))
    w2k = w2[:, 0]
    sh["cmp_w2k"] = f(np.concatenate([w2k, w2k, w2k[:, :, sw], w2k[:, :, sw]], axis=2))
    b2k = b2[:, 0]
    b2kp = np.concatenate([b2k, b2k], 1); b2ks = np.concatenate([b2k[:, sw], b2k[:, sw]], 1)
    sh["cmp_b2k"] = f(np.stack([b2kp, b2ks], -1).transpose(1, 0, 2))
    w2v = w2[:, 1]
    sh["cmp_w2v"] = f(np.concatenate([w2v, w2v], axis=2))
    sh["cmp_b2v"] = f(np.concatenate([b2[:, 1], b2[:, 1]], 1))
    sh["b_w_in"] = f(inp["b_w_in"])
    wkv = np.asarray(inp["w_kv_shared"])
    wa = np.zeros((D, 768 + 768 + 128), np.float32)
    wa[:, :1536] = wkv[:, :1536]; wa[:, 1536:1548] = wkv[:, 1536:1548]
    sh["w_kv_aug"] = wa
    sh["b_fgate_bc"] = f(np.broadcast_to(np.asarray(inp["b_fgate"])[None, :], (128, 12)))
    return sh


_CACHE = {}

def run(inputs, cfg, n_cores=8, x_override=None):
    key = repr(sorted(cfg.items()))
    if key not in _CACHE:
        _CACHE[key] = build(cfg)
    nc, K = _CACHE[key]
    sh = prep_shared(inputs)
    x = np.asarray(inputs["x"], dtype=np.float32) if x_override is None else x_override
    mem = np.asarray(inputs["mem"], dtype=np.float32)
    in_maps = []
    for b in range(n_cores):
        m = dict(sh)
        m["xT"] = np.ascontiguousarray(x[b].T)
        m["memT"] = np.ascontiguousarray(mem[b].T)
        in_maps.append(m)
    res = run_bass_kernel_spmd(nc, in_maps, core_ids=list(range(n_cores)))
    global LAST_RES
    LAST_RES = res.results
    return np.stack([np.ascontiguousarray(r["outT"].T) for r in res.results], 0)


def kernel(**inputs):
    return run(inputs, {"layers": (0, 1, 2, 3), "final": True}).astype(np.float32)
```
